# Optimizing a Trainium2 kernel written in Bass

```python
import math
import jax, jax.numpy as jnp
from jax import lax
import numpy as np

D_MODEL = 1024
BATCH = 32
SEQ = 256
DEPTH = 4
DEC_BATCH = 8
DEC_SEQ = 4096
PAST_LEN = 256

GRID_W = 64
RET_HEADS = 4
RET_DK = 64
RET_DV = 64
RET_CHUNK = 128
DIFF_HEADS = 4
DIFF_DK = 32
DIFF_DV = 2 * DIFF_DK
MLA_HEADS = 8
MLA_NOPE = 64
MLA_ROPE = 32
MLA_DQK = MLA_NOPE + MLA_ROPE
MLA_DV = 64
MLA_Q_RANK = 768
MLA_KV_RANK = 256
RET_W = RET_HEADS * RET_DV
DIFF_W = DIFF_HEADS * DIFF_DV
MLA_W = MLA_HEADS * MLA_DV
MIX_W = RET_W + DIFF_W + MLA_W
IN_SPLIT_SIZES = (RET_HEADS * RET_DK, RET_HEADS * RET_DK, RET_W, RET_W,
                  DIFF_HEADS * 2 * DIFF_DK, DIFF_HEADS * 2 * DIFF_DK, DIFF_W,
                  MLA_Q_RANK, MLA_KV_RANK, MLA_ROPE)
D_IN = sum(IN_SPLIT_SIZES)
D_FF = -(-8 * D_MODEL // (3 * 256)) * 256
ROPE_BASE = 10000.0
QUERY_BLOCK = 128
EPS = 1e-6

kernel_name = "hybrid_retention_diffattn_mla_dit_step"

F32 = jnp.float32


def rms_norm(x, gain):
    x32 = x.astype(F32)
    y = x32 * lax.rsqrt(jnp.mean(x32 * x32, axis=-1, keepdims=True) + EPS)
    return (y * gain.astype(F32)).astype(x.dtype)


def head_group_norm(o, gain):
    mu = jnp.mean(o, axis=-1, keepdims=True)
    var = jnp.mean(jnp.square(o - mu), axis=-1, keepdims=True)
    return (o - mu) * lax.rsqrt(var + EPS) * gain.astype(F32).reshape(o.shape[-2], o.shape[-1])


def axial_rope_tables(n, rot_dim):
    rows = n // GRID_W
    row = jnp.repeat(jnp.arange(rows, dtype=F32), GRID_W)
    col = jnp.tile(jnp.arange(GRID_W, dtype=F32), rows)
    n_freq = rot_dim // 4
    inv = 1.0 / (ROPE_BASE ** (jnp.arange(n_freq, dtype=F32) / n_freq))
    ang = jnp.concatenate([row[:, None] * inv[None], col[:, None] * inv[None]], axis=-1)
    return jnp.cos(ang), jnp.sin(ang)


def apply_rope(x, cos, sin):
    shape = (x.shape[1],) + (1,) * (x.ndim - 3) + (cos.shape[-1],)
    c = cos.reshape(shape)
    s = sin.reshape(shape)
    x32 = x.astype(F32)
    x1, x2 = jnp.split(x32, 2, axis=-1)
    return jnp.concatenate([x1 * c - x2 * s, x2 * c + x1 * s], axis=-1).astype(x.dtype)


def sweep_query_blocks(fn, q):
    b, n = q.shape[0], q.shape[1]
    nb = n // QUERY_BLOCK
    qb = jnp.moveaxis(q.reshape((b, nb, QUERY_BLOCK) + q.shape[2:]), 1, 0)
    out = jnp.moveaxis(lax.map(fn, qb), 0, 1)
    return out.reshape((b, n) + out.shape[3:])


def softmax_attend(q, k, v):
    scale = q.shape[-1] ** -0.5
    k32 = k.astype(F32)
    v32 = v.astype(F32)

    def block(qb):
        s = jnp.einsum('bqhd,bkhd->bhqk', qb.astype(F32), k32) * scale
        p = jax.nn.softmax(s, axis=-1)
        return jnp.einsum('bhqk,bkhe->bqhe', p, v32)

    return sweep_query_blocks(block, q)


def diff_attend(q, k, v, lam):
    scale = q.shape[-1] ** -0.5
    k32 = k.astype(F32)
    v32 = v.astype(F32)

    def block(qb):
        s = jnp.einsum('bqhmd,bkhmd->bhmqk', qb.astype(F32), k32) * scale
        p = jax.nn.softmax(s, axis=-1)
        a = p[:, :, 0] - lam * p[:, :, 1]
        return jnp.einsum('bhqk,bkhe->bqhe', a, v32)

    return sweep_query_blocks(block, q)


def retention_scan(q, k, v, log_gamma, s0):
    b, n, nh, _ = q.shape
    dv = v.shape[-1]
    C = RET_CHUNK
    nc = n // C
    idx = jnp.arange(C, dtype=F32)
    dist = idx[:, None] - idx[None, :]
    causal = dist >= 0
    intra = jnp.where(causal[None], jnp.exp(log_gamma[:, None, None] * jnp.where(causal, dist, 0.0)[None]), 0.0)
    q_decay = jnp.exp(log_gamma[None, :] * (idx[:, None] + 1.0))
    k_decay = jnp.exp(log_gamma[None, :] * (C - 1.0 - idx[:, None]))
    chunk_decay = jnp.exp(log_gamma * C)

    def to_chunks(a):
        return jnp.moveaxis(a.reshape(b, nc, C, nh, a.shape[-1]), 1, 0)

    def step(s, blk):
        qc, kc, vc = blk
        att = jnp.einsum('bnhd,bmhd->bhnm', qc, kc) * intra[None]
        o = (jnp.einsum('bhnm,bmhe->bnhe', att, vc)
             + jnp.einsum('bnhd,bhde->bnhe', qc, s) * q_decay[None, :, :, None])
        s = (s * chunk_decay[None, :, None, None]
             + jnp.einsum('bmhd,bmhe->bhde', kc * k_decay[None, :, :, None], vc))
        return s, o

    s_final, o = lax.scan(step, s0, (to_chunks(q), to_chunks(k), to_chunks(v)))
    return jnp.moveaxis(o, 0, 1).reshape(b, n, nh, dv), s_final


def bi_retention(q, k, v, log_gamma, s0):
    o_f, s_f = retention_scan(q, k, v, log_gamma[0], s0[:, 0])
    o_b, s_b = retention_scan(q[:, ::-1], k[:, ::-1], v[:, ::-1], log_gamma[1], s0[:, 1])
    return o_f + o_b[:, ::-1], jnp.stack([s_f, s_b], axis=1)


def mla_keys_values(ckv_n, kr, w_ukv, k_gain):
    b, n, _ = ckv_n.shape
    kv = (ckv_n @ w_ukv).reshape(b, n, MLA_HEADS, MLA_NOPE + MLA_DV)
    k = jnp.concatenate([kv[..., :MLA_NOPE],
                         jnp.broadcast_to(kr[:, :, None, :], (b, n, MLA_HEADS, MLA_ROPE)).astype(kv.dtype)], axis=-1)
    return rms_norm(k, k_gain), kv[..., MLA_NOPE:]


def token_mixers(h, lp, lam_init, ctx):
    b, n, _ = h.shape
    offs = np.cumsum(IN_SPLIT_SIZES)[:-1].tolist()
    rq, rk, rv, rg, dq, dk, dv, cq, ckv, kr = jnp.split(h @ lp["w_in"], offs, axis=-1)
    rq = rq.reshape(b, n, RET_HEADS, RET_DK)
    rk = rk.reshape(b, n, RET_HEADS, RET_DK)
    rv = rv.reshape(b, n, RET_HEADS, RET_DV)
    dq = rms_norm(dq.reshape(b, n, DIFF_HEADS, 2, DIFF_DK), lp["diff_qk_gain"][0])
    dk = rms_norm(dk.reshape(b, n, DIFF_HEADS, 2, DIFF_DK), lp["diff_qk_gain"][1])
    dv = dv.reshape(b, n, DIFF_HEADS, DIFF_DV)
    mq = rms_norm((rms_norm(cq, lp["mla_q_norm"]) @ lp["w_uq"]).reshape(b, n, MLA_HEADS, MLA_DQK),
                  lp["mla_qk_gain"][0])
    ckv_n = rms_norm(ckv, lp["mla_kv_norm"])
    mk, mv = mla_keys_values(ckv_n, kr, lp["w_ukv"], lp["mla_qk_gain"][1])

    if ctx is None:
        state0 = jnp.zeros((b, 2, RET_HEADS, RET_DK, RET_DV), F32)
        dk_all, dv_all, mk_all, mv_all = dk, dv, mk, mv
    else:
        state0, ctx_dk, ctx_dv, ctx_ckv, ctx_kr = ctx
        cos_r, sin_r = axial_rope_tables(n, RET_DK)
        cos_s, sin_s = axial_rope_tables(n, DIFF_DK)
        rq = apply_rope(rq, cos_r, sin_r)
        rk = apply_rope(rk, cos_r, sin_r)
        dq = apply_rope(dq, cos_s, sin_s)
        dk = apply_rope(dk, cos_s, sin_s)
        mq = jnp.concatenate([mq[..., :MLA_NOPE], apply_rope(mq[..., MLA_NOPE:], cos_s, sin_s)], axis=-1)
        mk = jnp.concatenate([mk[..., :MLA_NOPE], apply_rope(mk[..., MLA_NOPE:], cos_s, sin_s)], axis=-1)
        ctx_mk, ctx_mv = mla_keys_values(ctx_ckv.astype(h.dtype), ctx_kr.astype(h.dtype), lp["w_ukv"], lp["mla_qk_gain"][1])
        dk_all = jnp.concatenate([dk, ctx_dk.astype(dk.dtype)], axis=1)
        dv_all = jnp.concatenate([dv, ctx_dv.astype(dv.dtype)], axis=1)
        mk_all = jnp.concatenate([mk, ctx_mk.astype(mk.dtype)], axis=1)
        mv_all = jnp.concatenate([mv, ctx_mv.astype(mv.dtype)], axis=1)

    log_gamma = jax.nn.log_sigmoid(lp["ret_decay"].astype(F32))
    o_ret, states = bi_retention(rq.astype(F32), rk.astype(F32) * (RET_DK ** -0.5), rv.astype(F32),
                                 log_gamma, state0.astype(F32))
    ret = head_group_norm(o_ret, lp["ret_gn_gain"]).reshape(b, n, RET_W) * jax.nn.silu(rg.astype(F32))

    dl = lp["diff_lambda"].astype(F32)
    lam = jnp.exp(jnp.sum(dl[0] * dl[1])) - jnp.exp(jnp.sum(dl[2] * dl[3])) + lam_init
    o_diff = diff_attend(dq, dk_all, dv_all, lam)
    diff = (rms_norm(o_diff, lp["diff_subln_gain"]) * (1.0 - lam_init)).reshape(b, n, DIFF_W)

    mla = softmax_attend(mq, mk_all, mv_all).reshape(b, n, MLA_W)

    mixed = jnp.concatenate([ret.astype(h.dtype), diff.astype(h.dtype), mla.astype(h.dtype)], axis=-1)
    ctx_out = (states, dk, dv, ckv_n, kr) if ctx is None else None
    return mixed, ctx_out


def trunk_layer(x, cond, lp, lam_init, ctx):
    mod = (jax.nn.silu(cond) @ lp["w_mod"] + lp["b_mod"])[:, None, :]
    shift1, scale1, gate1, shift2, scale2, gate2 = jnp.split(mod, 6, axis=-1)
    h = rms_norm(x, lp["norm1"]) * (1.0 + scale1) + shift1
    mixed, ctx_out = token_mixers(h, lp, lam_init, ctx)
    x = x + gate1 * (mixed @ lp["w_out"])
    h = rms_norm(x, lp["norm2"]) * (1.0 + scale2) + shift2
    g, u = jnp.split(h @ lp["w_gu"], 2, axis=-1)
    x = x + gate2 * ((jax.nn.silu(g) * u) @ lp["w_down"])
    return x, ctx_out


def setup_inputs(seed: int = 0) -> dict:
    key = jax.random.key(seed)
    ks = iter(jax.random.split(key, 40))

    def nrm(shape, scale):
        return jax.random.normal(next(ks), shape, F32) * scale

    def gain(shape):
        return 1.0 + nrm(shape, 0.02)

    gamma0 = 1.0 - 2.0 ** (-5.0 - jnp.arange(RET_HEADS, dtype=F32))
    decay_logit0 = jnp.log(gamma0) - jnp.log1p(-gamma0)
    return {
        "x_prompt": nrm((BATCH, SEQ, D_MODEL), 1.0),
        "x_sample": nrm((DEC_BATCH, DEC_SEQ, D_MODEL), 1.0),
        "c": nrm((DEC_BATCH, D_MODEL), 1.0),
        "state_ret": nrm((DEC_BATCH, DEPTH, 2, RET_HEADS, RET_DK, RET_DV), 1.0),
        "cache_diff_k": nrm((DEC_BATCH, DEPTH, PAST_LEN, DIFF_HEADS, 2, DIFF_DK), 1.0),
        "cache_diff_v": nrm((DEC_BATCH, DEPTH, PAST_LEN, DIFF_HEADS, DIFF_DV), 1.0),
        "cache_mla_ckv": nrm((DEC_BATCH, DEPTH, PAST_LEN, MLA_KV_RANK), 1.0),
        "cache_mla_kr": nrm((DEC_BATCH, DEPTH, PAST_LEN, MLA_ROPE), 1.0),
        "c_ctx": nrm((D_MODEL,), 1.0),
        "w_mod": nrm((DEPTH, D_MODEL, 6 * D_MODEL), 0.5 * D_MODEL ** -0.5),
        "b_mod": nrm((DEPTH, 6 * D_MODEL), 0.01),
        "norm1": gain((DEPTH, D_MODEL)),
        "norm2": gain((DEPTH, D_MODEL)),
        "w_in": nrm((DEPTH, D_MODEL, D_IN), D_MODEL ** -0.5),
        "ret_decay": decay_logit0[None, None, :] + nrm((DEPTH, 2, RET_HEADS), 0.1),
        "ret_gn_gain": gain((DEPTH, RET_W)),
        "diff_qk_gain": gain((DEPTH, 2, DIFF_DK)),
        "diff_lambda": nrm((DEPTH, 4, DIFF_DK), 0.1),
        "diff_subln_gain": gain((DEPTH, DIFF_DV)),
        "mla_q_norm": gain((DEPTH, MLA_Q_RANK)),
        "mla_kv_norm": gain((DEPTH, MLA_KV_RANK)),
        "w_uq": nrm((DEPTH, MLA_Q_RANK, MLA_HEADS * MLA_DQK), MLA_Q_RANK ** -0.5),
        "w_ukv": nrm((DEPTH, MLA_KV_RANK, MLA_HEADS * (MLA_NOPE + MLA_DV)), MLA_KV_RANK ** -0.5),
        "mla_qk_gain": gain((DEPTH, 2, MLA_DQK)),
        "w_out": nrm((DEPTH, MIX_W, D_MODEL), MIX_W ** -0.5),
        "w_gu": nrm((DEPTH, D_MODEL, 2 * D_FF), D_MODEL ** -0.5),
        "w_down": nrm((DEPTH, D_FF, D_MODEL), D_FF ** -0.5),
    }


def reference(x_prompt, x_sample, c, state_ret, cache_diff_k, cache_diff_v, cache_mla_ckv, cache_mla_kr,
              c_ctx, w_mod, b_mod, norm1, norm2, w_in, ret_decay, ret_gn_gain, diff_qk_gain, diff_lambda,
              diff_subln_gain, mla_q_norm, mla_kv_norm, w_uq, w_ukv, mla_qk_gain, w_out, w_gu, w_down):
    y_p = x_prompt
    y_s = x_sample
    new_ret, new_dk, new_dv, new_ckv, new_kr = [], [], [], [], []
    for l in range(DEPTH):
        lp = {
            "w_mod": w_mod[l], "b_mod": b_mod[l], "norm1": norm1[l], "norm2": norm2[l],
            "w_in": w_in[l], "ret_decay": ret_decay[l], "ret_gn_gain": ret_gn_gain[l],
            "diff_qk_gain": diff_qk_gain[l], "diff_lambda": diff_lambda[l],
            "diff_subln_gain": diff_subln_gain[l], "mla_q_norm": mla_q_norm[l],
            "mla_kv_norm": mla_kv_norm[l], "w_uq": w_uq[l], "w_ukv": w_ukv[l],
            "mla_qk_gain": mla_qk_gain[l], "w_out": w_out[l], "w_gu": w_gu[l], "w_down": w_down[l],
        }
        lam_init = 0.8 - 0.6 * math.exp(-0.3 * l)
        y_p, (s_l, dk_l, dv_l, ckv_l, kr_l) = trunk_layer(y_p, c_ctx[None, :], lp, lam_init, None)
        new_ret.append(s_l)
        new_dk.append(dk_l)
        new_dv.append(dv_l)
        new_ckv.append(ckv_l)
        new_kr.append(kr_l)
        cache_l = (state_ret[:, l], cache_diff_k[:, l], cache_diff_v[:, l], cache_mla_ckv[:, l], cache_mla_kr[:, l])
        y_s, _ = trunk_layer(y_s, c, lp, lam_init, cache_l)
    new_state_ret = jnp.stack(new_ret, axis=1)
    new_diff_k = jnp.stack(new_dk, axis=1)
    new_diff_v = jnp.stack(new_dv, axis=1)
    new_mla_ckv = jnp.stack(new_ckv, axis=1)
    new_mla_kr = jnp.stack(new_kr, axis=1)
    return (y_p, y_s, new_state_ret, new_diff_k, new_diff_v, new_mla_ckv, new_mla_kr)
```

```python
import os
import math
import numpy as np
from contextlib import ExitStack
import concourse.bass as bass
import concourse.mybir as mybir
from concourse.bass_utils import run_bass_kernel_spmd

F32 = mybir.dt.float32
BF16 = mybir.dt.bfloat16
AF = mybir.ActivationFunctionType
ALU = mybir.AluOpType

D = 1024
DEPTH = 4
NCORES = 8
T = 512
NPT = 1024
NST = 4096
NCTX = 256
DIN = 2848
DFF = 2816
EPS = 1e-6
O_RQ, O_RK, O_RV, O_RG, O_DQ, O_DK, O_DV, O_CQ, O_CKV, O_KR = 0, 256, 512, 768, 1024, 1280, 1536, 1792, 2560, 2816
PV_N1, PV_N2, PV_BM, PV_QN, PV_KVN, PV_GN, PV_DQG, PV_DKG, PV_SUB, PV_MQG, PV_MKG, PV_RD, NPV = 0, 8, 16, 64, 70, 72, 74, 75, 76, 77, 78, 79, 95


class Buf:
    __slots__ = ("w", "r", "name", "ps")

    def __init__(self, name="", ps=False):
        self.w = None
        self.r = {}
        self.name = name
        self.ps = ps or name.startswith("ps")


class Eng:
    def __init__(self, name, h, sid, is_pe=False):
        self.name = name
        self.h = h
        self.sid = sid
        self.sem = None
        self.cnt = 0
        self.seen = {}
        self.is_pe = is_pe
        self.dangling = False


class Sched:
    NDMA = 20

    def __init__(self, nc, es):
        self.nc = nc
        self.sems = []

        def mk(name):
            s = es.enter_context(nc.semaphore(name))
            self.sems.append(s)
            return len(self.sems) - 1

        self.pe = Eng("pe", nc.tensor, mk("s_pe"), True)
        self.act = Eng("act", nc.scalar, mk("s_act"))
        self.dve = Eng("dve", nc.vector, mk("s_dve"))
        self.pool = Eng("pool", nc.gpsimd, mk("s_pool"))
        self.sp = Eng("sp", nc.sync, mk("s_sp"))
        self.engs = (self.pe, self.act, self.dve, self.pool, self.sp)
        for e in self.engs:
            e.sem = self.sems[e.sid]
        self.dma_sems = {}
        for e in (self.sp, self.pool):
            lst = [[mk(f"d_{e.name}{i}"), 0] for i in range(self.NDMA)]
            self.dma_sems[e.name] = [lst, 0]
        self.n_inst = 0

    def _wait(self, eng, deps):
        for s, v in deps.items():
            if eng.is_pe and s == eng.sid:
                continue
            if eng.seen.get(s, 0) < v:
                eng.h.wait_ge(self.sems[s], v)
                eng.seen[s] = v
                self.n_inst += 1

    @staticmethod
    def _deps(reads, writes, own=None):
        deps = {}
        for b in reads:
            if b.w is not None:
                s, v = b.w
                if deps.get(s, 0) < v:
                    deps[s] = v
            if b.ps:
                for s, v in b.r.items():
                    if s != own and deps.get(s, 0) < v:
                        deps[s] = v
        for b in writes:
            if b.w is not None:
                s, v = b.w
                if deps.get(s, 0) < v:
                    deps[s] = v
            for s, v in b.r.items():
                if deps.get(s, 0) < v:
                    deps[s] = v
        return deps

    @staticmethod
    def _record(tok, reads, writes):
        s, v = tok
        for b in reads:
            if b.r.get(s, 0) < v:
                b.r[s] = v
        for b in writes:
            b.w = tok
            b.r = {}

    mute = False

    def op(self, eng, fn, reads=(), writes=(), inc=True):
        if self.mute:
            return None
        self._wait(eng, self._deps(reads, writes, eng.sid))
        ins = fn()
        self.n_inst += 1
        if inc or not eng.is_pe:
            eng.cnt += 1
            ins.then_inc(eng.sem, 1)
            tok = (eng.sid, eng.cnt)
            eng.dangling = False
        else:
            tok = (eng.sid, eng.cnt + 1)
            eng.dangling = True
        self._record(tok, reads, writes)
        return tok

    def dma(self, eng, out, in_, reads=(), writes=(), **kw):
        if self.mute:
            return None
        pool = self.dma_sems[eng.name]
        lst, idx = pool
        ent = lst[idx % len(lst)]
        pool[1] = idx + 1
        sid, cnt = ent
        deps = self._deps(reads, writes)
        if cnt > 0 and deps.get(sid, 0) < cnt:
            deps[sid] = cnt
        self._wait(eng, deps)
        ins = eng.h.dma_start(out=out, in_=in_, **kw)
        ent[1] = cnt + 16
        ins.then_inc(self.sems[sid], 16)
        self.n_inst += 1
        tok = (sid, cnt + 16)
        self._record(tok, reads, writes)
        return tok

    def _all(self):
        assert not self.pe.dangling
        deps = {}
        for e in (self.pe, self.act, self.dve, self.pool):
            if e.cnt:
                deps[e.sid] = e.cnt
        for name, (lst, idx) in self.dma_sems.items():
            for sid, cnt in lst:
                if cnt:
                    deps[sid] = cnt
        return deps

    def barrier(self):
        if self.mute:
            return
        deps = self._all()
        for e in self.engs:
            self._wait(e, dict(deps))

    def finish(self):
        self._wait(self.sp, self._all())


def _rope_tables():
    n = NST
    gw = 64
    row = np.repeat(np.arange(n // gw, dtype=np.float32), gw)
    col = np.tile(np.arange(gw, dtype=np.float32), n // gw)

    def tab(rot):
        nf = rot // 4
        inv = (1.0 / (10000.0 ** (np.arange(nf, dtype=np.float32) / nf))).astype(np.float32)
        ang = np.concatenate([row[:, None] * inv[None], col[:, None] * inv[None]], -1).astype(np.float32)
        return np.cos(ang).astype(np.float32).T, np.sin(ang).astype(np.float32).T

    c64, s64 = tab(64)
    c32, s32 = tab(32)
    C64 = np.tile(np.concatenate([c64, c64], 0), (2, 1))
    S64 = np.tile(np.concatenate([-s64, s64], 0), (2, 1))
    C32 = np.tile(np.concatenate([c32, c32], 0), (4, 1))
    S32 = np.tile(np.concatenate([-s32, s32], 0), (4, 1))
    C96 = np.concatenate([np.ones((64, n), np.float32), c32, c32], 0)
    S96 = np.concatenate([np.zeros((64, n), np.float32), -s32, s32], 0)
    pad = np.zeros((32, n), np.float32)
    rope = np.stack([C64, S64, C32, S32, np.concatenate([C96, pad], 0), np.concatenate([S96, pad], 0)], 0)
    return np.ascontiguousarray(rope.astype(np.float32))


def _const_mats():
    I = np.eye(128, dtype=np.float32)
    ones = np.ones((128, 128), np.float32)
    bd32 = np.kron(np.eye(4, dtype=np.float32), np.ones((32, 32), np.float32))
    bd64m = np.kron(np.eye(2, dtype=np.float32), np.ones((64, 64), np.float32)) / 64.0
    p64 = np.zeros((128, 128), np.float32)
    for m in range(128):
        b, i = divmod(m, 64)
        p64[b * 64 + (i + 32) % 64, m] = 1.0
    p32 = np.zeros((128, 128), np.float32)
    for m in range(128):
        b, i = divmod(m, 32)
        p32[b * 32 + (i + 16) % 32, m] = 1.0
    p96 = np.zeros((128, 128), np.float32)
    for m in range(96):
        if m < 64:
            p96[m, m] = 1.0
        else:
            p96[64 + (m - 64 + 16) % 32, m] = 1.0
    idx = np.arange(128, dtype=np.float32)
    distf = np.maximum(idx[None, :] - idx[:, None], 0.0)
    distb = np.maximum(idx[:, None] - idx[None, :], 0.0)
    maskf = (idx[None, :] >= idx[:, None]).astype(np.float32)
    maskb = (idx[:, None] >= idx[None, :]).astype(np.float32)
    np1 = np.tile((idx + 1.0)[None, :], (128, 1))
    cmn = np.tile((128.0 - idx)[None, :], (128, 1))
    misc = np.zeros((128, 128), np.float32)
    misc[:, 0] = 127.0 - idx
    misc[:, 1] = idx
    mats = np.stack([I, ones, bd32, bd64m, p64, p32, p96, distf, distb, maskf, maskb, np1, cmn, misc], 1)
    return np.ascontiguousarray(mats.astype(np.float32))


M_I, M_ONES, M_BD32, M_BD64M, M_P64, M_P32, M_P96, M_DISTF, M_DISTB, M_MASKF, M_MASKB, M_NP1, M_CMN, M_MISC = range(14)
NMAT = 14
NBF = 7


def build_program(depth=DEPTH):
    nc = bass.Bass("TRN2", target_bir_lowering=False)
    I_ = {}

    def inp(name, shape, dt=F32):
        I_[name] = nc.dram_tensor(name, list(shape), dt, kind="ExternalInput").ap()
        return I_[name]

    def outp(name, shape, dt=F32):
        return nc.dram_tensor(name, list(shape), dt, kind="ExternalOutput").ap()

    def scr(name, shape, dt):
        return nc.dram_tensor(name, list(shape), dt, kind="Internal").ap()

    xs_in = inp("xs", [NST, D])
    xp_in = inp("xp", [NPT, D])
    condT = inp("condT", [128, 8, 2])
    st0 = inp("st0", [DEPTH, 2, 4, 64, 64])
    cdk = inp("cdk", [DEPTH, NCTX, 256])
    cdv = inp("cdv", [DEPTH, NCTX, 256])
    cckv = inp("cckv", [DEPTH, NCTX, 256])
    ckr = inp("ckr", [DEPTH, NCTX, 32])
    pvec = inp("pvec", [128, DEPTH, NPV])
    dlam = inp("dlam", [1, DEPTH, 128])
    w_mod = inp("w_mod", [DEPTH, D, 6 * D])
    w_in = inp("w_in", [DEPTH, D, DIN])
    w_uq = inp("w_uq", [DEPTH, 768, 768])
    w_ukv = inp("w_ukv", [DEPTH, 256, 1024])
    w_out = inp("w_out", [DEPTH, D, D])
    w_gu = inp("w_gu", [DEPTH, D, 2 * DFF])
    w_down = inp("w_down", [DEPTH, DFF, D])
    rope_in = inp("rope", [6, 128, NST])
    cmat_in = inp("cmat", [128, NMAT, 128])

    y_p = outp("y_p", [NPT, D])
    y_s = outp("y_s", [NST, D])
    o_state = outp("o_state", [4, DEPTH, 2, 4, 64, 64])
    o_dk = outp("o_dk", [4, DEPTH, 256, 256])
    o_dv = outp("o_dv", [4, DEPTH, 256, 256])
    o_ckv = outp("o_ckv", [4, DEPTH, 256, 256])
    o_kr = outp("o_kr", [4, DEPTH, 256, 32])

    G = {}
    for g, ntok, nkeys in (("p", NPT, NPT), ("s", NST, NST + NCTX)):
        G[g] = dict(
            ntok=ntok, nkeys=nkeys, cond=0 if g == "p" else 1,
            xT=scr(f"xT_{g}", [8, 128, ntok], F32),
            rqT=scr(f"rqT_{g}", [2, 128, ntok], BF16), rkT=scr(f"rkT_{g}", [2, 128, ntok], BF16),
            rkt=scr(f"rkt_{g}", [ntok, 256], BF16), rvt=scr(f"rvt_{g}", [ntok, 256], BF16),
            rgT=scr(f"rgT_{g}", [2, 128, ntok], F32),
            dqT=scr(f"dqT_{g}", [2, 128, ntok], BF16), dkT=scr(f"dkT_{g}", [2, 128, nkeys], BF16),
            dvt=scr(f"dvt_{g}", [nkeys, 4, 128], BF16),
            mqT=scr(f"mqT_{g}", [8, 96, ntok], BF16), mkT=scr(f"mkT_{g}", [8, 96, nkeys], BF16),
            mvt=scr(f"mvt_{g}", [nkeys, 8, 128], BF16),
            mixT=scr(f"mixT_{g}", [8, 128, ntok], BF16),
        )
    wguS = scr("wguS", [DEPTH, 128, 8, 2 * DFF], BF16)
    dbufs = {}

    def DB(*key):
        if key not in dbufs:
            dbufs[key] = Buf(str(key))
        return dbufs[key]

    es_top = ExitStack()
    with es_top:
        S = Sched(nc, es_top)
        pe, act, dve, pool, sp = S.pe, S.act, S.dve, S.pool, S.sp
        V, A, P, PEh = nc.vector, nc.scalar, nc.gpsimd, nc.tensor

        _uid = [0]

        def sbt(es, name, shape, dt):
            _uid[0] += 1
            name = f"{name}_{_uid[0]}"
            return es.enter_context(nc.sbuf_tensor(name, list(shape), dt)), Buf(name)

        psA = es_top.enter_context(nc.psum_tensor("psA", [128, 2, 512], F32))
        psB = es_top.enter_context(nc.psum_tensor("psB", [128, 2, 512], F32))
        psC = es_top.enter_context(nc.psum_tensor("psC", [128, 512], F32))
        psD = es_top.enter_context(nc.psum_tensor("psD", [128, 512], F32))
        psE2 = es_top.enter_context(nc.psum_tensor("psE2", [128, 2, 512], F32))
        psT = psE2[:, 1, :].bitcast(BF16)
        bA, bB = [Buf("psA0"), Buf("psA1")], [Buf("psB0"), Buf("psB1")]
        bE2 = [Buf("psE0"), Buf("psE1")]
        bC, bD, bE, bT = Buf("psC"), Buf("psD"), bE2[0], bE2[1]
        banks = [(psA[:, 0, :], bA[0]), (psA[:, 1, :], bA[1]), (psB[:, 0, :], bB[0]), (psB[:, 1, :], bB[1]),
                 (psC[:], bC), (psD[:], bD), (psE2[:, 0, :], bE)]

        class Rot:
            def __init__(self, items):
                self.items = items
                self.i = 0

            def next(self):
                it = self.items[self.i % len(self.items)]
                self.i += 1
                return it

        cmf, b_cmf = sbt(es_top, "cmf", [128, NMAT, 128], F32)
        cmb, b_cmb = sbt(es_top, "cmb", [128, NBF, 128], BF16)
        pv, b_pv = sbt(es_top, "pv", [128, DEPTH, NPV], F32)
        modv, b_modv = sbt(es_top, "modv", [128, DEPTH, 48, 2], F32)
        g1v, b_g1v = sbt(es_top, "g1v", [128, DEPTH, 2, 8, 2], F32)
        lamv, b_lamv = sbt(es_top, "lamv", [128, DEPTH, 4], F32)
        S.dma(sp, cmf[:], cmat_in, writes=[b_cmf])
        S.dma(sp, pv[:], pvec, writes=[b_pv])
        S.dma(pool, cmb[:], cmat_in[:, 0:NBF, :], writes=[b_cmb])

        def ident_f():
            return cmf[:, M_I, :]

        def prologue_x():
            with ExitStack() as es:
                xin = [sbt(es, f"pxin{i}", [128, D], F32) for i in range(4)]
                xo = [sbt(es, f"pxo{i}", [128, 8, 128], F32) for i in range(4)]
                rot = Rot(banks[0:4])
                k = 0
                for g, src in (("p", xp_in), ("s", xs_in)):
                    gd = G[g]
                    for blk in range(gd["ntok"] // 128):
                        (xi, bxi), (xoo, bxo) = xin[k % 4], xo[k % 4]
                        k += 1
                        S.dma(sp, xi[:], src[blk * 128:(blk + 1) * 128, :], writes=[bxi])
                        for half in range(2):
                            ps, bps = rot.next()
                            for j in range(4):
                                c = half * 4 + j
                                S.op(pe, lambda ps=ps, j=j, c=c, xi=xi: PEh.transpose(out=ps[:, j * 128:(j + 1) * 128], in_=xi[:, c * 128:(c + 1) * 128], identity=ident_f()),
                                     reads=[bxi, b_cmf], writes=[bps])
                            eng, h = (dve, V) if half == 0 else (act, A)
                            if half == 0:
                                S.op(dve, lambda ps=ps, xoo=xoo, half=half: V.tensor_copy(out=xoo[:, half * 4:(half + 1) * 4, :], in_=ps.rearrange("p (c t) -> p c t", c=4)), reads=[bps], writes=[bxo])
                            else:
                                S.op(act, lambda ps=ps, xoo=xoo, half=half: A.copy(out=xoo[:, half * 4:(half + 1) * 4, :], in_=ps.rearrange("p (c t) -> p c t", c=4)), reads=[bps], writes=[bxo])
                        S.dma(sp, gd["xT"][:, :, blk * 128:(blk + 1) * 128].rearrange("c p t -> p c t"), xoo[:], reads=[bxo], writes=[DB("xT", g, blk // 4)])

        def prologue_mod():
            with ExitStack() as es:
                ct, b_ct = sbt(es, "ct", [128, 8, 2], F32)
                sc, b_sc = sbt(es, "scnd", [128, 8, 2], BF16)
                wm = [sbt(es, f"wm{i}", [128, 8, 512], BF16) for i in range(6)]
                S.dma(sp, ct[:], condT, writes=[b_ct])
                S.op(act, lambda: A.activation(out=sc[:], in_=ct[:], func=AF.Silu), reads=[b_ct], writes=[b_sc])
                k = 0
                for l in range(depth):
                    mps, bmps = banks[4 + (l % 2)]
                    for j in range(12):
                        w, bw = wm[k % 6]
                        k += 1
                        S.dma(pool, w[:], w_mod[l, :, j * 512:(j + 1) * 512].rearrange("(c p) n -> p c n", p=128), writes=[bw])
                        for fc in range(4):
                            col = (j * 4 + fc) * 2
                            for kc in range(8):
                                S.op(pe, lambda w=w, fc=fc, kc=kc, mps=mps, col=col: PEh.matmul(mps[:, col:col + 2], lhsT=w[:, kc, fc * 128:(fc + 1) * 128], rhs=sc[:, kc, :], start=(kc == 0), stop=(kc == 7)),
                                     reads=[bw, b_sc], writes=[bmps], inc=(kc == 7))
                    S.op(dve, lambda l=l, mps=mps: V.tensor_tensor(out=modv[:, l, :, :], in0=mps[:, 0:96].rearrange("p (c j) -> p c j", j=2),
                                                                   in1=pv[:, l, PV_BM:PV_BM + 48].unsqueeze(2).broadcast_to([128, 48, 2]), op=ALU.add),
                         reads=[bmps, b_pv], writes=[b_modv])
                    for wh, (pvn, sc0) in enumerate(((PV_N1, 8), (PV_N2, 32))):
                        S.op(dve, lambda l=l, wh=wh, pvn=pvn, sc0=sc0: V.scalar_tensor_tensor(
                            out=g1v[:, l, wh, :, :], in0=modv[:, l, sc0:sc0 + 8, :], scalar=1.0,
                            in1=pv[:, l, pvn:pvn + 8].unsqueeze(2).broadcast_to([128, 8, 2]), op0=ALU.add, op1=ALU.mult),
                            reads=[b_modv, b_pv], writes=[b_g1v])
                dl, b_dl = sbt(es, "dl", [1, DEPTH, 128], F32)
                pr, b_pr = sbt(es, "dlpr", [1, DEPTH, 2, 32], F32)
                sm, b_sm = sbt(es, "dlsm", [1, DEPTH, 2], F32)
                lm, b_lm = sbt(es, "dllm", [1, DEPTH * 2], F32)
                S.dma(sp, dl[:], dlam, writes=[b_dl])
                S.op(pool, lambda: P.memset(lm[:], 0.0), writes=[b_lm])
                dlv = dl[:].rearrange("o l (a b e) -> o l a b e", a=2, b=2)
                S.op(dve, lambda: V.tensor_tensor(out=pr[:], in0=dlv[:, :, :, 0, :], in1=dlv[:, :, :, 1, :], op=ALU.mult), reads=[b_dl], writes=[b_pr])
                S.op(dve, lambda: V.tensor_reduce(out=sm[:], in_=pr[:], axis=mybir.AxisListType.X, op=ALU.add), reads=[b_pr], writes=[b_sm])
                S.op(act, lambda: A.activation(out=sm[:], in_=sm[:], func=AF.Exp), reads=[b_sm], writes=[b_sm])
                for l in range(depth):
                    lam_init = 0.8 - 0.6 * math.exp(-0.3 * l)
                    S.op(dve, lambda l=l, lam_init=lam_init: V.scalar_tensor_tensor(out=lm[:, 2 * l:2 * l + 1], in0=sm[:, l, 1:2], scalar=-lam_init, in1=sm[:, l, 0:1], op0=ALU.add, op1=ALU.subtract),
                         reads=[b_sm], writes=[b_lm])
                    S.op(pool, lambda l=l, lam_init=lam_init: P.memset(lm[:, 2 * l + 1:2 * l + 2], 1.0 - lam_init), writes=[b_lm])
                ps, bps = banks[6]
                on1, b_on1 = sbt(es, "on1", [1, 128], F32)
                S.op(pool, lambda: P.memset(on1[:], 1.0), writes=[b_on1])
                S.op(pe, lambda: PEh.matmul(ps[:, 0:2 * DEPTH], lhsT=on1[:], rhs=lm[:], start=True, stop=True), reads=[b_on1, b_lm], writes=[bps])
                S.op(dve, lambda: V.tensor_copy(out=lamv[:, :, 0:2], in_=ps[:, 0:2 * DEPTH].rearrange("p (l j) -> p l j", j=2)), reads=[bps], writes=[b_lamv])
                for l in range(depth):
                    S.op(dve, lambda l=l: V.tensor_tensor(out=lamv[:, l, 2:3], in0=pv[:, l, PV_SUB:PV_SUB + 1], in1=lamv[:, l, 1:2], op=ALU.mult), reads=[b_lamv, b_pv], writes=[b_lamv])

        def rstd_from(es_ps, T_, scale, out_sb, b_out, nrows=128):
            ps, bps = es_ps
            S.op(act, lambda: A.activation(out=out_sb[0:nrows, 0:T_], in_=ps[0:nrows, 0:T_], func=AF.Ln, scale=scale, bias=eps_col[0:nrows, :]), reads=[bps, b_eps], writes=[b_out])
            S.op(act, lambda: A.activation(out=out_sb[0:nrows, 0:T_], in_=out_sb[0:nrows, 0:T_], func=AF.Exp, scale=-0.5), reads=[b_out], writes=[b_out])

        eps_col, b_eps = sbt(es_top, "eps_col", [128, 1], F32)
        S.op(pool, lambda: P.memset(eps_col[:], EPS), writes=[b_eps])

        class _Stop(Exception):
            pass

        p1stop = int(os.environ.get("K_P1STOP", "0"))

        def stage(k):
            if p1stop == k:
                S.mute = True

        def phase_p1(l):
            _phase_p1(l)
            S.mute = False
            S.barrier()

        def _phase_p1(l):
            with ExitStack() as es:
                win, b_win = sbt(es, "win", [128, 8, DIN], BF16)
                wuq, b_wuq = sbt(es, "wuq", [128, 6, 768], BF16)
                wukv, b_wukv = sbt(es, "wukv", [128, 2, 1024], BF16)
                for kc in range(8):
                    S.dma(pool, win[:, kc, :], w_in[l, kc * 128:(kc + 1) * 128, :], writes=[b_win])
                S.dma(pool, wuq[:], w_uq[l].rearrange("(c p) n -> p c n", p=128), writes=[b_wuq])
                for kc in range(2):
                    for half in range(2):
                        S.dma(pool, wukv[:, kc, half * 512:(half + 1) * 512].rearrange("p (h x) -> p h x", x=64),
                              w_ukv[l, kc * 128:(kc + 1) * 128, :].rearrange("p (h x) -> p h x", x=128)[:, :, half * 64:(half + 1) * 64], writes=[b_wukv])
                x_sb, b_x = sbt(es, "p1x", [128, 8, T], F32)
                h, b_h = sbt(es, "p1h", [128, 8, T], BF16)
                rstd, b_rstd = sbt(es, "p1rstd", [128, T], F32)
                ropet, b_rope = sbt(es, "p1rope", [128, 6, T], F32)
                cqn, b_cqn = sbt(es, "p1cqn", [128, 6, T], BF16)
                ckvn, b_ckvn = sbt(es, "p1ckvn", [128, 2, T], BF16)
                krsb, b_krsb = sbt(es, "p1kr", [96, T], F32)
                kr0, b_kr0 = sbt(es, "p1kr0", [32, T], F32)
                sq96, b_sq96 = sbt(es, "p1sq96", [96, T], BF16)
                tmpf = [sbt(es, f"p1tf{i}", [128, T], F32) for i in range(1)]
                tmpb = [sbt(es, f"p1tb{i}", [128, T], BF16) for i in range(3)]
                tokf = [sbt(es, f"p1tokf{i}", [128, 512], F32) for i in range(2)]
                stg = {}
                for nm, shp, dt in (("rq", [128, 2, T], BF16), ("rk", [128, 2, T], BF16), ("rg", [128, 2, T], F32), ("dq", [128, 2, T], BF16),
                                    ("dk", [128, 2, T], BF16), ("mq", [96, 8, T], BF16),
                                    ("rkt", [128, 4, 256], BF16), ("rvdv", [128, 4, 512], BF16), ("mvt", [128, 4, 512], BF16)):
                    stg[nm] = sbt(es, "p1s_" + nm, shp, dt)
                stg["mk"] = stg["mq"]
                rf, rb = Rot(tmpf), Rot(tmpb)
                rtok = Rot(tokf)
                pm = Rot([banks[0], banks[1], banks[2]])
                pn = Rot([banks[4], banks[5]])
                prr = Rot([banks[6], (psE2[:, 1, :], bE2[1])])

                def wchunk(col0, m=128):
                    return lambda kc: win[:, kc, col0:col0 + m]

                def proj(wfn, src, nk, T_, m=128, out_rows=None, tp=None):
                    ps, bps = pm.next()
                    o = ps[0:m, 0:T_] if out_rows is None else ps[out_rows[0]:out_rows[1], 0:T_]
                    for kc in range(nk):
                        kw = {} if tp is None else dict(tile_position=tp)
                        S.op(pe, lambda kc=kc: PEh.matmul(o, lhsT=wfn(kc), rhs=src[0][:, kc, 0:T_], start=(kc == 0), stop=(kc == nk - 1), **kw),
                             reads=[src[1], b_win, b_wuq, b_wukv], writes=[bps], inc=(kc == nk - 1))
                    return ps, bps

                def group_ss(sq_ap, b_sqb, mat, nrows, T_):
                    ps, bps = pn.next()
                    S.op(pe, lambda: PEh.matmul(ps[0:nrows, 0:T_], lhsT=cmb[0:nrows, mat, 0:nrows], rhs=sq_ap, start=True, stop=True), reads=[b_sqb, b_cmb], writes=[bps])
                    return ps, bps

                def rope_apply(xs, b_xs, nrows, permmat, ci, T_, out_ap, b_outs):
                    ps, bps = prr.next()
                    S.op(pe, lambda: PEh.matmul(ps[0:nrows, 0:T_], lhsT=cmb[0:nrows, permmat, 0:nrows], rhs=xs[0:nrows, 0:T_], start=True, stop=True), reads=[b_xs, b_cmb], writes=[bps])
                    t1, bt1 = rf.next()
                    t2, bt2 = rf.next()
                    S.op(pool, lambda: P.tensor_tensor(out=t1[0:nrows, 0:T_], in0=xs[0:nrows, 0:T_], in1=ropet[0:nrows, ci, 0:T_], op=ALU.mult), reads=[b_xs, b_rope], writes=[bt1])
                    S.op(dve, lambda: V.tensor_tensor(out=t2[0:nrows, 0:T_], in0=ps[0:nrows, 0:T_], in1=ropet[0:nrows, ci + 1, 0:T_], op=ALU.mult), reads=[bps, b_rope], writes=[bt2])
                    S.op(pool, lambda: P.tensor_tensor(out=out_ap, in0=t1[0:nrows, 0:T_], in1=t2[0:nrows, 0:T_], op=ALU.add), reads=[bt1, bt2], writes=b_outs)

                def mla_kv(T_, rope, g, key0, out_tokmajor_blocks):
                    gd = G[g]
                    mk, b_mk = stg["mk"]
                    mv, b_mv = stg["mvt"]
                    if T_ != T:
                        for j in range(NS):
                            S.op(act, lambda j=j: A.activation(out=SQ96[j][64:96, 0:T_], in_=krsb[64:96, 0:T_], func=AF.Square), reads=[b_krsb], writes=[bSQ96[j]])
                    gens_ = []

                    def mmk(ps, bps, hd):
                        for kc in range(2):
                            S.op(pe, lambda kc=kc: PEh.matmul(ps[0:64, 0:T_], lhsT=wukv[:, kc, hd * 64:hd * 64 + 64], rhs=ckvn[:, kc, 0:T_], start=(kc == 0), stop=(kc == 1)),
                                 reads=[b_ckvn, b_wukv], writes=[bps], inc=(kc == 1))

                    def g_mv(slot, tb):
                        ps, bps = pmS[slot]
                        for kc in range(2):
                            S.op(pe, lambda kc=kc: PEh.matmul(ps[:, 0:512], lhsT=ckvn[:, kc, tb * 128:(tb + 1) * 128], rhs=wukv[:, kc, 512:1024], start=(kc == 0), stop=(kc == 1)),
                                 reads=[b_ckvn, b_wukv], writes=[bps], inc=(kc == 1))
                        S.op(act, lambda: A.copy(out=mv[:, tb, :], in_=ps[:]), reads=[bps], writes=[b_mv])
                        yield

                    for hd in range(8):
                        gens_.append(g_norm(len(gens_) % NS, lambda ps, bps, hd=hd: mmk(ps, bps, hd), 96, M_ONES, 1.0 / 96, PV_MKG, rope, M_P96, 4, mk[0:96, hd, 0:T_], b_mk, None, T_))
                    for tb in range(T_ // 128):
                        gens_.append(g_mv(len(gens_) % NS, tb))
                    run_pipe(gens_)
                    S.dma(sp, gd["mkT"][:, :, key0:key0 + T_].rearrange("h p t -> p h t"), mk[:, :, 0:T_], reads=[b_mk], writes=[DB("mkT", g, key0 // T)])
                    for tb in range(T_ // 128):
                        S.dma(sp, gd["mvt"][key0 + tb * 128:key0 + (tb + 1) * 128, :, 0:64], mv[:, tb, :].rearrange("p (h e) -> p h e", e=64), reads=[b_mv], writes=[DB("mvt", g, key0 // T)])

                def transpose_out(src_ap_fn, b_src, nchunks, width, T_, dst_fn):
                    for tb in range(T_ // 128):
                        ps, bps = pm.next()
                        for c in range(nchunks):
                            S.op(pe, lambda c=c, tb=tb: PEh.transpose(out=ps[:, c * width:(c + 1) * width], in_=src_ap_fn(c)[:, tb * 128:(tb + 1) * 128], identity=cmf[0:width, M_I, 0:width]),
                                 reads=[b_src, b_cmf], writes=[bps])
                        tk, btk = rtok.next()
                        S.op(act, lambda: A.copy(out=tk[:, 0:nchunks * width], in_=ps[:, 0:nchunks * width]), reads=[bps], writes=[btk])
                        dst_fn(tb, tk[:, 0:nchunks * width], btk)

                NS = 4
                R, bR, T1, bT1, T2, bT2, XS, bXS, SQ, bSQ, SQ96, bSQ96 = [], [], [], [], [], [], [], [], [], [], [], []
                for j in range(NS):
                    for lst, blst, nm_, shp, dt in ((R, bR, "R", [128, T], F32), (T1, bT1, "T1", [128, T], F32), (T2, bT2, "T2", [128, T], F32),
                                                   (XS, bXS, "XS", [128, T], BF16), (SQ, bSQ, "SQ", [128, T], BF16), (SQ96, bSQ96, "SQ96", [96, T], BF16)):
                        t_, b_ = sbt(es, f"p1{nm_}{j}", shp, dt)
                        lst.append(t_)
                        blst.append(b_)
                pmS = [banks[0], banks[1], banks[2], banks[3]]
                kf = [sbt(es, f"p1kf{i}", [128, T], F32) for i in range(2)]

                def run_pipe(gens_):
                    active = []
                    it = iter(gens_)
                    while True:
                        try:
                            active.insert(0, next(it))
                        except StopIteration:
                            if not active:
                                break
                        for gg in list(active):
                            try:
                                next(gg)
                            except StopIteration:
                                active.remove(gg)

                def mm_proj(ps, bps, wfn, src, nk, T_, m):
                    for kc in range(nk):
                        S.op(pe, lambda kc=kc: PEh.matmul(ps[0:m, 0:T_], lhsT=wfn(kc), rhs=src[0][:, kc, 0:T_], start=(kc == 0), stop=(kc == nk - 1)),
                             reads=[src[1], b_win, b_wuq, b_wukv], writes=[bps], inc=(kc == nk - 1))

                def rope_slot(slot, nrows, permmat, ci, T_, out_ap, b_out):
                    ps, bps = prr.next()
                    xs, b_xs = XS[slot], bXS[slot]
                    t1, bt1, t2, bt2 = T1[slot], bT1[slot], T2[slot], bT2[slot]
                    S.op(pe, lambda: PEh.matmul(ps[0:nrows, 0:T_], lhsT=cmb[0:nrows, permmat, 0:nrows], rhs=xs[0:nrows, 0:T_], start=True, stop=True), reads=[b_xs, b_cmb], writes=[bps])
                    S.op(pool, lambda: P.tensor_tensor(out=t1[0:nrows, 0:T_], in0=xs[0:nrows, 0:T_], in1=ropet[0:nrows, ci, 0:T_], op=ALU.mult), reads=[b_xs, b_rope], writes=[bt1])
                    S.op(dve, lambda: V.tensor_tensor(out=t2[0:nrows, 0:T_], in0=ps[0:nrows, 0:T_], in1=ropet[0:nrows, ci + 1, 0:T_], op=ALU.mult), reads=[bps, b_rope], writes=[bt2])
                    S.op(pool, lambda: P.tensor_tensor(out=out_ap, in0=t1[0:nrows, 0:T_], in1=t2[0:nrows, 0:T_], op=ALU.add), reads=[bt1, bt2], writes=[b_out])

                def g_plain(slot, wfn, src, scl, roped, permmat, ci, out_ap, b_out, _unused):
                    ps, bps = pmS[slot]
                    mm_proj(ps, bps, wfn, src, 8, T, 128)
                    if roped:
                        S.op(act, lambda: A.activation(out=XS[slot][:], in_=ps[:], func=AF.Identity, scale=scl), reads=[bps], writes=[bXS[slot]])
                        yield
                        rope_slot(slot, 128, permmat, ci, T, out_ap, b_out)
                    else:
                        S.op(act, lambda: A.activation(out=out_ap, in_=ps[:], func=AF.Identity, scale=scl), reads=[bps], writes=[b_out])

                def g_norm(slot, mmfn, nrows, ssmat, inv, gcol, roped, permmat, ci, out_ap, b_out, f32o, extra):
                    ps, bps = pmS[slot]
                    mmfn(ps, bps)
                    if extra is None:
                        sq_, bsq_ = SQ[slot], bSQ[slot]
                        S.op(act, lambda: A.activation(out=sq_[0:nrows, :], in_=ps[0:nrows, :], func=AF.Square), reads=[bps], writes=[bsq_])
                        T_ = T
                        n0 = nrows
                    else:
                        sq_, bsq_, T_ = SQ96[slot], bSQ96[slot], extra
                        S.op(act, lambda: A.activation(out=sq_[0:64, 0:T_], in_=ps[0:64, 0:T_], func=AF.Square), reads=[bps], writes=[bsq_])
                        n0 = 64
                    yield
                    yield
                    ss, bss = pn.next()
                    S.op(pe, lambda: PEh.matmul(ss[0:nrows, 0:T_], lhsT=cmb[0:nrows, ssmat, 0:nrows], rhs=sq_[0:nrows, 0:T_], start=True, stop=True), reads=[bsq_, b_cmb], writes=[bss])
                    r, br = R[slot], bR[slot]
                    rstd_from((ss, bss), T_, inv, r, br, nrows=nrows)
                    if f32o is not None:
                        S.op(dve, lambda: V.scalar_tensor_tensor(out=f32o[0][:], in0=ps[:], scalar=pv[:, l, gcol:gcol + 1], in1=r[:], op0=ALU.mult, op1=ALU.mult), reads=[bps, br, b_pv], writes=[f32o[1]])
                        S.op(pool, lambda: P.tensor_copy(out=out_ap, in_=f32o[0][:]), reads=[f32o[1]], writes=[b_out])
                        return
                    if roped:
                        d0, d1, bd = XS[slot][0:n0, 0:T_], XS[slot][64:96, 0:T_], bXS[slot]
                    else:
                        d0, bd = out_ap[0:n0], b_out
                        d1 = out_ap[64:96] if extra is not None else None
                    S.op(dve, lambda: V.scalar_tensor_tensor(out=d0, in0=ps[0:n0, 0:T_], scalar=pv[0:n0, l, gcol:gcol + 1], in1=r[0:n0, 0:T_], op0=ALU.mult, op1=ALU.mult), reads=[bps, br, b_pv], writes=[bd])
                    if extra is not None:
                        S.op(dve, lambda: V.scalar_tensor_tensor(out=d1, in0=krsb[64:96, 0:T_], scalar=pv[64:96, l, gcol:gcol + 1], in1=r[64:96, 0:T_], op0=ALU.mult, op1=ALU.mult),
                             reads=[b_krsb, br, b_pv], writes=[bd])
                    if roped:
                        yield
                        rope_slot(slot, nrows, permmat, ci, T_, out_ap, b_out)

                def g_rvdv(slot, tb, t0, is_s):
                    ps, bps = pmS[slot]
                    rvdv, brvdv = stg["rvdv"]
                    for kc in range(8):
                        S.op(pe, lambda kc=kc: PEh.matmul(ps[:, 0:256], lhsT=h[:, kc, tb * 128:(tb + 1) * 128], rhs=win[:, kc, O_RV:O_RV + 256], start=(kc == 0), stop=(kc == 7)),
                             reads=[b_h, b_win], writes=[bps], inc=False)
                    for kc in range(8):
                        S.op(pe, lambda kc=kc: PEh.matmul(ps[:, 256:512], lhsT=h[:, kc, tb * 128:(tb + 1) * 128], rhs=win[:, kc, O_DV:O_DV + 256], start=(kc == 0), stop=(kc == 7)),
                             reads=[b_h, b_win], writes=[bps], inc=(kc == 7))
                    S.op(act, lambda: A.copy(out=rvdv[:, tb, :], in_=ps[:]), reads=[bps], writes=[brvdv])
                    if not is_s:
                        tk, btk = rtok.next()
                        S.op(dve, lambda: V.tensor_copy(out=tk[:, 0:256], in_=ps[:, 256:512]), reads=[bps], writes=[btk])
                        tok = t0 + tb * 128
                        S.dma(sp, o_dv[tok // 256, l, tok % 256:tok % 256 + 128, :], tk[:, 0:256], reads=[btk])
                    yield

                def g_cq(slot, wfn, xf, ssbank, c, n):
                    ps, bps = pmS[slot]
                    mm_proj(ps, bps, wfn, (h, b_h), 8, T, 128)
                    S.op(act, lambda: A.copy(out=xf, in_=ps[:]), reads=[bps], writes=[b_x])
                    S.op(pool, lambda: P.tensor_tensor(out=SQ[slot][:], in0=xf, in1=xf, op=ALU.mult), reads=[b_x], writes=[bSQ[slot]])
                    yield
                    S.op(pe, lambda: PEh.matmul(ssbank[0][:, 0:T], lhsT=cmb[:, M_ONES, :], rhs=SQ[slot][:], start=(c == 0), stop=(c == n - 1)), reads=[bSQ[slot], b_cmb], writes=[ssbank[1]])

                def g_kr(slot):
                    ps, bps = pmS[slot]
                    mm_proj(ps, bps, lambda kc: win[:, kc, O_KR:O_KR + 32], (h, b_h), 8, T, 32)
                    S.op(dve, lambda: V.tensor_copy(out=krsb[64:96, :], in_=ps[0:32, :]), reads=[bps], writes=[b_krsb])
                    S.op(dve, lambda: V.tensor_copy(out=kr0[0:32, :], in_=ps[0:32, :]), reads=[bps], writes=[b_kr0])
                    yield

                for g in ("p", "s"):
                    gd = G[g]
                    cond = gd["cond"]
                    is_s = (g == "s")
                    for t in range(gd["ntok"] // T):
                        t0 = t * T
                        S.dma(sp, x_sb[:], gd["xT"][:, :, t0:t0 + T].rearrange("c p t -> p c t"), reads=[DB("xT", g, t)], writes=[b_x])
                        if is_s:
                            S.dma(sp, ropet[:], rope_in[:, :, t0:t0 + T].rearrange("k p t -> p k t"), writes=[b_rope])
                        ss, bss = pn.next()
                        for c in range(8):
                            sqb, bsqb = rb.next()
                            S.op(act, lambda c=c, sqb=sqb: A.activation(out=sqb[:], in_=x_sb[:, c, :], func=AF.Square), reads=[b_x], writes=[bsqb])
                            S.op(pe, lambda c=c, sqb=sqb: PEh.matmul(ss[:, 0:T], lhsT=cmb[:, M_ONES, :], rhs=sqb[:], start=(c == 0), stop=(c == 7)), reads=[bsqb, b_cmb], writes=[bss])
                        rstd_from((ss, bss), T, 1.0 / D, rstd, b_rstd)
                        S.op(dve, lambda: V.tensor_tensor(out=x_sb[:], in0=x_sb[:], in1=rstd[:].unsqueeze(1).broadcast_to([128, 8, T]), op=ALU.mult), reads=[b_x, b_rstd], writes=[b_x])
                        for c in range(8):
                            S.op(act, lambda c=c: A.activation(out=h[:, c, :], in_=x_sb[:, c, :], func=AF.Identity, scale=g1v[:, l, 0, c, cond:cond + 1], bias=modv[:, l, c, cond:cond + 1]),
                                 reads=[b_x, b_g1v, b_modv], writes=[b_h])
                        hs = (h, b_h)
                        stage(1)
                        rq_st, rk_st, rg_st, dq_st, dk_st = stg["rq"], stg["rk"], stg["rg"], stg["dq"], stg["dk"]
                        need_f32 = not is_s
                        gens = []

                        def add(gf, *a):
                            gens.append(gf(len(gens) % NS, *a))

                        for nm, off, scl in (("rq", O_RQ, 1.0), ("rk", O_RK, 0.125)):
                            for c in range(2):
                                add(g_plain, wchunk(off + c * 128), hs, scl, is_s, M_P64, 0, stg[nm][0][:, c, :], stg[nm][1], None)
                        for c in range(2):
                            add(g_plain, wchunk(O_RG + c * 128), hs, 1.0, False, 0, 0, rg_st[0][:, c, :], rg_st[1], None)
                        for nm, off, gcol in (("dq", O_DQ, PV_DQG), ("dk", O_DK, PV_DKG)):
                            for c in range(2):
                                f32o = kf[c] if (nm == "dk" and need_f32) else None
                                add(g_norm, lambda ps, bps, off=off, c=c: mm_proj(ps, bps, wchunk(off + c * 128), hs, 8, T, 128), 128, M_BD32, 1.0 / 32, gcol,
                                    is_s, M_P32, 2, stg[nm][0][:, c, :], stg[nm][1], f32o, None)
                        for tb in range(4):
                            add(g_rvdv, tb, t0, is_s)
                        ssq_cq, ssq_ckv = banks[4], banks[5]
                        for c in range(6):
                            add(g_cq, wchunk(O_CQ + c * 128), x_sb[:, c, :], ssq_cq, c, 6)
                        for c in range(2):
                            add(g_cq, wchunk(O_CKV + c * 128), x_sb[:, 6 + c, :], ssq_ckv, c, 2)
                        add(g_kr)
                        run_pipe(gens)
                        for nm in ("rq", "rk", "dq", "dk"):
                            S.dma(sp, gd[nm + "T"][:, :, t0:t0 + T].rearrange("c p t -> p c t"), stg[nm][0][:], reads=[stg[nm][1]], writes=[DB(nm + "T", g, t)])
                        S.dma(sp, gd["rgT"][:, :, t0:t0 + T].rearrange("c p t -> p c t"), rg_st[0][:], reads=[rg_st[1]], writes=[DB("rgT", g, t)])
                        rvdv, brvdv = stg["rvdv"]
                        S.dma(sp, gd["rvt"][t0:t0 + T, :].rearrange("(b p) n -> p b n", p=128), rvdv[:, :, 0:256], reads=[brvdv], writes=[DB("rvt", g, t)])
                        for tb in range(4):
                            S.dma(sp, gd["dvt"][t0 + tb * 128:t0 + (tb + 1) * 128, :, 0:64], rvdv[:, tb, 256:512].rearrange("p (h e) -> p h e", e=64), reads=[brvdv], writes=[DB("dvt", g, t)])
                        rstd_from(ssq_cq, T, 1.0 / 768, rstd, b_rstd)
                        for c in range(6):
                            S.op(dve, lambda c=c: V.scalar_tensor_tensor(out=cqn[:, c, :], in0=x_sb[:, c, :], scalar=pv[:, l, PV_QN + c:PV_QN + c + 1], in1=rstd[:], op0=ALU.mult, op1=ALU.mult),
                                 reads=[b_x, b_rstd, b_pv], writes=[b_cqn])
                        r2, br2 = kf[0] if is_s else (R[0], bR[0])
                        rstd_from(ssq_ckv, T, 1.0 / 256, r2, br2)
                        for c in range(2):
                            S.op(dve, lambda c=c: V.scalar_tensor_tensor(out=x_sb[:, 6 + c, :], in0=x_sb[:, 6 + c, :], scalar=pv[:, l, PV_KVN + c:PV_KVN + c + 1], in1=r2[:], op0=ALU.mult, op1=ALU.mult),
                                 reads=[b_x, br2, b_pv], writes=[b_x])
                            S.op(pool, lambda c=c: P.tensor_copy(out=ckvn[:, c, :], in_=x_sb[:, 6 + c, :]), reads=[b_x], writes=[b_ckvn])
                        for j in range(NS):
                            S.op(act, lambda j=j: A.activation(out=SQ96[j][64:96, :], in_=krsb[64:96, :], func=AF.Square), reads=[b_krsb], writes=[bSQ96[j]])
                        mq, b_mq = stg["mq"]
                        gens = []
                        for hd in range(8):
                            add(g_norm, lambda ps, bps, hd=hd: mm_proj(ps, bps, lambda kc, hd=hd: wuq[:, kc, hd * 96:(hd + 1) * 96], (cqn, b_cqn), 6, T, 96), 96, M_ONES, 1.0 / 96, PV_MQG,
                                is_s, M_P96, 4, mq[0:96, hd, :], b_mq, None, None)
                        run_pipe(gens)
                        S.dma(sp, gd["mqT"][:, :, t0:t0 + T].rearrange("h p t -> p h t"), mq[:], reads=[b_mq], writes=[DB("mqT", g, t)])
                        rk_, brk = stg["rk"]
                        rkt_, brkt = stg["rkt"]
                        for tb in range(4):
                            for c in range(2):
                                S.op(pe, lambda c=c, tb=tb: PEh.transpose(out=psT[:, (tb % 2) * 512 + c * 128:(tb % 2) * 512 + (c + 1) * 128], in_=rk_[:, c, tb * 128:(tb + 1) * 128], identity=cmb[:, M_I, :]),
                                     reads=[brk, b_cmb], writes=[bT])
                            S.op(dve, lambda tb=tb: V.tensor_copy(out=rkt_[:, tb, :], in_=psT[:, (tb % 2) * 512:(tb % 2) * 512 + 256]), reads=[bT], writes=[brkt])
                        S.dma(sp, gd["rkt"][t0:t0 + T, :].rearrange("(b p) n -> p b n", p=128), rkt_[:], reads=[brkt], writes=[DB("rkt", g, t)])
                        if not is_s:
                            for tb in range(4):
                                tok = t0 + tb * 128
                                ps, bps = pmS[tb % NS]
                                for c in range(2):
                                    S.op(pe, lambda c=c, tb=tb, ps=ps: PEh.transpose(out=ps[:, c * 128:(c + 1) * 128], in_=kf[c][0][:, tb * 128:(tb + 1) * 128], identity=cmf[:, M_I, :]),
                                         reads=[kf[c][1], b_cmf], writes=[bps])
                                tk, btk = rtok.next()
                                S.op(act, lambda ps=ps, tk=tk: A.copy(out=tk[:, 0:256], in_=ps[:, 0:256]), reads=[bps], writes=[btk])
                                S.dma(sp, o_dk[tok // 256, l, tok % 256:tok % 256 + 128, :], tk[:, 0:256], reads=[btk])
                            for tb in range(4):
                                tok = t0 + tb * 128
                                ps2, bps2 = pmS[(tb + 1) % NS]
                                for c in range(2):
                                    S.op(pe, lambda c=c, tb=tb, ps2=ps2: PEh.transpose(out=ps2[:, c * 128:(c + 1) * 128], in_=x_sb[:, 6 + c, tb * 128:(tb + 1) * 128], identity=cmf[:, M_I, :]),
                                         reads=[b_x, b_cmf], writes=[bps2])
                                S.op(pe, lambda tb=tb, ps2=ps2: PEh.transpose(out=ps2[:, 256:288], in_=kr0[0:32, tb * 128:(tb + 1) * 128], identity=cmf[0:32, M_I, 0:32]),
                                     reads=[b_kr0, b_cmf], writes=[bps2])
                                tk, btk = rtok.next()
                                S.op(act, lambda ps2=ps2, tk=tk: A.copy(out=tk[:, 0:288], in_=ps2[:, 0:288]), reads=[bps2], writes=[btk])
                                S.dma(sp, o_ckv[tok // 256, l, tok % 256:tok % 256 + 128, :], tk[:, 0:256], reads=[btk])
                                S.dma(sp, o_kr[tok // 256, l, tok % 256:tok % 256 + 128, :], tk[:, 256:288], reads=[btk])
                        stage(6)
                        mla_kv(T, is_s, g, t0, None)
                        stage(7)
                    if is_s:
                        T_ = NCTX
                        ctk = [sbt(es, f"p1ctk{i}", [128, 2, 256], F32) for i in range(2)]
                        ckr_t, b_ckr_t = sbt(es, "p1ckrt", [128, 2, 32], F32)
                        dkc, b_dkc = sbt(es, "p1dkc", [128, 2, T_], BF16)
                        S.dma(sp, ctk[0][0][:], cdk[l].rearrange("(b p) n -> p b n", p=128), writes=[ctk[0][1]])
                        S.dma(sp, ctk[1][0][:], cckv[l].rearrange("(b p) n -> p b n", p=128), writes=[ctk[1][1]])
                        S.dma(sp, ckr_t[:], ckr[l].rearrange("(b p) n -> p b n", p=128), writes=[b_ckr_t])
                        for which, (dstt, bdst) in enumerate(((dkc, b_dkc), (ckvn, b_ckvn))):
                            src, bsrc = ctk[which]
                            for c in range(2):
                                ps, bps = pm.next()
                                for tb in range(2):
                                    S.op(pe, lambda c=c, tb=tb, src=src, ps=ps: PEh.transpose(out=ps[:, tb * 128:(tb + 1) * 128], in_=src[:, tb, c * 128:(c + 1) * 128], identity=cmf[:, M_I, :]),
                                         reads=[bsrc, b_cmf], writes=[bps])
                                S.op(act, lambda c=c, ps=ps, dstt=dstt: A.copy(out=dstt[:, c, 0:T_], in_=ps[:, 0:T_]), reads=[bps], writes=[bdst])
                        S.dma(sp, gd["dkT"][:, :, NST:NST + T_].rearrange("c p t -> p c t"), dkc[:], reads=[b_dkc], writes=[DB("dkT", g, 8)])
                        S.dma(pool, gd["dvt"][NST:NST + T_, :, 0:64], cdv[l].rearrange("t (h e) -> t h e", e=64), writes=[DB("dvt", g, 8)])
                        ps, bps = pm.next()
                        for tb in range(2):
                            S.op(pe, lambda tb=tb, ps=ps: PEh.transpose(out=ps[0:32, tb * 128:(tb + 1) * 128], in_=ckr_t[:, tb, :], identity=cmf[:, M_I, :]), reads=[b_ckr_t, b_cmf], writes=[bps])
                        S.op(dve, lambda ps=ps: V.tensor_copy(out=krsb[64:96, 0:T_], in_=ps[0:32, 0:T_]), reads=[bps], writes=[b_krsb])
                        mla_kv(T_, False, g, NST, None)
            S.barrier()

        def phase_p2(l):
            with ExitStack() as es:
                lg, b_lg = sbt(es, "p2lg", [128, 16], F32)
                mc, b_mc = sbt(es, "p2mc", [128, 4, 128], F32)
                mtmp, b_mtmp = sbt(es, "p2mtmp", [128, 128], F32)
                qd, b_qd = sbt(es, "p2qd", [128, 2, 2, 128], F32)
                kdt, b_kdt = sbt(es, "p2kd", [128, 2, 4], F32)
                cdp, b_cdp = sbt(es, "p2cd", [128, 2, 2], F32)
                S.op(act, lambda: A.activation(out=lg[:], in_=pv[:, l, PV_RD:PV_RD + 16], func=AF.Exp, scale=-1.0), reads=[b_pv], writes=[b_lg])
                S.op(act, lambda: A.activation(out=lg[:], in_=lg[:], func=AF.Ln, bias=1.0), reads=[b_lg], writes=[b_lg])
                S.op(dve, lambda: V.tensor_scalar(out=lg[:], in0=lg[:], scalar1=-1.0, scalar2=None, op0=ALU.mult), reads=[b_lg], writes=[b_lg])
                for hd in range(4):
                    S.op(act, lambda hd=hd: A.activation(out=mc[:, hd, :], in_=cmf[:, M_DISTF, :], func=AF.Exp, scale=lg[:, hd:hd + 1]), reads=[b_cmf, b_lg], writes=[b_mc])
                    S.op(dve, lambda hd=hd: V.tensor_tensor(out=mc[:, hd, :], in0=mc[:, hd, :], in1=cmf[:, M_MASKF, :], op=ALU.mult), reads=[b_mc, b_cmf], writes=[b_mc])
                    S.op(act, lambda hd=hd: A.activation(out=mtmp[:], in_=cmf[:, M_DISTB, :], func=AF.Exp, scale=lg[:, 4 + hd:5 + hd]), reads=[b_cmf, b_lg], writes=[b_mtmp])
                    S.op(dve, lambda: V.tensor_tensor(out=mtmp[:], in0=mtmp[:], in1=cmf[:, M_MASKB, :], op=ALU.mult), reads=[b_mtmp, b_cmf], writes=[b_mtmp])
                    S.op(dve, lambda hd=hd: V.tensor_tensor(out=mc[:, hd, :], in0=mc[:, hd, :], in1=mtmp[:], op=ALU.add), reads=[b_mc, b_mtmp], writes=[b_mc])
                    S.op(act, lambda hd=hd: A.activation(out=kdt[:, 0, hd:hd + 1], in_=cmf[:, M_MISC, 0:1], func=AF.Exp, scale=lg[:, hd:hd + 1]), reads=[b_cmf, b_lg], writes=[b_kdt])
                    S.op(act, lambda hd=hd: A.activation(out=kdt[:, 1, hd:hd + 1], in_=cmf[:, M_MISC, 1:2], func=AF.Exp, scale=lg[:, 4 + hd:5 + hd]), reads=[b_cmf, b_lg], writes=[b_kdt])
                for dr in range(2):
                    for hp in range(2):
                        sc_ = lg[:, 8 + dr * 2 + hp:9 + dr * 2 + hp]
                        S.op(act, lambda dr=dr, hp=hp, sc_=sc_: A.activation(out=qd[:, dr, hp, :], in_=cmf[:, M_NP1 if dr == 0 else M_CMN, :], func=AF.Exp, scale=sc_), reads=[b_cmf, b_lg], writes=[b_qd])
                        S.op(act, lambda dr=dr, hp=hp, sc_=sc_: A.activation(out=cdp[:, dr, hp:hp + 1], in_=sc_, func=AF.Exp, scale=128.0), reads=[b_lg], writes=[b_cdp])
                for g in ("p", "s"):
                    gd = G[g]
                    nseq, n = (4, 256) if g == "p" else (1, NST)
                    nch = n // 128
                    with ExitStack() as es2:
                        qT, b_qT = sbt(es2, "p2q", [128, 2, n], BF16)
                        kT, b_kT = sbt(es2, "p2k", [128, 2, n], BF16)
                        kt, b_kt = sbt(es2, "p2kt", [128, nch, 256], BF16)
                        vt, b_vt = sbt(es2, "p2vt", [128, nch, 256], BF16)
                        Sst = [[sbt(es2, f"p2S{dr}{hp}", [128, 128], F32) for hp in range(2)] for dr in range(2)]
                        Ssn = [[sbt(es2, f"p2Sn{dr}{hp}", [128, nch, 128], BF16) for hp in range(2)] for dr in range(2)]
                        kd_ = [sbt(es2, f"p2kdt{i}", [128, 256], BF16) for i in range(2)]
                        qdd = [sbt(es2, f"p2qdd{i}", [128, 2, 128], BF16) for i in range(4)]
                        attm = [sbt(es2, f"p2att{i}", [128, 128], BF16) for i in range(3)]
                        rg_, b_rg = sbt(es2, "p2rg", [128, 2, T], F32)
                        osb, b_osb = sbt(es2, "p2osb", [128, T], F32)
                        obf, b_obf = sbt(es2, "p2obf", [128, T], BF16)
                        osq, b_osq = sbt(es2, "p2osq", [128, T], BF16)
                        m2, b_m2 = sbt(es2, "p2m2", [128, T], F32)
                        var, b_var = sbt(es2, "p2var", [128, T], F32)
                        mixo, b_mixo = sbt(es2, "p2mix", [128, 2, T], BF16)
                        rkd, rqd, ratt = Rot(kd_), Rot(qdd), Rot(attm)
                        pst = Rot([banks[0], banks[1]])
                        patt = Rot([banks[2], banks[3]])
                        po = Rot([banks[4], banks[5]])
                        pnrm = Rot([banks[6], banks[0], banks[1]])
                        for sq_i in range(nseq):
                            t0 = sq_i * n
                            tiles = range(t0 // T, (t0 + n + T - 1) // T)
                            S.dma(sp, qT[:], gd["rqT"][:, :, t0:t0 + n].rearrange("c p t -> p c t"), reads=[DB("rqT", g, t) for t in tiles], writes=[b_qT])
                            S.dma(sp, kT[:], gd["rkT"][:, :, t0:t0 + n].rearrange("c p t -> p c t"), reads=[DB("rkT", g, t) for t in tiles], writes=[b_kT])
                            S.dma(sp, kt[:], gd["rkt"][t0:t0 + n, :].rearrange("(b p) n -> p b n", p=128), reads=[DB("rkt", g, t) for t in tiles], writes=[b_kt])
                            S.dma(sp, vt[:], gd["rvt"][t0:t0 + n, :].rearrange("(b p) n -> p b n", p=128), reads=[DB("rvt", g, t) for t in tiles], writes=[b_vt])
                            for dr in range(2):
                                for hp in range(2):
                                    st, bst = Sst[dr][hp]
                                    S.op(pool, lambda st=st: P.memset(st[:], 0.0), writes=[bst])
                                    if g == "s":
                                        for j in range(2):
                                            S.dma(sp, st[j * 64:(j + 1) * 64, j * 64:(j + 1) * 64], st0[l, dr, hp * 2 + j], writes=[bst])
                            for dr in range(2):
                                order = range(nch) if dr == 0 else range(nch - 1, -1, -1)
                                for i in order:
                                    kd, bkd = rkd.next()
                                    S.op(pool, lambda i=i, dr=dr, kd=kd: P.tensor_tensor(out=kd[:].rearrange("p (h d) -> p h d", d=64), in0=kt[:, i, :].rearrange("p (h d) -> p h d", d=64),
                                                                                 in1=kdt[:, dr, :].unsqueeze(2).broadcast_to([128, 4, 64]), op=ALU.mult), reads=[b_kt, b_kdt], writes=[bkd])
                                    for hp in range(2):
                                        st, bst = Sst[dr][hp]
                                        sn, bsn = Ssn[dr][hp]
                                        S.op(act, lambda i=i, st=st, sn=sn: A.copy(out=sn[:, i, :], in_=st[:]), reads=[bst], writes=[bsn])
                                        ps, bps = pst.next()
                                        S.op(pe, lambda hp=hp, i=i, kd=kd, ps=ps: PEh.matmul(ps[:, 0:128], lhsT=kd[:, hp * 128:(hp + 1) * 128], rhs=vt[:, i, hp * 128:(hp + 1) * 128], start=True, stop=True),
                                             reads=[bkd, b_vt], writes=[bps])
                                        S.op(dve, lambda st=st, ps=ps, dr=dr, hp=hp: V.scalar_tensor_tensor(out=st[:], in0=st[:], scalar=cdp[:, dr, hp:hp + 1], in1=ps[:, 0:128], op0=ALU.mult, op1=ALU.add),
                                             reads=[bst, bps, b_cdp], writes=[bst])
                                if g == "p":
                                    for hp in range(2):
                                        st, bst = Sst[dr][hp]
                                        for j in range(2):
                                            S.dma(sp, o_state[sq_i, l, dr, hp * 2 + j], st[j * 64:(j + 1) * 64, j * 64:(j + 1) * 64], reads=[bst])
                            for tt in range((n + T - 1) // T):
                                Tn = min(T, n)
                                tg0 = t0 + tt * T
                                S.dma(sp, rg_[:, :, 0:Tn], gd["rgT"][:, :, tg0:tg0 + Tn].rearrange("c p t -> p c t"), reads=[DB("rgT", g, tg0 // T)], writes=[b_rg])
                                for hp in range(2):
                                    ops, bops = po.next()
                                    for ci in range(Tn // 128):
                                        i = tt * (T // 128) + ci
                                        cs = slice(i * 128, (i + 1) * 128)
                                        qq, bqq = rqd.next()
                                        for dr in range(2):
                                            S.op(pool, lambda dr=dr, hp=hp, cs=cs, qq=qq: P.tensor_tensor(out=qq[:, dr, :], in0=qT[:, hp, cs], in1=qd[:, dr, hp, :], op=ALU.mult), reads=[b_qT, b_qd], writes=[bqq])
                                        for j in range(2):
                                            hd = hp * 2 + j
                                            pb = j * 64
                                            aps, baps = patt.next()
                                            S.op(pe, lambda pb=pb, hp=hp, cs=cs, aps=aps: PEh.matmul(aps[:, 0:128], lhsT=kT[pb:pb + 64, hp, cs], rhs=qT[pb:pb + 64, hp, cs], start=True, stop=True, tile_position=(pb, 0)),
                                                 reads=[b_kT, b_qT], writes=[baps])
                                            am, bam = ratt.next()
                                            S.op(dve, lambda hd=hd, aps=aps, am=am: V.tensor_tensor(out=am[:], in0=aps[:, 0:128], in1=mc[:, hd, :], op=ALU.mult), reads=[baps, b_mc], writes=[bam])
                                            o = ops[pb:pb + 64, ci * 128:(ci + 1) * 128]
                                            S.op(pe, lambda o=o, hd=hd, i=i, am=am, pb=pb: PEh.matmul(o, lhsT=vt[:, i, hd * 64:(hd + 1) * 64], rhs=am[:], start=True, stop=False, tile_position=(0, pb)),
                                                 reads=[b_vt, bam], writes=[bops])
                                            for dr in range(2):
                                                sn, bsn = Ssn[dr][hp]
                                                S.op(pe, lambda o=o, sn=sn, i=i, pb=pb, dr=dr, qq=qq: PEh.matmul(o, lhsT=sn[pb:pb + 64, i, pb:pb + 64], rhs=qq[pb:pb + 64, dr, :], start=False, stop=(dr == 1), tile_position=(pb, pb)),
                                                     reads=[bsn, bqq], writes=[bops])
                                    S.op(act, lambda ops=ops: A.copy(out=osb[:, 0:Tn], in_=ops[:, 0:Tn]), reads=[bops], writes=[b_osb])
                                    S.op(act, lambda ops=ops: A.activation(out=osq[:, 0:Tn], in_=ops[:, 0:Tn], func=AF.Square), reads=[bops], writes=[b_osq])
                                    S.op(dve, lambda ops=ops: V.tensor_copy(out=obf[:, 0:Tn], in_=ops[:, 0:Tn]), reads=[bops], writes=[b_obf])
                                    mps, bmps = pnrm.next()
                                    qps, bqps = pnrm.next()
                                    S.op(pe, lambda mps=mps: PEh.matmul(mps[:, 0:Tn], lhsT=cmb[:, M_BD64M, :], rhs=obf[:, 0:Tn], start=True, stop=True), reads=[b_obf, b_cmb], writes=[bmps])
                                    S.op(pe, lambda qps=qps: PEh.matmul(qps[:, 0:Tn], lhsT=cmb[:, M_BD64M, :], rhs=osq[:, 0:Tn], start=True, stop=True), reads=[b_osq, b_cmb], writes=[bqps])
                                    S.op(dve, lambda mps=mps: V.tensor_tensor(out=osb[:, 0:Tn], in0=osb[:, 0:Tn], in1=mps[:, 0:Tn], op=ALU.subtract), reads=[b_osb, bmps], writes=[b_osb])
                                    S.op(act, lambda mps=mps: A.activation(out=m2[:, 0:Tn], in_=mps[:, 0:Tn], func=AF.Square), reads=[bmps], writes=[b_m2])
                                    S.op(dve, lambda qps=qps: V.tensor_tensor(out=var[:, 0:Tn], in0=qps[:, 0:Tn], in1=m2[:, 0:Tn], op=ALU.subtract), reads=[bqps, b_m2], writes=[b_var])
                                    S.op(dve, lambda: V.tensor_scalar(out=var[:, 0:Tn], in0=var[:, 0:Tn], scalar1=0.0, scalar2=None, op0=ALU.max), reads=[b_var], writes=[b_var])
                                    S.op(act, lambda: A.activation(out=var[:, 0:Tn], in_=var[:, 0:Tn], func=AF.Ln, bias=eps_col[:, :]), reads=[b_var, b_eps], writes=[b_var])
                                    S.op(act, lambda: A.activation(out=var[:, 0:Tn], in_=var[:, 0:Tn], func=AF.Exp, scale=-0.5), reads=[b_var], writes=[b_var])
                                    S.op(dve, lambda: V.tensor_tensor(out=osb[:, 0:Tn], in0=osb[:, 0:Tn], in1=var[:, 0:Tn], op=ALU.mult), reads=[b_osb, b_var], writes=[b_osb])
                                    S.op(act, lambda hp=hp: A.activation(out=m2[:, 0:Tn], in_=rg_[:, hp, 0:Tn], func=AF.Silu), reads=[b_rg], writes=[b_m2])
                                    S.op(dve, lambda hp=hp: V.scalar_tensor_tensor(out=mixo[:, hp, 0:Tn], in0=osb[:, 0:Tn], scalar=pv[:, l, PV_GN + hp:PV_GN + hp + 1], in1=m2[:, 0:Tn], op0=ALU.mult, op1=ALU.mult),
                                         reads=[b_osb, b_m2, b_pv], writes=[b_mixo])
                                S.dma(sp, gd["mixT"][0:2, :, tg0:tg0 + Tn].rearrange("c p t -> p c t"), mixo[:, :, 0:Tn], reads=[b_mixo], writes=[DB("mixr", g, tg0 // T)])
                        S.barrier()
            S.barrier()

        def phase_attn(l, kind):
            nh, rows, vw = (4, 128, 256) if kind == "d" else (8, 96, 512)
            scale = (32.0 ** -0.5) if kind == "d" else (96.0 ** -0.5)
            kname, vname, qname = ("dkT", "dvt", "dqT") if kind == "d" else ("mkT", "mvt", "mqT")
            with ExitStack() as es:
                NK = NST + NCTX
                nkb_max = NK // 128
                nkc = 2 if kind == "d" else 8
                Ksb, b_K = sbt(es, "atK", [rows, nkc, NK], BF16)
                V1, b_V1 = sbt(es, "atV", [128, nkb_max, nh, 128], BF16)
                if kind == "d":
                    Qsb = [sbt(es, f"atQ{i}", [128, 2, 4, T], BF16) for i in range(2)]
                    for Qz, b_Qz in Qsb:
                        S.op(pool, lambda Qz=Qz: P.memset(Qz[:], 0.0), writes=[b_Qz])
                else:
                    Qsb = [sbt(es, f"atQ{i}", [rows, nkc, T], BF16) for i in range(2)]
                ptl = [sbt(es, f"atP{i}", [128, 2, T], BF16) for i in range(4)]
                rcp = [sbt(es, f"atR{i}", [64, T], F32) for i in range(4)]
                av = [sbt(es, f"atA{i}", [64, T], F32) for i in range(6)]
                sqd, b_sqd = sbt(es, "atSq", [64, T], BF16)
                rsd, b_rsd = sbt(es, "atRs", [64, T], F32)
                mixo = [sbt(es, f"atM{i}", [128, T], BF16) for i in range(2)]
                rq_, rp_, rr_, ra_, rm_ = Rot(Qsb), Rot(ptl), Rot(rcp), Rot(av), Rot(mixo)
                scb = Rot([(psA, bA), (psB, bB), (psE2, bE2)])
                accb = Rot([banks[4], banks[5]])
                for g in ("p", "s"):
                    gd = G[g]
                    nseq, n, nk = (4, 256, 256) if g == "p" else (1, NST, NK)
                    for sq_i in range(nseq):
                        k0 = sq_i * n
                        nkb = nk // 128
                        ktiles = sorted(set((k0 + j * 128) // T for j in range(nkb)))
                        S.dma(sp, Ksb[:, :, 0:nk], gd[kname][:, :, k0:k0 + nk].rearrange("c p t -> p c t"), reads=[DB(kname, g, t) for t in ktiles], writes=[b_K])
                        for kb0 in range(0, nkb, 8):
                            kb1 = min(nkb, kb0 + 8)
                            S.dma(sp, V1[:, kb0:kb1, :, :].rearrange("p b h e -> p b (h e)"), gd[vname][k0 + kb0 * 128:k0 + kb1 * 128, :, :].rearrange("(b p) h e -> p b (h e)", p=128),
                                  reads=[DB(vname, g, t) for t in ktiles] + [DB(vname + "1", g)], writes=[b_V1])
                        nqt = (n + T - 1) // T
                        nq = min(T, n)

                        def loadQ(qt_):
                            Qn, b_Qn = rq_.next()
                            q0_ = k0 + qt_ * T
                            if kind == "d":
                                for g4 in range(4):
                                    S.dma(sp, Qn[32 * g4:32 * g4 + 32, :, g4, 0:nq], gd[qname][:, 32 * g4:32 * g4 + 32, q0_:q0_ + nq].rearrange("c p t -> p c t"),
                                          reads=[DB(qname, g, q0_ // T)], writes=[b_Qn])
                            else:
                                S.dma(sp, Qn[:, :, 0:nq], gd[qname][:, :, q0_:q0_ + nq].rearrange("c p t -> p c t"), reads=[DB(qname, g, q0_ // T)], writes=[b_Qn])
                            return Qn, b_Qn

                        Qnext = loadQ(0)
                        for qt in range(nqt):
                            q0 = k0 + qt * T
                            Q, b_Q = Qnext
                            if qt + 1 < nqt:
                                Qnext = loadQ(qt + 1)
                            for pr_i in range(4):
                                if kind == "d":
                                    hd = pr_i
                                    c = hd // 2
                                    ops_ = [(32 * (2 * (hd % 2) + m), c, hd) for m in range(2)]
                                else:
                                    ops_ = [(0, 2 * pr_i + m, 2 * pr_i + m) for m in range(2)]
                                acc = [accb.next() for _ in range(2)]

                                def qk(kb, ops_=ops_, Q=Q, b_Q=b_Q):
                                    sc, bsc = scb.next()
                                    for j, (rb_, cc, vh) in enumerate(ops_):
                                        if kind == "d":
                                            g4 = rb_ // 32
                                            S.op(pe, lambda j=j, g4=g4, cc=cc, kb=kb, sc=sc: PEh.matmul(sc[:, j, 0:nq], lhsT=Ksb[:, cc, kb * 128:(kb + 1) * 128], rhs=Q[:, cc, g4, 0:nq], start=True, stop=True),
                                                 reads=[b_K, b_Q], writes=[bsc[j]])
                                        else:
                                            S.op(pe, lambda j=j, cc=cc, kb=kb, sc=sc: PEh.matmul(sc[:, j, 0:nq], lhsT=Ksb[0:96, cc, kb * 128:(kb + 1) * 128], rhs=Q[0:96, cc, 0:nq], start=True, stop=True),
                                                 reads=[b_K, b_Q], writes=[bsc[j]])
                                    return sc, bsc

                                pend = [qk(0)]
                                if nkb > 1:
                                    pend.append(qk(1))
                                for kb in range(nkb):
                                    if kb + 2 < nkb:
                                        pend.append(qk(kb + 2))
                                    sc, bsc = pend.pop(0)
                                    pt, bpt = rp_.next()
                                    S.op(act, lambda sc=sc, pt=pt: A.activation(out=pt[:, :, 0:nq], in_=sc[:, :, 0:nq], func=AF.Exp, scale=scale), reads=[bsc[0], bsc[1]], writes=[bpt])
                                    for j, (rb_, cc, vh) in enumerate(ops_):
                                        S.op(pe, lambda j=j, vh=vh, kb=kb, pt=pt: PEh.matmul(acc[j][0][:, 0:nq], lhsT=V1[:, kb, vh, :], rhs=pt[:, j, 0:nq], start=(kb == 0), stop=(kb == nkb - 1)),
                                             reads=[b_V1, bpt], writes=[acc[j][1]])
                                res = []
                                for j in range(2):
                                    a_ps, ba = acc[j]
                                    r, br = rr_.next()
                                    o, bo = ra_.next()
                                    S.op(dve, lambda a_ps=a_ps, o=o: V.tensor_copy(out=o[0:64, 0:nq], in_=a_ps[0:64, 0:nq]), reads=[ba], writes=[bo])
                                    S.op(dve, lambda a_ps=a_ps, r=r: V.tensor_copy(out=r[0:64, 0:nq], in_=a_ps[64:128, 0:nq]), reads=[ba], writes=[br])
                                    S.op(act, lambda r=r: A.activation(out=r[0:64, 0:nq], in_=r[0:64, 0:nq], func=AF.Ln), reads=[br], writes=[br])
                                    S.op(act, lambda r=r: A.activation(out=r[0:64, 0:nq], in_=r[0:64, 0:nq], func=AF.Exp, scale=-1.0), reads=[br], writes=[br])
                                    S.op(pool, lambda r=r, o=o: P.tensor_tensor(out=o[0:64, 0:nq], in0=o[0:64, 0:nq], in1=r[0:64, 0:nq], op=ALU.mult), reads=[bo, br], writes=[bo])
                                    res.append((o, bo))
                                if kind == "d":
                                    hd = pr_i
                                    if hd % 2 == 0:
                                        mx, bmx = rm_.next()
                                    d_, bd_ = ra_.next()
                                    S.op(dve, lambda d_=d_: V.scalar_tensor_tensor(out=d_[0:64, 0:nq], in0=res[1][0][0:64, 0:nq], scalar=lamv[0:64, l, 0:1], in1=res[0][0][0:64, 0:nq], op0=ALU.mult, op1=ALU.add),
                                         reads=[res[0][1], res[1][1], b_lamv], writes=[bd_])
                                    S.op(act, lambda d_=d_: A.activation(out=sqd[:, 0:nq], in_=d_[0:64, 0:nq], func=AF.Square), reads=[bd_], writes=[b_sqd])
                                    ps, bps = acc[1]
                                    S.op(pe, lambda ps=ps: PEh.matmul(ps[0:64, 0:nq], lhsT=cmb[0:64, M_ONES, 0:64], rhs=sqd[:, 0:nq], start=True, stop=True), reads=[b_sqd, b_cmb], writes=[bps])
                                    rstd_from((ps, bps), nq, 1.0 / 64, rsd, b_rsd, nrows=64)
                                    S.op(dve, lambda d_=d_: V.scalar_tensor_tensor(out=d_[0:64, 0:nq], in0=d_[0:64, 0:nq], scalar=lamv[0:64, l, 2:3], in1=rsd[0:64, 0:nq], op0=ALU.mult, op1=ALU.mult),
                                         reads=[bd_, b_rsd, b_lamv], writes=[bd_])
                                    pb = (hd % 2) * 64
                                    S.op(pool, lambda d_=d_, mx=mx, pb=pb: P.tensor_copy(out=mx[pb:pb + 64, 0:nq], in_=d_[0:64, 0:nq]), reads=[bd_], writes=[bmx])
                                    if hd % 2 == 1:
                                        S.dma(sp, gd["mixT"][2 + hd // 2, :, q0:q0 + nq], mx[:, 0:nq], reads=[bmx], writes=[DB("mixd", g, q0 // T, hd // 2)])
                                else:
                                    mx, bmx = rm_.next()
                                    for j in range(2):
                                        S.op(pool, lambda j=j, mx=mx: P.tensor_copy(out=mx[j * 64:(j + 1) * 64, 0:nq], in_=res[j][0][0:64, 0:nq]), reads=[res[j][1]], writes=[bmx])
                                    S.dma(sp, gd["mixT"][4 + pr_i, :, q0:q0 + nq], mx[:, 0:nq], reads=[bmx], writes=[DB("mixm", g, q0 // T, pr_i)])
            S.barrier()

        def phase_p5(l, last):
            with ExitStack() as es:
                wo, b_wo = sbt(es, "p5wo", [128, 8, D], BF16)
                wd, b_wd = sbt(es, "p5wd", [128, 22, D], BF16)
                S.dma(pool, wo[:], w_out[l].rearrange("(c p) n -> p c n", p=128), writes=[b_wo])
                for j in range(2):
                    S.dma(pool, wd[:, j * 11:(j + 1) * 11, :], w_down[l, j * 1408:(j + 1) * 1408, :].rearrange("(c p) n -> p c n", p=128), writes=[b_wd])
                xs_ = [sbt(es, f"p5x{i}", [128, 8, T], F32) for i in range(2)]
                mix, b_mix = sbt(es, "p5mix", [128, 8, T], BF16)
                sq, b_sq = sbt(es, "p5sq", [128, 8, T], BF16)
                xt = [sbt(es, f"p5xt{i}", [128, T], F32) for i in range(2)]
                h2, b_h2 = sbt(es, "p5h", [128, 8, T], BF16)
                rstd, b_rstd = sbt(es, "p5rstd", [128, T], F32)
                actt, b_act = sbt(es, "p5act", [128, 22, T], BF16)
                sg = [sbt(es, f"p5sg{i}", [128, T], F32) for i in range(2)]
                wg = [sbt(es, f"p5wg{i}", [128, 8, 2, 256], BF16) for i in range(3)]
                tok = [sbt(es, f"p5tok{i}", [128, D], F32) for i in range(2)]
                rsg, rwg, rtk, rxt = Rot(sg), Rot(wg), Rot(tok), Rot(xt)
                pw = Rot([banks[6], banks[5]])
                pgu = Rot([(psA, bA), (psB, bB)])
                pd = Rot([banks[4], banks[5]])
                tiles = [(g, t) for g in ("p", "s") for t in range(G[g]["ntok"] // T)]
                ssb = banks[6]

                def A1(i):
                    g, t = tiles[i]
                    gd = G[g]
                    cond = gd["cond"]
                    t0 = t * T
                    x_sb, b_x = xs_[i % 2]
                    S.dma(sp, x_sb[:], gd["xT"][:, :, t0:t0 + T].rearrange("c p t -> p c t"), reads=[DB("xT", g, t)], writes=[b_x])
                    mixreads = [DB("mixr", g, t)] + [DB("mixd", g, t, j) for j in range(2)] + [DB("mixm", g, t, j) for j in range(4)]
                    S.dma(sp, mix[:], gd["mixT"][:, :, t0:t0 + T].rearrange("c p t -> p c t"), reads=mixreads, writes=[b_mix])
                    for oc in range(8):
                        ps, bps = pw.next()
                        for kc in range(8):
                            S.op(pe, lambda oc=oc, kc=kc, ps=ps: PEh.matmul(ps[:, 0:T], lhsT=wo[:, kc, oc * 128:(oc + 1) * 128], rhs=mix[:, kc, :], start=(kc == 0), stop=(kc == 7)),
                                 reads=[b_wo, b_mix], writes=[bps], inc=(kc == 7))
                        S.op(dve, lambda oc=oc, ps=ps: V.scalar_tensor_tensor(out=x_sb[:, oc, :], in0=ps[:, 0:T], scalar=modv[:, l, 16 + oc, cond:cond + 1], in1=x_sb[:, oc, :], op0=ALU.mult, op1=ALU.add),
                             reads=[bps, b_x, b_modv], writes=[b_x])
                    S.op(act, lambda: A.activation(out=sq[:], in_=x_sb[:], func=AF.Square), reads=[b_x], writes=[b_sq])

                def A2ss(i):
                    ss, bss = ssb
                    for c in range(8):
                        S.op(pe, lambda c=c: PEh.matmul(ss[:, 0:T], lhsT=cmb[:, M_ONES, :], rhs=sq[:, c, :], start=(c == 0), stop=(c == 7)), reads=[b_sq, b_cmb], writes=[bss], inc=(c == 7))

                def A2rest(i):
                    g, t = tiles[i]
                    cond = G[g]["cond"]
                    x_sb, b_x = xs_[i % 2]
                    rstd_from(ssb, T, 1.0 / D, rstd, b_rstd)
                    for c in range(8):
                        tt_, btt = rxt.next()
                        S.op(dve, lambda c=c, tt_=tt_: V.scalar_tensor_tensor(out=tt_[:], in0=x_sb[:, c, :], scalar=g1v[:, l, 1, c, cond:cond + 1], in1=rstd[:], op0=ALU.mult, op1=ALU.mult),
                             reads=[b_x, b_g1v, b_rstd], writes=[btt])
                        S.op(act, lambda c=c, tt_=tt_: A.activation(out=h2[:, c, :], in_=tt_[:], func=AF.Identity, bias=modv[:, l, 24 + c, cond:cond + 1]),
                             reads=[btt, b_modv], writes=[b_h2])

                def Up(i):
                    for pj in range(11):
                        w, bw = rwg.next()
                        for gu in range(2):
                            S.dma(sp, w[:, :, gu, :], wguS[l, :, :, gu * DFF + pj * 256:gu * DFF + (pj + 1) * 256], reads=[DB("wguS", l)], writes=[bw])
                        for f2 in range(2):
                            fc = pj * 2 + f2
                            gu_ps, bgu = pgu.next()
                            for gu in range(2):
                                for kc in range(8):
                                    S.op(pe, lambda gu=gu, kc=kc, f2=f2, w=w, gu_ps=gu_ps: PEh.matmul(gu_ps[:, gu, :], lhsT=w[:, kc, gu, f2 * 128:(f2 + 1) * 128], rhs=h2[:, kc, :], start=(kc == 0), stop=(kc == 7)),
                                         reads=[bw, b_h2], writes=[bgu[gu]], inc=(kc == 7))
                            s_, bs_ = rsg.next()
                            S.op(act, lambda gu_ps=gu_ps, s_=s_: A.activation(out=s_[:], in_=gu_ps[:, 0, :], func=AF.Silu), reads=[bgu[0]], writes=[bs_])
                            S.op(dve, lambda fc=fc, gu_ps=gu_ps, s_=s_: V.tensor_tensor(out=actt[:, fc, :], in0=gu_ps[:, 1, :], in1=s_[:], op=ALU.mult), reads=[bgu[1], bs_], writes=[b_act])

                def Down(i, ocs):
                    g, t = tiles[i]
                    cond = G[g]["cond"]
                    x_sb, b_x = xs_[i % 2]
                    for oc in ocs:
                        ps, bps = pd.next()
                        for fc in range(22):
                            S.op(pe, lambda oc=oc, fc=fc, ps=ps: PEh.matmul(ps[:, 0:T], lhsT=wd[:, fc, oc * 128:(oc + 1) * 128], rhs=actt[:, fc, :], start=(fc == 0), stop=(fc == 21)),
                                 reads=[b_wd, b_act], writes=[bps], inc=(fc == 21))
                        S.op(dve, lambda oc=oc, ps=ps: V.scalar_tensor_tensor(out=x_sb[:, oc, :], in0=ps[:, 0:T], scalar=modv[:, l, 40 + oc, cond:cond + 1], in1=x_sb[:, oc, :], op0=ALU.mult, op1=ALU.add),
                             reads=[bps, b_x, b_modv], writes=[b_x])

                def Store(i):
                    g, t = tiles[i]
                    gd = G[g]
                    t0 = t * T
                    x_sb, b_x = xs_[i % 2]
                    if not last:
                        S.dma(sp, gd["xT"][:, :, t0:t0 + T].rearrange("c p t -> p c t"), x_sb[:], reads=[b_x], writes=[DB("xT", g, t)])
                    else:
                        ydst = y_p if g == "p" else y_s
                        for tb in range(4):
                            tk, btk = rtk.next()
                            for half in range(2):
                                ps, bps = pw.next()
                                for j in range(4):
                                    c = half * 4 + j
                                    S.op(pe, lambda c=c, j=j, tb=tb, ps=ps: PEh.transpose(out=ps[:, j * 128:(j + 1) * 128], in_=x_sb[:, c, tb * 128:(tb + 1) * 128], identity=cmf[:, M_I, :]),
                                         reads=[b_x, b_cmf], writes=[bps])
                                S.op(act, lambda half=half, ps=ps, tk=tk: A.copy(out=tk[:, half * 512:(half + 1) * 512], in_=ps[:, 0:512]), reads=[bps], writes=[btk])
                            S.dma(sp, ydst[t0 + tb * 128:t0 + (tb + 1) * 128, :], tk[:], reads=[btk])

                n = len(tiles)
                A1(0)
                A2ss(0)
                A2rest(0)
                for i in range(n):
                    Up(i)
                    if i + 1 < n:
                        A1(i + 1)
                    Down(i, range(0, 4))
                    if i + 1 < n:
                        A2ss(i + 1)
                    Down(i, range(4, 8))
                    if i + 1 < n:
                        A2rest(i + 1)
                    Store(i)
            S.barrier()

        stop_after = os.environ.get("K_STOP", "")
        S.barrier()
        prologue_x()
        S.barrier()
        if stop_after != "px":
            prologue_mod()
        S.barrier()
        def cast_wgu(l):
            for kc in range(8):
                S.dma(pool, wguS[l, :, kc, :], w_gu[l, kc * 128:(kc + 1) * 128, :], writes=[DB("wguS", l)])

        if stop_after not in ("px", "pm"):
            cast_wgu(0)
        with ExitStack() as es_on:
            onesb, b_onesb = sbt(es_on, "onesb", [128, 8, 64], BF16)
            S.op(pool, lambda: P.memset(onesb[:], 1.0), writes=[b_onesb])
            for g in ("p", "s"):
                for kb in range(G[g]["nkeys"] // 128):
                    S.dma(sp, G[g]["dvt"][kb * 128:(kb + 1) * 128, :, 64:128], onesb[:, 0:4, :], reads=[b_onesb], writes=[DB("dvt1", g)])
                    S.dma(sp, G[g]["mvt"][kb * 128:(kb + 1) * 128, :, 64:128], onesb[:, :, :], reads=[b_onesb], writes=[DB("mvt1", g)])
            S.barrier()
        for l in range(depth):
            if stop_after in ("px", "pm", "wg"):
                break
            phase_p1(l)
            if stop_after == "p1":
                break
            phase_p2(l)
            if stop_after == "p2":
                break
            if l + 1 < depth:
                cast_wgu(l + 1)
            phase_attn(l, "d")
            phase_attn(l, "m")
            if stop_after == "p4":
                break
            phase_p5(l, last=(l == depth - 1))
        S.finish()
        print("instructions emitted:", S.n_inst, flush=True)
    return nc


_PROG_CACHE = {}


def _host_inputs(inputs):
    f = lambda a: np.ascontiguousarray(np.asarray(a, dtype=np.float32))
    x_prompt, x_sample = f(inputs["x_prompt"]), f(inputs["x_sample"])
    c, c_ctx = f(inputs["c"]), f(inputs["c_ctx"])
    pvec = np.zeros((128, DEPTH, NPV), np.float32)
    for l in range(DEPTH):
        pvec[:, l, PV_N1:PV_N1 + 8] = f(inputs["norm1"])[l].reshape(8, 128).T
        pvec[:, l, PV_N2:PV_N2 + 8] = f(inputs["norm2"])[l].reshape(8, 128).T
        pvec[:, l, PV_BM:PV_BM + 48] = f(inputs["b_mod"])[l].reshape(48, 128).T
        pvec[:, l, PV_QN:PV_QN + 6] = f(inputs["mla_q_norm"])[l].reshape(6, 128).T
        pvec[:, l, PV_KVN:PV_KVN + 2] = f(inputs["mla_kv_norm"])[l].reshape(2, 128).T
        pvec[:, l, PV_GN:PV_GN + 2] = f(inputs["ret_gn_gain"])[l].reshape(2, 128).T
        pvec[:, l, PV_DQG] = np.tile(f(inputs["diff_qk_gain"])[l, 0], 4)
        pvec[:, l, PV_DKG] = np.tile(f(inputs["diff_qk_gain"])[l, 1], 4)
        pvec[:, l, PV_SUB] = np.tile(f(inputs["diff_subln_gain"])[l], 2)
        pvec[:96, l, PV_MQG] = f(inputs["mla_qk_gain"])[l, 0]
        pvec[:96, l, PV_MKG] = f(inputs["mla_qk_gain"])[l, 1]
        rd = f(inputs["ret_decay"])[l]
        pvec[:, l, PV_RD:PV_RD + 8] = rd.reshape(1, 8)
        for dr in range(2):
            for hp in range(2):
                pvec[0:64, l, PV_RD + 8 + dr * 2 + hp] = rd[dr, 2 * hp]
                pvec[64:128, l, PV_RD + 8 + dr * 2 + hp] = rd[dr, 2 * hp + 1]
    dlam = f(inputs["diff_lambda"]).reshape(1, DEPTH, 128)
    shared = dict(pvec=pvec, dlam=dlam, w_mod=f(inputs["w_mod"]), w_in=f(inputs["w_in"]), w_uq=f(inputs["w_uq"]), w_ukv=f(inputs["w_ukv"]),
                  w_out=f(inputs["w_out"]), w_gu=f(inputs["w_gu"]), w_down=f(inputs["w_down"]), rope=_rope_tables(), cmat=_const_mats())
    in_maps = []
    for i in range(NCORES):
        cond = np.stack([c_ctx, c[i]], 0)
        condT = np.ascontiguousarray(cond.reshape(2, 8, 128).transpose(2, 1, 0))
        m = dict(shared)
        m.update(xs=x_sample[i], xp=np.ascontiguousarray(x_prompt[4 * i:4 * i + 4].reshape(NPT, D)), condT=condT,
                 st0=f(inputs["state_ret"])[i], cdk=f(inputs["cache_diff_k"])[i].reshape(DEPTH, NCTX, 256),
                 cdv=f(inputs["cache_diff_v"])[i].reshape(DEPTH, NCTX, 256), cckv=f(inputs["cache_mla_ckv"])[i],
                 ckr=f(inputs["cache_mla_kr"])[i])
        in_maps.append(m)
    return in_maps


def kernel(**inputs):
    depth = int(os.environ.get("K_DEPTH", DEPTH))
    nc = build_program(depth)
    in_maps = _host_inputs(inputs)
    res = run_bass_kernel_spmd(nc, in_maps, core_ids=list(range(NCORES)))
    R = res.results
    y_p = np.concatenate([np.asarray(r["y_p"]).reshape(4, 256, D) for r in R], 0)
    y_s = np.stack([np.asarray(r["y_s"]) for r in R], 0)
    st = np.concatenate([np.asarray(r["o_state"]) for r in R], 0)
    dk = np.concatenate([np.asarray(r["o_dk"]).reshape(4, DEPTH, 256, 4, 2, 32) for r in R], 0)
    dv = np.concatenate([np.asarray(r["o_dv"]).reshape(4, DEPTH, 256, 4, 64) for r in R], 0)
    ckv = np.concatenate([np.asarray(r["o_ckv"]) for r in R], 0)
    kr = np.concatenate([np.asarray(r["o_kr"]) for r in R], 0)
    return (y_p.astype(np.float32), y_s.astype(np.float32), st.astype(np.float32), dk.astype(np.float32),
            dv.astype(np.float32), ckv.astype(np.float32), kr.astype(np.float32))
```

```python
import os
import math
import numpy as np
from contextlib import ExitStack
import concourse.bass as bass
import concourse.mybir as mybir
from concourse.bass_utils import run_bass_kernel_spmd

F32 = mybir.dt.float32
BF16 = mybir.dt.bfloat16
AF = mybir.ActivationFunctionType
ALU = mybir.AluOpType

D = 1024
DEPTH = 4
NCORES = 8
T = 512
NPT = 1024
NST = 4096
NCTX = 256
DIN = 2848
DFF = 2816
EPS = 1e-6
O_RQ, O_RK, O_RV, O_RG, O_DQ, O_DK, O_DV, O_CQ, O_CKV, O_KR = 0, 256, 512, 768, 1024, 1280, 1536, 1792, 2560, 2816
PV_N1, PV_N2, PV_BM, PV_QN, PV_KVN, PV_GN, PV_DQG, PV_DKG, PV_SUB, PV_MQG, PV_MKG, PV_RD, NPV = 0, 8, 16, 64, 70, 72, 74, 75, 76, 77, 78, 79, 95


class Buf:
    __slots__ = ("w", "r", "name", "ps")

    def __init__(self, name="", ps=False):
        self.w = None
        self.r = {}
        self.name = name
        self.ps = ps or name.startswith("ps")


class Eng:
    def __init__(self, name, h, sid, is_pe=False):
        self.name = name
        self.h = h
        self.sid = sid
        self.sem = None
        self.cnt = 0
        self.seen = {}
        self.is_pe = is_pe
        self.dangling = False


class Sched:
    NDMA = 20

    def __init__(self, nc, es):
        self.nc = nc
        self.sems = []

        def mk(name):
            s = es.enter_context(nc.semaphore(name))
            self.sems.append(s)
            return len(self.sems) - 1

        self.pe = Eng("pe", nc.tensor, mk("s_pe"), True)
        self.act = Eng("act", nc.scalar, mk("s_act"))
        self.dve = Eng("dve", nc.vector, mk("s_dve"))
        self.pool = Eng("pool", nc.gpsimd, mk("s_pool"))
        self.sp = Eng("sp", nc.sync, mk("s_sp"))
        self.engs = (self.pe, self.act, self.dve, self.pool, self.sp)
        for e in self.engs:
            e.sem = self.sems[e.sid]
        self.dma_sems = {}
        for e in (self.sp, self.pool):
            lst = [[mk(f"d_{e.name}{i}"), 0] for i in range(self.NDMA)]
            self.dma_sems[e.name] = [lst, 0]
        self.n_inst = 0

    def _wait(self, eng, deps):
        for s, v in deps.items():
            if eng.is_pe and s == eng.sid:
                continue
            if eng.seen.get(s, 0) < v:
                eng.h.wait_ge(self.sems[s], v)
                eng.seen[s] = v
                self.n_inst += 1

    @staticmethod
    def _deps(reads, writes, own=None):
        deps = {}
        for b in reads:
            if b.w is not None:
                s, v = b.w
                if deps.get(s, 0) < v:
                    deps[s] = v
            if b.ps:
                for s, v in b.r.items():
                    if s != own and deps.get(s, 0) < v:
                        deps[s] = v
        for b in writes:
            if b.w is not None:
                s, v = b.w
                if deps.get(s, 0) < v:
                    deps[s] = v
            for s, v in b.r.items():
                if deps.get(s, 0) < v:
                    deps[s] = v
        return deps

    @staticmethod
    def _record(tok, reads, writes):
        s, v = tok
        for b in reads:
            if b.r.get(s, 0) < v:
                b.r[s] = v
        for b in writes:
            b.w = tok
            b.r = {}

    mute = False

    def op(self, eng, fn, reads=(), writes=(), inc=True):
        if self.mute:
            return None
        self._wait(eng, self._deps(reads, writes, eng.sid))
        ins = fn()
        self.n_inst += 1
        if inc or not eng.is_pe:
            eng.cnt += 1
            ins.then_inc(eng.sem, 1)
            tok = (eng.sid, eng.cnt)
            eng.dangling = False
        else:
            tok = (eng.sid, eng.cnt + 1)
            eng.dangling = True
        self._record(tok, reads, writes)
        return tok

    def dma(self, eng, out, in_, reads=(), writes=(), **kw):
        if self.mute:
            return None
        pool = self.dma_sems[eng.name]
        lst, idx = pool
        ent = lst[idx % len(lst)]
        pool[1] = idx + 1
        sid, cnt = ent
        deps = self._deps(reads, writes)
        if cnt > 0 and deps.get(sid, 0) < cnt:
            deps[sid] = cnt
        self._wait(eng, deps)
        ins = eng.h.dma_start(out=out, in_=in_, **kw)
        ent[1] = cnt + 16
        ins.then_inc(self.sems[sid], 16)
        self.n_inst += 1
        tok = (sid, cnt + 16)
        self._record(tok, reads, writes)
        return tok

    def _all(self):
        assert not self.pe.dangling
        deps = {}
        for e in (self.pe, self.act, self.dve, self.pool):
            if e.cnt:
                deps[e.sid] = e.cnt
        for name, (lst, idx) in self.dma_sems.items():
            for sid, cnt in lst:
                if cnt:
                    deps[sid] = cnt
        return deps

    def barrier(self):
        if self.mute:
            return
        deps = self._all()
        for e in self.engs:
            self._wait(e, dict(deps))

    def finish(self):
        self._wait(self.sp, self._all())


def _rope_tables():
    n = NST
    gw = 64
    row = np.repeat(np.arange(n // gw, dtype=np.float32), gw)
    col = np.tile(np.arange(gw, dtype=np.float32), n // gw)

    def tab(rot):
        nf = rot // 4
        inv = (1.0 / (10000.0 ** (np.arange(nf, dtype=np.float32) / nf))).astype(np.float32)
        ang = np.concatenate([row[:, None] * inv[None], col[:, None] * inv[None]], -1).astype(np.float32)
        return np.cos(ang).astype(np.float32).T, np.sin(ang).astype(np.float32).T

    c64, s64 = tab(64)
    c32, s32 = tab(32)
    C64 = np.tile(np.concatenate([c64, c64], 0), (2, 1))
    S64 = np.tile(np.concatenate([-s64, s64], 0), (2, 1))
    C32 = np.tile(np.concatenate([c32, c32], 0), (4, 1))
    S32 = np.tile(np.concatenate([-s32, s32], 0), (4, 1))
    C96 = np.concatenate([np.ones((64, n), np.float32), c32, c32], 0)
    S96 = np.concatenate([np.zeros((64, n), np.float32), -s32, s32], 0)
    pad = np.zeros((32, n), np.float32)
    rope = np.stack([C64, S64, C32, S32, np.concatenate([C96, pad], 0), np.concatenate([S96, pad], 0)], 0)
    return np.ascontiguousarray(rope.astype(np.float32))


def _const_mats():
    I = np.eye(128, dtype=np.float32)
    ones = np.ones((128, 128), np.float32)
    bd32 = np.kron(np.eye(4, dtype=np.float32), np.ones((32, 32), np.float32))
    bd64m = np.kron(np.eye(2, dtype=np.float32), np.ones((64, 64), np.float32)) / 64.0
    p64 = np.zeros((128, 128), np.float32)
    for m in range(128):
        b, i = divmod(m, 64)
        p64[b * 64 + (i + 32) % 64, m] = 1.0
    p32 = np.zeros((128, 128), np.float32)
    for m in range(128):
        b, i = divmod(m, 32)
        p32[b * 32 + (i + 16) % 32, m] = 1.0
    p96 = np.zeros((128, 128), np.float32)
    for m in range(96):
        if m < 64:
            p96[m, m] = 1.0
        else:
            p96[64 + (m - 64 + 16) % 32, m] = 1.0
    idx = np.arange(128, dtype=np.float32)
    distf = np.maximum(idx[None, :] - idx[:, None], 0.0)
    distb = np.maximum(idx[:, None] - idx[None, :], 0.0)
    maskf = (idx[None, :] >= idx[:, None]).astype(np.float32)
    maskb = (idx[:, None] >= idx[None, :]).astype(np.float32)
    np1 = np.tile((idx + 1.0)[None, :], (128, 1))
    cmn = np.tile((128.0 - idx)[None, :], (128, 1))
    misc = np.zeros((128, 128), np.float32)
    misc[:, 0] = 127.0 - idx
    misc[:, 1] = idx
    mats = np.stack([I, ones, bd32, bd64m, p64, p32, p96, distf, distb, maskf, maskb, np1, cmn, misc], 1)
    return np.ascontiguousarray(mats.astype(np.float32))


M_I, M_ONES, M_BD32, M_BD64M, M_P64, M_P32, M_P96, M_DISTF, M_DISTB, M_MASKF, M_MASKB, M_NP1, M_CMN, M_MISC = range(14)
NMAT = 14
NBF = 7


def build_program(depth=DEPTH):
    nc = bass.Bass("TRN2", target_bir_lowering=False)
    I_ = {}

    def inp(name, shape, dt=F32):
        I_[name] = nc.dram_tensor(name, list(shape), dt, kind="ExternalInput").ap()
        return I_[name]

    def outp(name, shape, dt=F32):
        return nc.dram_tensor(name, list(shape), dt, kind="ExternalOutput").ap()

    def scr(name, shape, dt):
        return nc.dram_tensor(name, list(shape), dt, kind="Internal").ap()

    xs_in = inp("xs", [NST, D])
    xp_in = inp("xp", [NPT, D])
    condT = inp("condT", [128, 8, 2])
    st0 = inp("st0", [DEPTH, 2, 4, 64, 64])
    cdk = inp("cdk", [DEPTH, NCTX, 256])
    cdv = inp("cdv", [DEPTH, NCTX, 256])
    cckv = inp("cckv", [DEPTH, NCTX, 256])
    ckr = inp("ckr", [DEPTH, NCTX, 32])
    pvec = inp("pvec", [128, DEPTH, NPV])
    dlam = inp("dlam", [1, DEPTH, 128])
    w_mod = inp("w_mod", [DEPTH, D, 6 * D])
    w_in = inp("w_in", [DEPTH, D, DIN])
    w_uq = inp("w_uq", [DEPTH, 768, 768])
    w_ukv = inp("w_ukv", [DEPTH, 256, 1024])
    w_out = inp("w_out", [DEPTH, D, D])
    w_gu = inp("w_gu", [DEPTH, D, 2 * DFF])
    w_down = inp("w_down", [DEPTH, DFF, D])
    rope_in = inp("rope", [6, 128, NST])
    cmat_in = inp("cmat", [128, NMAT, 128])

    y_p = outp("y_p", [NPT, D])
    y_s = outp("y_s", [NST, D])
    o_state = outp("o_state", [4, DEPTH, 2, 4, 64, 64])
    o_dk = outp("o_dk", [4, DEPTH, 256, 256])
    o_dv = outp("o_dv", [4, DEPTH, 256, 256])
    o_ckv = outp("o_ckv", [4, DEPTH, 256, 256])
    o_kr = outp("o_kr", [4, DEPTH, 256, 32])

    G = {}
    for g, ntok, nkeys in (("p", NPT, NPT), ("s", NST, NST + NCTX)):
        G[g] = dict(
            ntok=ntok, nkeys=nkeys, cond=0 if g == "p" else 1,
            xT=scr(f"xT_{g}", [8, 128, ntok], F32),
            rqT=scr(f"rqT_{g}", [2, 128, ntok], BF16), rkT=scr(f"rkT_{g}", [2, 128, ntok], BF16),
            rkt=scr(f"rkt_{g}", [ntok, 256], BF16), rvt=scr(f"rvt_{g}", [ntok, 256], BF16),
            rgT=scr(f"rgT_{g}", [2, 128, ntok], F32),
            dqT=scr(f"dqT_{g}", [2, 128, ntok], BF16), dkT=scr(f"dkT_{g}", [2, 128, nkeys], BF16),
            dvt=scr(f"dvt_{g}", [nkeys, 4, 128], BF16),
            mqT=scr(f"mqT_{g}", [8, 96, ntok], BF16), mkT=scr(f"mkT_{g}", [8, 96, nkeys], BF16),
            mvt=scr(f"mvt_{g}", [nkeys, 8, 128], BF16),
            mixT=scr(f"mixT_{g}", [8, 128, ntok], BF16),
        )
    wguS = scr("wguS", [DEPTH, 128, 8, 2 * DFF], BF16)
    dbufs = {}

    def DB(*key):
        if key not in dbufs:
            dbufs[key] = Buf(str(key))
        return dbufs[key]

    es_top = ExitStack()
    with es_top:
        S = Sched(nc, es_top)
        pe, act, dve, pool, sp = S.pe, S.act, S.dve, S.pool, S.sp
        V, A, P, PEh = nc.vector, nc.scalar, nc.gpsimd, nc.tensor

        _uid = [0]

        def sbt(es, name, shape, dt):
            _uid[0] += 1
            name = f"{name}_{_uid[0]}"
            return es.enter_context(nc.sbuf_tensor(name, list(shape), dt)), Buf(name)

        psA = es_top.enter_context(nc.psum_tensor("psA", [128, 2, 512], F32))
        psB = es_top.enter_context(nc.psum_tensor("psB", [128, 2, 512], F32))
        psC = es_top.enter_context(nc.psum_tensor("psC", [128, 512], F32))
        psD = es_top.enter_context(nc.psum_tensor("psD", [128, 512], F32))
        psE2 = es_top.enter_context(nc.psum_tensor("psE2", [128, 2, 512], F32))
        psT = psE2[:, 1, :].bitcast(BF16)
        bA, bB = [Buf("psA0"), Buf("psA1")], [Buf("psB0"), Buf("psB1")]
        bE2 = [Buf("psE0"), Buf("psE1")]
        bC, bD, bE, bT = Buf("psC"), Buf("psD"), bE2[0], bE2[1]
        banks = [(psA[:, 0, :], bA[0]), (psA[:, 1, :], bA[1]), (psB[:, 0, :], bB[0]), (psB[:, 1, :], bB[1]),
                 (psC[:], bC), (psD[:], bD), (psE2[:, 0, :], bE)]

        class Rot:
            def __init__(self, items):
                self.items = items
                self.i = 0

            def next(self):
                it = self.items[self.i % len(self.items)]
                self.i += 1
                return it

        cmf, b_cmf = sbt(es_top, "cmf", [128, NMAT, 128], F32)
        cmb, b_cmb = sbt(es_top, "cmb", [128, NBF, 128], BF16)
        pv, b_pv = sbt(es_top, "pv", [128, DEPTH, NPV], F32)
        modv, b_modv = sbt(es_top, "modv", [128, DEPTH, 48, 2], F32)
        g1v, b_g1v = sbt(es_top, "g1v", [128, DEPTH, 2, 8, 2], F32)
        lamv, b_lamv = sbt(es_top, "lamv", [128, DEPTH, 4], F32)
        S.dma(sp, cmf[:], cmat_in, writes=[b_cmf])
        S.dma(sp, pv[:], pvec, writes=[b_pv])
        S.dma(pool, cmb[:], cmat_in[:, 0:NBF, :], writes=[b_cmb])

        def ident_f():
            return cmf[:, M_I, :]

        def prologue_x():
            with ExitStack() as es:
                xin = [sbt(es, f"pxin{i}", [128, D], F32) for i in range(4)]
                xo = [sbt(es, f"pxo{i}", [128, 8, 128], F32) for i in range(4)]
                rot = Rot(banks[0:4])
                k = 0
                for g, src in (("p", xp_in), ("s", xs_in)):
                    gd = G[g]
                    for blk in range(gd["ntok"] // 128):
                        (xi, bxi), (xoo, bxo) = xin[k % 4], xo[k % 4]
                        k += 1
                        S.dma(sp, xi[:], src[blk * 128:(blk + 1) * 128, :], writes=[bxi])
                        for half in range(2):
                            ps, bps = rot.next()
                            for j in range(4):
                                c = half * 4 + j
                                S.op(pe, lambda ps=ps, j=j, c=c, xi=xi: PEh.transpose(out=ps[:, j * 128:(j + 1) * 128], in_=xi[:, c * 128:(c + 1) * 128], identity=ident_f()),
                                     reads=[bxi, b_cmf], writes=[bps])
                            eng, h = (dve, V) if half == 0 else (act, A)
                            if half == 0:
                                S.op(dve, lambda ps=ps, xoo=xoo, half=half: V.tensor_copy(out=xoo[:, half * 4:(half + 1) * 4, :], in_=ps.rearrange("p (c t) -> p c t", c=4)), reads=[bps], writes=[bxo])
                            else:
                                S.op(act, lambda ps=ps, xoo=xoo, half=half: A.copy(out=xoo[:, half * 4:(half + 1) * 4, :], in_=ps.rearrange("p (c t) -> p c t", c=4)), reads=[bps], writes=[bxo])
                        S.dma(sp, gd["xT"][:, :, blk * 128:(blk + 1) * 128].rearrange("c p t -> p c t"), xoo[:], reads=[bxo], writes=[DB("xT", g, blk // 4)])

        def prologue_mod():
            with ExitStack() as es:
                ct, b_ct = sbt(es, "ct", [128, 8, 2], F32)
                sc, b_sc = sbt(es, "scnd", [128, 8, 2], BF16)
                wm = [sbt(es, f"wm{i}", [128, 8, 512], BF16) for i in range(6)]
                S.dma(sp, ct[:], condT, writes=[b_ct])
                S.op(act, lambda: A.activation(out=sc[:], in_=ct[:], func=AF.Silu), reads=[b_ct], writes=[b_sc])
                k = 0
                for l in range(depth):
                    mps, bmps = banks[4 + (l % 2)]
                    for j in range(12):
                        w, bw = wm[k % 6]
                        k += 1
                        S.dma(pool, w[:], w_mod[l, :, j * 512:(j + 1) * 512].rearrange("(c p) n -> p c n", p=128), writes=[bw])
                        for fc in range(4):
                            col = (j * 4 + fc) * 2
                            for kc in range(8):
                                S.op(pe, lambda w=w, fc=fc, kc=kc, mps=mps, col=col: PEh.matmul(mps[:, col:col + 2], lhsT=w[:, kc, fc * 128:(fc + 1) * 128], rhs=sc[:, kc, :], start=(kc == 0), stop=(kc == 7)),
                                     reads=[bw, b_sc], writes=[bmps], inc=(kc == 7))
                    S.op(dve, lambda l=l, mps=mps: V.tensor_tensor(out=modv[:, l, :, :], in0=mps[:, 0:96].rearrange("p (c j) -> p c j", j=2),
                                                                   in1=pv[:, l, PV_BM:PV_BM + 48].unsqueeze(2).broadcast_to([128, 48, 2]), op=ALU.add),
                         reads=[bmps, b_pv], writes=[b_modv])
                    for wh, (pvn, sc0) in enumerate(((PV_N1, 8), (PV_N2, 32))):
                        S.op(dve, lambda l=l, wh=wh, pvn=pvn, sc0=sc0: V.scalar_tensor_tensor(
                            out=g1v[:, l, wh, :, :], in0=modv[:, l, sc0:sc0 + 8, :], scalar=1.0,
                            in1=pv[:, l, pvn:pvn + 8].unsqueeze(2).broadcast_to([128, 8, 2]), op0=ALU.add, op1=ALU.mult),
                            reads=[b_modv, b_pv], writes=[b_g1v])
                dl, b_dl = sbt(es, "dl", [1, DEPTH, 128], F32)
                pr, b_pr = sbt(es, "dlpr", [1, DEPTH, 2, 32], F32)
                sm, b_sm = sbt(es, "dlsm", [1, DEPTH, 2], F32)
                lm, b_lm = sbt(es, "dllm", [1, DEPTH * 2], F32)
                S.dma(sp, dl[:], dlam, writes=[b_dl])
                S.op(pool, lambda: P.memset(lm[:], 0.0), writes=[b_lm])
                dlv = dl[:].rearrange("o l (a b e) -> o l a b e", a=2, b=2)
                S.op(dve, lambda: V.tensor_tensor(out=pr[:], in0=dlv[:, :, :, 0, :], in1=dlv[:, :, :, 1, :], op=ALU.mult), reads=[b_dl], writes=[b_pr])
                S.op(dve, lambda: V.tensor_reduce(out=sm[:], in_=pr[:], axis=mybir.AxisListType.X, op=ALU.add), reads=[b_pr], writes=[b_sm])
                S.op(act, lambda: A.activation(out=sm[:], in_=sm[:], func=AF.Exp), reads=[b_sm], writes=[b_sm])
                for l in range(depth):
                    lam_init = 0.8 - 0.6 * math.exp(-0.3 * l)
                    S.op(dve, lambda l=l, lam_init=lam_init: V.scalar_tensor_tensor(out=lm[:, 2 * l:2 * l + 1], in0=sm[:, l, 1:2], scalar=-lam_init, in1=sm[:, l, 0:1], op0=ALU.add, op1=ALU.subtract),
                         reads=[b_sm], writes=[b_lm])
                    S.op(pool, lambda l=l, lam_init=lam_init: P.memset(lm[:, 2 * l + 1:2 * l + 2], 1.0 - lam_init), writes=[b_lm])
                ps, bps = banks[6]
                on1, b_on1 = sbt(es, "on1", [1, 128], F32)
                S.op(pool, lambda: P.memset(on1[:], 1.0), writes=[b_on1])
                S.op(pe, lambda: PEh.matmul(ps[:, 0:2 * DEPTH], lhsT=on1[:], rhs=lm[:], start=True, stop=True), reads=[b_on1, b_lm], writes=[bps])
                S.op(dve, lambda: V.tensor_copy(out=lamv[:, :, 0:2], in_=ps[:, 0:2 * DEPTH].rearrange("p (l j) -> p l j", j=2)), reads=[bps], writes=[b_lamv])
                for l in range(depth):
                    S.op(dve, lambda l=l: V.tensor_tensor(out=lamv[:, l, 2:3], in0=pv[:, l, PV_SUB:PV_SUB + 1], in1=lamv[:, l, 1:2], op=ALU.mult), reads=[b_lamv, b_pv], writes=[b_lamv])

        def rstd_from(es_ps, T_, scale, out_sb, b_out, nrows=128):
            ps, bps = es_ps
            S.op(act, lambda: A.activation(out=out_sb[0:nrows, 0:T_], in_=ps[0:nrows, 0:T_], func=AF.Ln, scale=scale, bias=eps_col[0:nrows, :]), reads=[bps, b_eps], writes=[b_out])
            S.op(act, lambda: A.activation(out=out_sb[0:nrows, 0:T_], in_=out_sb[0:nrows, 0:T_], func=AF.Exp, scale=-0.5), reads=[b_out], writes=[b_out])

        eps_col, b_eps = sbt(es_top, "eps_col", [128, 1], F32)
        S.op(pool, lambda: P.memset(eps_col[:], EPS), writes=[b_eps])

        class _Stop(Exception):
            pass

        p1stop = int(os.environ.get("K_P1STOP", "0"))

        def stage(k):
            if p1stop == k:
                S.mute = True

        def phase_p1(l):
            _phase_p1(l)
            S.mute = False
            S.barrier()

        def _phase_p1(l):
            with ExitStack() as es:
                win, b_win = sbt(es, "win", [128, 8, DIN], BF16)
                wuq, b_wuq = sbt(es, "wuq", [128, 6, 768], BF16)
                wukv, b_wukv = sbt(es, "wukv", [128, 2, 1024], BF16)
                for kc in range(8):
                    S.dma(pool, win[:, kc, :], w_in[l, kc * 128:(kc + 1) * 128, :], writes=[b_win])
                S.dma(pool, wuq[:], w_uq[l].rearrange("(c p) n -> p c n", p=128), writes=[b_wuq])
                for kc in range(2):
                    for half in range(2):
                        S.dma(pool, wukv[:, kc, half * 512:(half + 1) * 512].rearrange("p (h x) -> p h x", x=64),
                              w_ukv[l, kc * 128:(kc + 1) * 128, :].rearrange("p (h x) -> p h x", x=128)[:, :, half * 64:(half + 1) * 64], writes=[b_wukv])
                x_sb, b_x = sbt(es, "p1x", [128, 8, T], F32)
                h, b_h = sbt(es, "p1h", [128, 8, T], BF16)
                rstd, b_rstd = sbt(es, "p1rstd", [128, T], F32)
                ropeA, b_ropeA = sbt(es, "p1ropeA", [128, 4, T], F32)
                ropeB = [sbt(es, f"p1ropeB{i}", [128, 2, T], F32) for i in range(2)]
                cur = {"rb": 0}

                def rtab(ci):
                    if ci < 4:
                        return ropeA, b_ropeA, ci
                    tb_, bb_ = ropeB[cur["rb"]]
                    return tb_, bb_, 0
                cqn, b_cqn = sbt(es, "p1cqn", [128, 6, T], BF16)
                ckvn, b_ckvn = sbt(es, "p1ckvn", [128, 2, T], BF16)
                krsb, b_krsb = sbt(es, "p1kr", [96, T], F32)
                kr0, b_kr0 = sbt(es, "p1kr0", [32, T], F32)
                sq96, b_sq96 = sbt(es, "p1sq96", [96, T], BF16)
                tmpf = [sbt(es, f"p1tf{i}", [128, T], F32) for i in range(1)]
                tmpb = [sbt(es, f"p1tb{i}", [128, T], BF16) for i in range(3)]
                tokf = [sbt(es, f"p1tokf{i}", [128, 512], F32) for i in range(2)]
                stg = {}
                for nm, shp, dt in (("rq", [128, 2, T], BF16), ("rk", [128, 2, T], BF16), ("rg", [128, 2, T], F32), ("dq", [128, 2, T], BF16),
                                    ("dk", [128, 2, T], BF16), ("mq", [96, 8, T], BF16),
                                    ("rkt", [128, 4, 256], BF16), ("rvdv", [128, 4, 512], BF16), ("mvt", [128, 4, 512], BF16)):
                    stg[nm] = sbt(es, "p1s_" + nm, shp, dt)
                stg["mk"] = stg["mq"]
                rf, rb = Rot(tmpf), Rot(tmpb)
                rtok = Rot(tokf)
                pm = Rot([banks[0], banks[1], banks[2]])
                pn = Rot([banks[4], banks[5]])
                prr = Rot([banks[6], (psE2[:, 1, :], bE2[1])])

                def wchunk(col0, m=128):
                    return lambda kc: win[:, kc, col0:col0 + m]

                def proj(wfn, src, nk, T_, m=128, out_rows=None, tp=None):
                    ps, bps = pm.next()
                    o = ps[0:m, 0:T_] if out_rows is None else ps[out_rows[0]:out_rows[1], 0:T_]
                    for kc in range(nk):
                        kw = {} if tp is None else dict(tile_position=tp)
                        S.op(pe, lambda kc=kc: PEh.matmul(o, lhsT=wfn(kc), rhs=src[0][:, kc, 0:T_], start=(kc == 0), stop=(kc == nk - 1), **kw),
                             reads=[src[1], b_win, b_wuq, b_wukv], writes=[bps], inc=(kc == nk - 1))
                    return ps, bps

                def group_ss(sq_ap, b_sqb, mat, nrows, T_):
                    ps, bps = pn.next()
                    S.op(pe, lambda: PEh.matmul(ps[0:nrows, 0:T_], lhsT=cmb[0:nrows, mat, 0:nrows], rhs=sq_ap, start=True, stop=True), reads=[b_sqb, b_cmb], writes=[bps])
                    return ps, bps

                def rope_apply(xs, b_xs, nrows, permmat, ci, T_, out_ap, b_outs):
                    ps, bps = prr.next()
                    S.op(pe, lambda: PEh.matmul(ps[0:nrows, 0:T_], lhsT=cmb[0:nrows, permmat, 0:nrows], rhs=xs[0:nrows, 0:T_], start=True, stop=True), reads=[b_xs, b_cmb], writes=[bps])
                    t1, bt1 = rf.next()
                    t2, bt2 = rf.next()
                    S.op(pool, lambda: P.tensor_tensor(out=t1[0:nrows, 0:T_], in0=xs[0:nrows, 0:T_], in1=rtab(ci)[0][0:nrows, rtab(ci)[2], 0:T_], op=ALU.mult), reads=[b_xs, rtab(ci)[1]], writes=[bt1])
                    S.op(dve, lambda: V.tensor_tensor(out=t2[0:nrows, 0:T_], in0=ps[0:nrows, 0:T_], in1=rtab(ci)[0][0:nrows, rtab(ci)[2] + 1, 0:T_], op=ALU.mult), reads=[bps, rtab(ci)[1]], writes=[bt2])
                    S.op(pool, lambda: P.tensor_tensor(out=out_ap, in0=t1[0:nrows, 0:T_], in1=t2[0:nrows, 0:T_], op=ALU.add), reads=[bt1, bt2], writes=b_outs)

                def mla_kv(T_, rope, g, key0, out_tokmajor_blocks):
                    gd = G[g]
                    mk, b_mk = stg["mk"]
                    mv, b_mv = stg["mvt"]
                    if T_ != T:
                        for j in range(NS):
                            S.op(act, lambda j=j: A.activation(out=SQ96[j][64:96, 0:T_], in_=krsb[64:96, 0:T_], func=AF.Square), reads=[b_krsb], writes=[bSQ96[j]])
                    gens_ = []

                    def mmk(ps, bps, hd):
                        for kc in range(2):
                            S.op(pe, lambda kc=kc: PEh.matmul(ps[0:64, 0:T_], lhsT=wukv[:, kc, hd * 64:hd * 64 + 64], rhs=ckvn[:, kc, 0:T_], start=(kc == 0), stop=(kc == 1)),
                                 reads=[b_ckvn, b_wukv], writes=[bps], inc=(kc == 1))

                    def g_mv(slot, tb):
                        ps, bps = pmS[slot]
                        for kc in range(2):
                            S.op(pe, lambda kc=kc: PEh.matmul(ps[:, 0:512], lhsT=ckvn[:, kc, tb * 128:(tb + 1) * 128], rhs=wukv[:, kc, 512:1024], start=(kc == 0), stop=(kc == 1)),
                                 reads=[b_ckvn, b_wukv], writes=[bps], inc=(kc == 1))
                        S.op(act, lambda: A.copy(out=mv[:, tb, :], in_=ps[:]), reads=[bps], writes=[b_mv])
                        yield

                    for hd in range(8):
                        gens_.append(g_norm(len(gens_) % NS, lambda ps, bps, hd=hd: mmk(ps, bps, hd), 96, M_ONES, 1.0 / 96, PV_MKG, rope, M_P96, 4, mk[0:96, hd, 0:T_], b_mk, None, T_))
                    for tb in range(T_ // 128):
                        gens_.append(g_mv(len(gens_) % NS, tb))
                    run_pipe(gens_)
                    S.dma(sp, gd["mkT"][:, :, key0:key0 + T_].rearrange("h p t -> p h t"), mk[:, :, 0:T_], reads=[b_mk], writes=[DB("mkT", g, key0 // T)])
                    for tb in range(T_ // 128):
                        S.dma(sp, gd["mvt"][key0 + tb * 128:key0 + (tb + 1) * 128, :, 0:64], mv[:, tb, :].rearrange("p (h e) -> p h e", e=64), reads=[b_mv], writes=[DB("mvt", g, key0 // T)])

                def transpose_out(src_ap_fn, b_src, nchunks, width, T_, dst_fn):
                    for tb in range(T_ // 128):
                        ps, bps = pm.next()
                        for c in range(nchunks):
                            S.op(pe, lambda c=c, tb=tb: PEh.transpose(out=ps[:, c * width:(c + 1) * width], in_=src_ap_fn(c)[:, tb * 128:(tb + 1) * 128], identity=cmf[0:width, M_I, 0:width]),
                                 reads=[b_src, b_cmf], writes=[bps])
                        tk, btk = rtok.next()
                        S.op(act, lambda: A.copy(out=tk[:, 0:nchunks * width], in_=ps[:, 0:nchunks * width]), reads=[bps], writes=[btk])
                        dst_fn(tb, tk[:, 0:nchunks * width], btk)

                NS = 3
                R, bR, T1, bT1, T2, bT2, XS, bXS, SQ, bSQ, SQ96, bSQ96 = [], [], [], [], [], [], [], [], [], [], [], []
                for j in range(NS):
                    for lst, blst, nm_, shp, dt in ((R, bR, "R", [128, T], F32), (T1, bT1, "T1", [128, T], F32), (T2, bT2, "T2", [128, T], F32),
                                                   (XS, bXS, "XS", [128, T], BF16), (SQ, bSQ, "SQ", [128, T], BF16), (SQ96, bSQ96, "SQ96", [96, T], BF16)):
                        t_, b_ = sbt(es, f"p1{nm_}{j}", shp, dt)
                        lst.append(t_)
                        blst.append(b_)
                pmS = [banks[0], banks[1], banks[2]]
                kf = [sbt(es, f"p1kf{i}", [128, T], F32) for i in range(2)]

                def run_pipe(gens_):
                    active = []
                    it = iter(gens_)
                    while True:
                        try:
                            active.insert(0, next(it))
                        except StopIteration:
                            if not active:
                                break
                        for gg in list(active):
                            try:
                                next(gg)
                            except StopIteration:
                                active.remove(gg)

                def mm_proj(ps, bps, wfn, src, nk, T_, m):
                    for kc in range(nk):
                        S.op(pe, lambda kc=kc: PEh.matmul(ps[0:m, 0:T_], lhsT=wfn(kc), rhs=src[0][:, kc, 0:T_], start=(kc == 0), stop=(kc == nk - 1)),
                             reads=[src[1], b_win, b_wuq, b_wukv], writes=[bps], inc=(kc == nk - 1))

                def rope_slot(slot, nrows, permmat, ci, T_, out_ap, b_out):
                    ps, bps = prr.next()
                    xs, b_xs = XS[slot], bXS[slot]
                    t1, bt1, t2, bt2 = T1[slot], bT1[slot], T2[slot], bT2[slot]
                    S.op(pe, lambda: PEh.matmul(ps[0:nrows, 0:T_], lhsT=cmb[0:nrows, permmat, 0:nrows], rhs=xs[0:nrows, 0:T_], start=True, stop=True), reads=[b_xs, b_cmb], writes=[bps])
                    S.op(pool, lambda: P.tensor_tensor(out=t1[0:nrows, 0:T_], in0=xs[0:nrows, 0:T_], in1=rtab(ci)[0][0:nrows, rtab(ci)[2], 0:T_], op=ALU.mult), reads=[b_xs, rtab(ci)[1]], writes=[bt1])
                    S.op(dve, lambda: V.tensor_tensor(out=t2[0:nrows, 0:T_], in0=ps[0:nrows, 0:T_], in1=rtab(ci)[0][0:nrows, rtab(ci)[2] + 1, 0:T_], op=ALU.mult), reads=[bps, rtab(ci)[1]], writes=[bt2])
                    S.op(pool, lambda: P.tensor_tensor(out=out_ap, in0=t1[0:nrows, 0:T_], in1=t2[0:nrows, 0:T_], op=ALU.add), reads=[bt1, bt2], writes=[b_out])

                def g_plain(slot, wfn, src, scl, roped, permmat, ci, out_ap, b_out, _unused):
                    ps, bps = pmS[slot]
                    mm_proj(ps, bps, wfn, src, 8, T, 128)
                    if roped:
                        S.op(act, lambda: A.activation(out=XS[slot][:], in_=ps[:], func=AF.Identity, scale=scl), reads=[bps], writes=[bXS[slot]])
                        yield
                        rope_slot(slot, 128, permmat, ci, T, out_ap, b_out)
                    else:
                        S.op(act, lambda: A.activation(out=out_ap, in_=ps[:], func=AF.Identity, scale=scl), reads=[bps], writes=[b_out])

                def g_norm(slot, mmfn, nrows, ssmat, inv, gcol, roped, permmat, ci, out_ap, b_out, f32o, extra):
                    ps, bps = pmS[slot]
                    mmfn(ps, bps)
                    if extra is None:
                        sq_, bsq_ = SQ[slot], bSQ[slot]
                        S.op(act, lambda: A.activation(out=sq_[0:nrows, :], in_=ps[0:nrows, :], func=AF.Square), reads=[bps], writes=[bsq_])
                        T_ = T
                        n0 = nrows
                    else:
                        sq_, bsq_, T_ = SQ96[slot], bSQ96[slot], extra
                        S.op(act, lambda: A.activation(out=sq_[0:64, 0:T_], in_=ps[0:64, 0:T_], func=AF.Square), reads=[bps], writes=[bsq_])
                        n0 = 64
                    yield
                    ss, bss = pn.next()
                    S.op(pe, lambda: PEh.matmul(ss[0:nrows, 0:T_], lhsT=cmb[0:nrows, ssmat, 0:nrows], rhs=sq_[0:nrows, 0:T_], start=True, stop=True), reads=[bsq_, b_cmb], writes=[bss])
                    r, br = R[slot], bR[slot]
                    rstd_from((ss, bss), T_, inv, r, br, nrows=nrows)
                    if f32o is not None:
                        S.op(dve, lambda: V.scalar_tensor_tensor(out=f32o[0][:], in0=ps[:], scalar=pv[:, l, gcol:gcol + 1], in1=r[:], op0=ALU.mult, op1=ALU.mult), reads=[bps, br, b_pv], writes=[f32o[1]])
                        S.op(pool, lambda: P.tensor_copy(out=out_ap, in_=f32o[0][:]), reads=[f32o[1]], writes=[b_out])
                        return
                    if roped:
                        d0, d1, bd = XS[slot][0:n0, 0:T_], XS[slot][64:96, 0:T_], bXS[slot]
                    else:
                        d0, bd = out_ap[0:n0], b_out
                        d1 = out_ap[64:96] if extra is not None else None
                    S.op(dve, lambda: V.scalar_tensor_tensor(out=d0, in0=ps[0:n0, 0:T_], scalar=pv[0:n0, l, gcol:gcol + 1], in1=r[0:n0, 0:T_], op0=ALU.mult, op1=ALU.mult), reads=[bps, br, b_pv], writes=[bd])
                    if extra is not None:
                        S.op(dve, lambda: V.scalar_tensor_tensor(out=d1, in0=krsb[64:96, 0:T_], scalar=pv[64:96, l, gcol:gcol + 1], in1=r[64:96, 0:T_], op0=ALU.mult, op1=ALU.mult),
                             reads=[b_krsb, br, b_pv], writes=[bd])
                    if roped:
                        yield
                        rope_slot(slot, nrows, permmat, ci, T_, out_ap, b_out)

                def g_rvdv(slot, tb, t0, is_s):
                    ps, bps = pmS[slot]
                    rvdv, brvdv = stg["rvdv"]
                    for kc in range(8):
                        S.op(pe, lambda kc=kc: PEh.matmul(ps[:, 0:256], lhsT=h[:, kc, tb * 128:(tb + 1) * 128], rhs=win[:, kc, O_RV:O_RV + 256], start=(kc == 0), stop=(kc == 7)),
                             reads=[b_h, b_win], writes=[bps], inc=False)
                    for kc in range(8):
                        S.op(pe, lambda kc=kc: PEh.matmul(ps[:, 256:512], lhsT=h[:, kc, tb * 128:(tb + 1) * 128], rhs=win[:, kc, O_DV:O_DV + 256], start=(kc == 0), stop=(kc == 7)),
                             reads=[b_h, b_win], writes=[bps], inc=(kc == 7))
                    S.op(act, lambda: A.copy(out=rvdv[:, tb, :], in_=ps[:]), reads=[bps], writes=[brvdv])
                    if not is_s:
                        tk, btk = rtok.next()
                        S.op(dve, lambda: V.tensor_copy(out=tk[:, 0:256], in_=ps[:, 256:512]), reads=[bps], writes=[btk])
                        tok = t0 + tb * 128
                        S.dma(sp, o_dv[tok // 256, l, tok % 256:tok % 256 + 128, :], tk[:, 0:256], reads=[btk])
                    yield

                def g_cq(slot, wfn, xf, ssbank, c, n):
                    ps, bps = pmS[slot]
                    mm_proj(ps, bps, wfn, (h, b_h), 8, T, 128)
                    S.op(act, lambda: A.copy(out=xf, in_=ps[:]), reads=[bps], writes=[b_x])
                    S.op(pool, lambda: P.tensor_tensor(out=SQ[slot][:], in0=xf, in1=xf, op=ALU.mult), reads=[b_x], writes=[bSQ[slot]])
                    yield
                    S.op(pe, lambda: PEh.matmul(ssbank[0][:, 0:T], lhsT=cmb[:, M_ONES, :], rhs=SQ[slot][:], start=(c == 0), stop=(c == n - 1)), reads=[bSQ[slot], b_cmb], writes=[ssbank[1]])

                def g_kr(slot):
                    ps, bps = pmS[slot]
                    mm_proj(ps, bps, lambda kc: win[:, kc, O_KR:O_KR + 32], (h, b_h), 8, T, 32)
                    S.op(dve, lambda: V.tensor_copy(out=krsb[64:96, :], in_=ps[0:32, :]), reads=[bps], writes=[b_krsb])
                    S.op(dve, lambda: V.tensor_copy(out=kr0[0:32, :], in_=ps[0:32, :]), reads=[bps], writes=[b_kr0])
                    yield

                tiles1 = [(g, t) for g in ("p", "s") for t in range(G[g]["ntok"] // T)]

                def t_start(ti):
                    g, t = tiles1[ti]
                    gd = G[g]
                    cond = gd["cond"]
                    is_s = (g == "s")
                    t0 = t * T
                    S.dma(sp, x_sb[:], gd["xT"][:, :, t0:t0 + T].rearrange("c p t -> p c t"), reads=[DB("xT", g, t)], writes=[b_x])
                    if is_s:
                        S.dma(sp, ropeA[:], rope_in[0:4, :, t0:t0 + T].rearrange("k p t -> p k t"), writes=[b_ropeA])
                        S.dma(sp, ropeB[ti % 2][0][:], rope_in[4:6, :, t0:t0 + T].rearrange("k p t -> p k t"), writes=[ropeB[ti % 2][1]])
                    ss, bss = pn.next()
                    for c in range(8):
                        sqb, bsqb = rb.next()
                        S.op(act, lambda c=c, sqb=sqb: A.activation(out=sqb[:], in_=x_sb[:, c, :], func=AF.Square), reads=[b_x], writes=[bsqb])
                        S.op(pe, lambda c=c, sqb=sqb: PEh.matmul(ss[:, 0:T], lhsT=cmb[:, M_ONES, :], rhs=sqb[:], start=(c == 0), stop=(c == 7)), reads=[bsqb, b_cmb], writes=[bss])
                    rstd_from((ss, bss), T, 1.0 / D, rstd, b_rstd)
                    S.op(dve, lambda: V.tensor_tensor(out=x_sb[:], in0=x_sb[:], in1=rstd[:].unsqueeze(1).broadcast_to([128, 8, T]), op=ALU.mult), reads=[b_x, b_rstd], writes=[b_x])
                    for c in range(8):
                        S.op(act, lambda c=c: A.activation(out=h[:, c, :], in_=x_sb[:, c, :], func=AF.Identity, scale=g1v[:, l, 0, c, cond:cond + 1], bias=modv[:, l, c, cond:cond + 1]),
                             reads=[b_x, b_g1v, b_modv], writes=[b_h])

                def t_body1(ti):
                    g, t = tiles1[ti]
                    gd = G[g]
                    cond = gd["cond"]
                    is_s = (g == "s")
                    t0 = t * T
                    cur["rb"] = ti % 2
                    hs = (h, b_h)
                    stage(1)
                    rq_st, rk_st, rg_st, dq_st, dk_st = stg["rq"], stg["rk"], stg["rg"], stg["dq"], stg["dk"]
                    need_f32 = not is_s
                    gens = []

                    def add(gf, *a):
                        gens.append(gf(len(gens) % NS, *a))

                    for nm, off, scl in (("rq", O_RQ, 1.0), ("rk", O_RK, 0.125)):
                        for c in range(2):
                            add(g_plain, wchunk(off + c * 128), hs, scl, is_s, M_P64, 0, stg[nm][0][:, c, :], stg[nm][1], None)
                    for c in range(2):
                        add(g_plain, wchunk(O_RG + c * 128), hs, 1.0, False, 0, 0, rg_st[0][:, c, :], rg_st[1], None)
                    for nm, off, gcol in (("dq", O_DQ, PV_DQG), ("dk", O_DK, PV_DKG)):
                        for c in range(2):
                            f32o = kf[c] if (nm == "dk" and need_f32) else None
                            add(g_norm, lambda ps, bps, off=off, c=c: mm_proj(ps, bps, wchunk(off + c * 128), hs, 8, T, 128), 128, M_BD32, 1.0 / 32, gcol,
                                is_s, M_P32, 2, stg[nm][0][:, c, :], stg[nm][1], f32o, None)
                    for tb in range(4):
                        add(g_rvdv, tb, t0, is_s)
                    ssq_cq, ssq_ckv = banks[4], banks[5]
                    for c in range(6):
                        add(g_cq, wchunk(O_CQ + c * 128), x_sb[:, c, :], ssq_cq, c, 6)
                    for c in range(2):
                        add(g_cq, wchunk(O_CKV + c * 128), x_sb[:, 6 + c, :], ssq_ckv, c, 2)
                    add(g_kr)
                    run_pipe(gens)
                    for nm in ("rq", "rk", "dq", "dk"):
                        S.dma(sp, gd[nm + "T"][:, :, t0:t0 + T].rearrange("c p t -> p c t"), stg[nm][0][:], reads=[stg[nm][1]], writes=[DB(nm + "T", g, t)])
                    S.dma(sp, gd["rgT"][:, :, t0:t0 + T].rearrange("c p t -> p c t"), rg_st[0][:], reads=[rg_st[1]], writes=[DB("rgT", g, t)])
                    rvdv, brvdv = stg["rvdv"]
                    S.dma(sp, gd["rvt"][t0:t0 + T, :].rearrange("(b p) n -> p b n", p=128), rvdv[:, :, 0:256], reads=[brvdv], writes=[DB("rvt", g, t)])
                    for tb in range(4):
                        S.dma(sp, gd["dvt"][t0 + tb * 128:t0 + (tb + 1) * 128, :, 0:64], rvdv[:, tb, 256:512].rearrange("p (h e) -> p h e", e=64), reads=[brvdv], writes=[DB("dvt", g, t)])
                    rstd_from(ssq_cq, T, 1.0 / 768, rstd, b_rstd)
                    for c in range(6):
                        S.op(dve, lambda c=c: V.scalar_tensor_tensor(out=cqn[:, c, :], in0=x_sb[:, c, :], scalar=pv[:, l, PV_QN + c:PV_QN + c + 1], in1=rstd[:], op0=ALU.mult, op1=ALU.mult),
                             reads=[b_x, b_rstd, b_pv], writes=[b_cqn])
                    r2, br2 = kf[0] if is_s else (R[0], bR[0])
                    rstd_from(ssq_ckv, T, 1.0 / 256, r2, br2)
                    for c in range(2):
                        S.op(dve, lambda c=c: V.scalar_tensor_tensor(out=x_sb[:, 6 + c, :], in0=x_sb[:, 6 + c, :], scalar=pv[:, l, PV_KVN + c:PV_KVN + c + 1], in1=r2[:], op0=ALU.mult, op1=ALU.mult),
                             reads=[b_x, br2, b_pv], writes=[b_x])
                        S.op(pool, lambda c=c: P.tensor_copy(out=ckvn[:, c, :], in_=x_sb[:, 6 + c, :]), reads=[b_x], writes=[b_ckvn])
                    for j in range(NS):
                        S.op(act, lambda j=j: A.activation(out=SQ96[j][64:96, :], in_=krsb[64:96, :], func=AF.Square), reads=[b_krsb], writes=[bSQ96[j]])
                    mq, b_mq = stg["mq"]
                    gens = []
                    for hd in range(8):
                        add(g_norm, lambda ps, bps, hd=hd: mm_proj(ps, bps, lambda kc, hd=hd: wuq[:, kc, hd * 96:(hd + 1) * 96], (cqn, b_cqn), 6, T, 96), 96, M_ONES, 1.0 / 96, PV_MQG,
                            is_s, M_P96, 4, mq[0:96, hd, :], b_mq, None, None)
                    run_pipe(gens)
                    S.dma(sp, gd["mqT"][:, :, t0:t0 + T].rearrange("h p t -> p h t"), mq[:], reads=[b_mq], writes=[DB("mqT", g, t)])
                    rk_, brk = stg["rk"]
                    rkt_, brkt = stg["rkt"]
                    for tb in range(4):
                        for c in range(2):
                            S.op(pe, lambda c=c, tb=tb: PEh.transpose(out=psT[:, (tb % 2) * 512 + c * 128:(tb % 2) * 512 + (c + 1) * 128], in_=rk_[:, c, tb * 128:(tb + 1) * 128], identity=cmb[:, M_I, :]),
                                 reads=[brk, b_cmb], writes=[bT])
                        S.op(dve, lambda tb=tb: V.tensor_copy(out=rkt_[:, tb, :], in_=psT[:, (tb % 2) * 512:(tb % 2) * 512 + 256]), reads=[bT], writes=[brkt])
                    S.dma(sp, gd["rkt"][t0:t0 + T, :].rearrange("(b p) n -> p b n", p=128), rkt_[:], reads=[brkt], writes=[DB("rkt", g, t)])
                    if not is_s:
                        for tb in range(4):
                            tok = t0 + tb * 128
                            ps, bps = pmS[tb % NS]
                            for c in range(2):
                                S.op(pe, lambda c=c, tb=tb, ps=ps: PEh.transpose(out=ps[:, c * 128:(c + 1) * 128], in_=kf[c][0][:, tb * 128:(tb + 1) * 128], identity=cmf[:, M_I, :]),
                                     reads=[kf[c][1], b_cmf], writes=[bps])
                            tk, btk = rtok.next()
                            S.op(act, lambda ps=ps, tk=tk: A.copy(out=tk[:, 0:256], in_=ps[:, 0:256]), reads=[bps], writes=[btk])
                            S.dma(sp, o_dk[tok // 256, l, tok % 256:tok % 256 + 128, :], tk[:, 0:256], reads=[btk])
                        for tb in range(4):
                            tok = t0 + tb * 128
                            ps2, bps2 = pmS[(tb + 1) % NS]
                            for c in range(2):
                                S.op(pe, lambda c=c, tb=tb, ps2=ps2: PEh.transpose(out=ps2[:, c * 128:(c + 1) * 128], in_=x_sb[:, 6 + c, tb * 128:(tb + 1) * 128], identity=cmf[:, M_I, :]),
                                     reads=[b_x, b_cmf], writes=[bps2])
                            S.op(pe, lambda tb=tb, ps2=ps2: PEh.transpose(out=ps2[:, 256:288], in_=kr0[0:32, tb * 128:(tb + 1) * 128], identity=cmf[0:32, M_I, 0:32]),
                                 reads=[b_kr0, b_cmf], writes=[bps2])
                            tk, btk = rtok.next()
                            S.op(act, lambda ps2=ps2, tk=tk: A.copy(out=tk[:, 0:288], in_=ps2[:, 0:288]), reads=[bps2], writes=[btk])
                            S.dma(sp, o_ckv[tok // 256, l, tok % 256:tok % 256 + 128, :], tk[:, 0:256], reads=[btk])
                            S.dma(sp, o_kr[tok // 256, l, tok % 256:tok % 256 + 128, :], tk[:, 256:288], reads=[btk])

                def t_body2(ti):
                    g, t = tiles1[ti]
                    is_s = (g == "s")
                    cur["rb"] = ti % 2
                    stage(6)
                    mla_kv(T, is_s, g, t * T, None)
                    stage(7)

                def t_ctx():
                    g = "s"
                    gd = G[g]
                    is_s = True
                    if True:
                        T_ = NCTX
                        ctk = [sbt(es, f"p1ctk{i}", [128, 2, 256], F32) for i in range(2)]
                        ckr_t, b_ckr_t = sbt(es, "p1ckrt", [128, 2, 32], F32)
                        dkc, b_dkc = sbt(es, "p1dkc", [128, 2, T_], BF16)
                        S.dma(sp, ctk[0][0][:], cdk[l].rearrange("(b p) n -> p b n", p=128), writes=[ctk[0][1]])
                        S.dma(sp, ctk[1][0][:], cckv[l].rearrange("(b p) n -> p b n", p=128), writes=[ctk[1][1]])
                        S.dma(sp, ckr_t[:], ckr[l].rearrange("(b p) n -> p b n", p=128), writes=[b_ckr_t])
                        for which, (dstt, bdst) in enumerate(((dkc, b_dkc), (ckvn, b_ckvn))):
                            src, bsrc = ctk[which]
                            for c in range(2):
                                ps, bps = pm.next()
                                for tb in range(2):
                                    S.op(pe, lambda c=c, tb=tb, src=src, ps=ps: PEh.transpose(out=ps[:, tb * 128:(tb + 1) * 128], in_=src[:, tb, c * 128:(c + 1) * 128], identity=cmf[:, M_I, :]),
                                         reads=[bsrc, b_cmf], writes=[bps])
                                S.op(act, lambda c=c, ps=ps, dstt=dstt: A.copy(out=dstt[:, c, 0:T_], in_=ps[:, 0:T_]), reads=[bps], writes=[bdst])
                        S.dma(sp, gd["dkT"][:, :, NST:NST + T_].rearrange("c p t -> p c t"), dkc[:], reads=[b_dkc], writes=[DB("dkT", g, 8)])
                        S.dma(pool, gd["dvt"][NST:NST + T_, :, 0:64], cdv[l].rearrange("t (h e) -> t h e", e=64), writes=[DB("dvt", g, 8)])
                        ps, bps = pm.next()
                        for tb in range(2):
                            S.op(pe, lambda tb=tb, ps=ps: PEh.transpose(out=ps[0:32, tb * 128:(tb + 1) * 128], in_=ckr_t[:, tb, :], identity=cmf[:, M_I, :]), reads=[b_ckr_t, b_cmf], writes=[bps])
                        S.op(dve, lambda ps=ps: V.tensor_copy(out=krsb[64:96, 0:T_], in_=ps[0:32, 0:T_]), reads=[bps], writes=[b_krsb])
                        mla_kv(T_, False, g, NST, None)

                t_start(0)
                for ti in range(len(tiles1)):
                    t_body1(ti)
                    if ti + 1 < len(tiles1):
                        t_start(ti + 1)
                    t_body2(ti)
                t_ctx()
            S.barrier()

        def phase_p2(l):
            with ExitStack() as es:
                lg, b_lg = sbt(es, "p2lg", [128, 16], F32)
                mc, b_mc = sbt(es, "p2mc", [128, 4, 128], F32)
                mtmp, b_mtmp = sbt(es, "p2mtmp", [128, 128], F32)
                qd, b_qd = sbt(es, "p2qd", [128, 2, 2, 128], F32)
                kdt, b_kdt = sbt(es, "p2kd", [128, 2, 4], F32)
                cdp, b_cdp = sbt(es, "p2cd", [128, 2, 2], F32)
                S.op(act, lambda: A.activation(out=lg[:], in_=pv[:, l, PV_RD:PV_RD + 16], func=AF.Exp, scale=-1.0), reads=[b_pv], writes=[b_lg])
                S.op(act, lambda: A.activation(out=lg[:], in_=lg[:], func=AF.Ln, bias=1.0), reads=[b_lg], writes=[b_lg])
                S.op(dve, lambda: V.tensor_scalar(out=lg[:], in0=lg[:], scalar1=-1.0, scalar2=None, op0=ALU.mult), reads=[b_lg], writes=[b_lg])
                for hd in range(4):
                    S.op(act, lambda hd=hd: A.activation(out=mc[:, hd, :], in_=cmf[:, M_DISTF, :], func=AF.Exp, scale=lg[:, hd:hd + 1]), reads=[b_cmf, b_lg], writes=[b_mc])
                    S.op(dve, lambda hd=hd: V.tensor_tensor(out=mc[:, hd, :], in0=mc[:, hd, :], in1=cmf[:, M_MASKF, :], op=ALU.mult), reads=[b_mc, b_cmf], writes=[b_mc])
                    S.op(act, lambda hd=hd: A.activation(out=mtmp[:], in_=cmf[:, M_DISTB, :], func=AF.Exp, scale=lg[:, 4 + hd:5 + hd]), reads=[b_cmf, b_lg], writes=[b_mtmp])
                    S.op(dve, lambda: V.tensor_tensor(out=mtmp[:], in0=mtmp[:], in1=cmf[:, M_MASKB, :], op=ALU.mult), reads=[b_mtmp, b_cmf], writes=[b_mtmp])
                    S.op(dve, lambda hd=hd: V.tensor_tensor(out=mc[:, hd, :], in0=mc[:, hd, :], in1=mtmp[:], op=ALU.add), reads=[b_mc, b_mtmp], writes=[b_mc])
                    S.op(act, lambda hd=hd: A.activation(out=kdt[:, 0, hd:hd + 1], in_=cmf[:, M_MISC, 0:1], func=AF.Exp, scale=lg[:, hd:hd + 1]), reads=[b_cmf, b_lg], writes=[b_kdt])
                    S.op(act, lambda hd=hd: A.activation(out=kdt[:, 1, hd:hd + 1], in_=cmf[:, M_MISC, 1:2], func=AF.Exp, scale=lg[:, 4 + hd:5 + hd]), reads=[b_cmf, b_lg], writes=[b_kdt])
                for dr in range(2):
                    for hp in range(2):
                        sc_ = lg[:, 8 + dr * 2 + hp:9 + dr * 2 + hp]
                        S.op(act, lambda dr=dr, hp=hp, sc_=sc_: A.activation(out=qd[:, dr, hp, :], in_=cmf[:, M_NP1 if dr == 0 else M_CMN, :], func=AF.Exp, scale=sc_), reads=[b_cmf, b_lg], writes=[b_qd])
                        S.op(act, lambda dr=dr, hp=hp, sc_=sc_: A.activation(out=cdp[:, dr, hp:hp + 1], in_=sc_, func=AF.Exp, scale=128.0), reads=[b_lg], writes=[b_cdp])
                for g in ("p", "s"):
                    gd = G[g]
                    nseq, n = (4, 256) if g == "p" else (1, NST)
                    nch = n // 128
                    with ExitStack() as es2:
                        qT, b_qT = sbt(es2, "p2q", [128, 2, n], BF16)
                        kT, b_kT = sbt(es2, "p2k", [128, 2, n], BF16)
                        kt, b_kt = sbt(es2, "p2kt", [128, nch, 256], BF16)
                        vt, b_vt = sbt(es2, "p2vt", [128, nch, 256], BF16)
                        Sst = [[sbt(es2, f"p2S{dr}{hp}", [128, 128], F32) for hp in range(2)] for dr in range(2)]
                        Ssn = [[sbt(es2, f"p2Sn{dr}{hp}", [128, nch, 128], BF16) for hp in range(2)] for dr in range(2)]
                        kd_ = [sbt(es2, f"p2kdt{i}", [128, 256], BF16) for i in range(2)]
                        qdd = [sbt(es2, f"p2qdd{i}", [128, 2, 128], BF16) for i in range(4)]
                        attm = [sbt(es2, f"p2att{i}", [128, 128], BF16) for i in range(3)]
                        rg_, b_rg = sbt(es2, "p2rg", [128, 2, T], F32)
                        osb, b_osb = sbt(es2, "p2osb", [128, T], F32)
                        obf, b_obf = sbt(es2, "p2obf", [128, T], BF16)
                        osq, b_osq = sbt(es2, "p2osq", [128, T], BF16)
                        m2, b_m2 = sbt(es2, "p2m2", [128, T], F32)
                        var, b_var = sbt(es2, "p2var", [128, T], F32)
                        mixo, b_mixo = sbt(es2, "p2mix", [128, 2, T], BF16)
                        rkd, rqd, ratt = Rot(kd_), Rot(qdd), Rot(attm)
                        pst = Rot([banks[0], banks[1]])
                        patt = Rot([banks[2], banks[3]])
                        po = Rot([banks[4], banks[5]])
                        pnrm = Rot([banks[6], banks[0], banks[1]])
                        for sq_i in range(nseq):
                            t0 = sq_i * n
                            tiles = range(t0 // T, (t0 + n + T - 1) // T)
                            S.dma(sp, qT[:], gd["rqT"][:, :, t0:t0 + n].rearrange("c p t -> p c t"), reads=[DB("rqT", g, t) for t in tiles], writes=[b_qT])
                            S.dma(sp, kT[:], gd["rkT"][:, :, t0:t0 + n].rearrange("c p t -> p c t"), reads=[DB("rkT", g, t) for t in tiles], writes=[b_kT])
                            S.dma(sp, kt[:], gd["rkt"][t0:t0 + n, :].rearrange("(b p) n -> p b n", p=128), reads=[DB("rkt", g, t) for t in tiles], writes=[b_kt])
                            S.dma(sp, vt[:], gd["rvt"][t0:t0 + n, :].rearrange("(b p) n -> p b n", p=128), reads=[DB("rvt", g, t) for t in tiles], writes=[b_vt])
                            for dr in range(2):
                                for hp in range(2):
                                    st, bst = Sst[dr][hp]
                                    S.op(pool, lambda st=st: P.memset(st[:], 0.0), writes=[bst])
                                    if g == "s":
                                        for j in range(2):
                                            S.dma(sp, st[j * 64:(j + 1) * 64, j * 64:(j + 1) * 64], st0[l, dr, hp * 2 + j], writes=[bst])
                            for dr in range(2):
                                order = range(nch) if dr == 0 else range(nch - 1, -1, -1)
                                for i in order:
                                    kd, bkd = rkd.next()
                                    S.op(pool, lambda i=i, dr=dr, kd=kd: P.tensor_tensor(out=kd[:].rearrange("p (h d) -> p h d", d=64), in0=kt[:, i, :].rearrange("p (h d) -> p h d", d=64),
                                                                                 in1=kdt[:, dr, :].unsqueeze(2).broadcast_to([128, 4, 64]), op=ALU.mult), reads=[b_kt, b_kdt], writes=[bkd])
                                    for hp in range(2):
                                        st, bst = Sst[dr][hp]
                                        sn, bsn = Ssn[dr][hp]
                                        S.op(act, lambda i=i, st=st, sn=sn: A.copy(out=sn[:, i, :], in_=st[:]), reads=[bst], writes=[bsn])
                                        ps, bps = pst.next()
                                        S.op(pe, lambda hp=hp, i=i, kd=kd, ps=ps: PEh.matmul(ps[:, 0:128], lhsT=kd[:, hp * 128:(hp + 1) * 128], rhs=vt[:, i, hp * 128:(hp + 1) * 128], start=True, stop=True),
                                             reads=[bkd, b_vt], writes=[bps])
                                        S.op(dve, lambda st=st, ps=ps, dr=dr, hp=hp: V.scalar_tensor_tensor(out=st[:], in0=st[:], scalar=cdp[:, dr, hp:hp + 1], in1=ps[:, 0:128], op0=ALU.mult, op1=ALU.add),
                                             reads=[bst, bps, b_cdp], writes=[bst])
                                if g == "p":
                                    for hp in range(2):
                                        st, bst = Sst[dr][hp]
                                        for j in range(2):
                                            S.dma(sp, o_state[sq_i, l, dr, hp * 2 + j], st[j * 64:(j + 1) * 64, j * 64:(j + 1) * 64], reads=[bst])
                            for tt in range((n + T - 1) // T):
                                Tn = min(T, n)
                                tg0 = t0 + tt * T
                                S.dma(sp, rg_[:, :, 0:Tn], gd["rgT"][:, :, tg0:tg0 + Tn].rearrange("c p t -> p c t"), reads=[DB("rgT", g, tg0 // T)], writes=[b_rg])
                                for hp in range(2):
                                    ops, bops = po.next()
                                    for ci in range(Tn // 128):
                                        i = tt * (T // 128) + ci
                                        cs = slice(i * 128, (i + 1) * 128)
                                        qq, bqq = rqd.next()
                                        for dr in range(2):
                                            S.op(pool, lambda dr=dr, hp=hp, cs=cs, qq=qq: P.tensor_tensor(out=qq[:, dr, :], in0=qT[:, hp, cs], in1=qd[:, dr, hp, :], op=ALU.mult), reads=[b_qT, b_qd], writes=[bqq])
                                        for j in range(2):
                                            hd = hp * 2 + j
                                            pb = j * 64
                                            aps, baps = patt.next()
                                            S.op(pe, lambda pb=pb, hp=hp, cs=cs, aps=aps: PEh.matmul(aps[:, 0:128], lhsT=kT[pb:pb + 64, hp, cs], rhs=qT[pb:pb + 64, hp, cs], start=True, stop=True, tile_position=(pb, 0)),
                                                 reads=[b_kT, b_qT], writes=[baps])
                                            am, bam = ratt.next()
                                            S.op(dve, lambda hd=hd, aps=aps, am=am: V.tensor_tensor(out=am[:], in0=aps[:, 0:128], in1=mc[:, hd, :], op=ALU.mult), reads=[baps, b_mc], writes=[bam])
                                            o = ops[pb:pb + 64, ci * 128:(ci + 1) * 128]
                                            S.op(pe, lambda o=o, hd=hd, i=i, am=am, pb=pb: PEh.matmul(o, lhsT=vt[:, i, hd * 64:(hd + 1) * 64], rhs=am[:], start=True, stop=False, tile_position=(0, pb)),
                                                 reads=[b_vt, bam], writes=[bops])
                                            for dr in range(2):
                                                sn, bsn = Ssn[dr][hp]
                                                S.op(pe, lambda o=o, sn=sn, i=i, pb=pb, dr=dr, qq=qq: PEh.matmul(o, lhsT=sn[pb:pb + 64, i, pb:pb + 64], rhs=qq[pb:pb + 64, dr, :], start=False, stop=(dr == 1), tile_position=(pb, pb)),
                                                     reads=[bsn, bqq], writes=[bops])
                                    S.op(act, lambda ops=ops: A.copy(out=osb[:, 0:Tn], in_=ops[:, 0:Tn]), reads=[bops], writes=[b_osb])
                                    S.op(act, lambda ops=ops: A.activation(out=osq[:, 0:Tn], in_=ops[:, 0:Tn], func=AF.Square), reads=[bops], writes=[b_osq])
                                    S.op(dve, lambda ops=ops: V.tensor_copy(out=obf[:, 0:Tn], in_=ops[:, 0:Tn]), reads=[bops], writes=[b_obf])
                                    mps, bmps = pnrm.next()
                                    qps, bqps = pnrm.next()
                                    S.op(pe, lambda mps=mps: PEh.matmul(mps[:, 0:Tn], lhsT=cmb[:, M_BD64M, :], rhs=obf[:, 0:Tn], start=True, stop=True), reads=[b_obf, b_cmb], writes=[bmps])
                                    S.op(pe, lambda qps=qps: PEh.matmul(qps[:, 0:Tn], lhsT=cmb[:, M_BD64M, :], rhs=osq[:, 0:Tn], start=True, stop=True), reads=[b_osq, b_cmb], writes=[bqps])
                                    S.op(dve, lambda mps=mps: V.tensor_tensor(out=osb[:, 0:Tn], in0=osb[:, 0:Tn], in1=mps[:, 0:Tn], op=ALU.subtract), reads=[b_osb, bmps], writes=[b_osb])
                                    S.op(act, lambda mps=mps: A.activation(out=m2[:, 0:Tn], in_=mps[:, 0:Tn], func=AF.Square), reads=[bmps], writes=[b_m2])
                                    S.op(dve, lambda qps=qps: V.tensor_tensor(out=var[:, 0:Tn], in0=qps[:, 0:Tn], in1=m2[:, 0:Tn], op=ALU.subtract), reads=[bqps, b_m2], writes=[b_var])
                                    S.op(dve, lambda: V.tensor_scalar(out=var[:, 0:Tn], in0=var[:, 0:Tn], scalar1=0.0, scalar2=None, op0=ALU.max), reads=[b_var], writes=[b_var])
                                    S.op(act, lambda: A.activation(out=var[:, 0:Tn], in_=var[:, 0:Tn], func=AF.Ln, bias=eps_col[:, :]), reads=[b_var, b_eps], writes=[b_var])
                                    S.op(act, lambda: A.activation(out=var[:, 0:Tn], in_=var[:, 0:Tn], func=AF.Exp, scale=-0.5), reads=[b_var], writes=[b_var])
                                    S.op(dve, lambda: V.tensor_tensor(out=osb[:, 0:Tn], in0=osb[:, 0:Tn], in1=var[:, 0:Tn], op=ALU.mult), reads=[b_osb, b_var], writes=[b_osb])
                                    S.op(act, lambda hp=hp: A.activation(out=m2[:, 0:Tn], in_=rg_[:, hp, 0:Tn], func=AF.Silu), reads=[b_rg], writes=[b_m2])
                                    S.op(dve, lambda hp=hp: V.scalar_tensor_tensor(out=mixo[:, hp, 0:Tn], in0=osb[:, 0:Tn], scalar=pv[:, l, PV_GN + hp:PV_GN + hp + 1], in1=m2[:, 0:Tn], op0=ALU.mult, op1=ALU.mult),
                                         reads=[b_osb, b_m2, b_pv], writes=[b_mixo])
                                S.dma(sp, gd["mixT"][0:2, :, tg0:tg0 + Tn].rearrange("c p t -> p c t"), mixo[:, :, 0:Tn], reads=[b_mixo], writes=[DB("mixr", g, tg0 // T)])
                        S.barrier()
            S.barrier()

        def phase_attn(l, kind):
            nh, rows, vw = (4, 128, 256) if kind == "d" else (8, 96, 512)
            scale = (32.0 ** -0.5) if kind == "d" else (96.0 ** -0.5)
            kname, vname, qname = ("dkT", "dvt", "dqT") if kind == "d" else ("mkT", "mvt", "mqT")
            with ExitStack() as es:
                NK = NST + NCTX
                nkb_max = NK // 128
                nkc = 2 if kind == "d" else 8
                Ksb, b_K = sbt(es, "atK", [rows, nkc, NK], BF16)
                V1, b_V1 = sbt(es, "atV", [128, nkb_max, nh, 128], BF16)
                if kind == "d":
                    Qsb = [sbt(es, f"atQ{i}", [128, 2, 4, T], BF16) for i in range(2)]
                    for Qz, b_Qz in Qsb:
                        S.op(pool, lambda Qz=Qz: P.memset(Qz[:], 0.0), writes=[b_Qz])
                else:
                    Qsb = [sbt(es, f"atQ{i}", [rows, nkc, T], BF16) for i in range(2)]
                ptl = [sbt(es, f"atP{i}", [128, 2, T], BF16) for i in range(4)]
                rcp = [sbt(es, f"atR{i}", [64, T], F32) for i in range(4)]
                av = [sbt(es, f"atA{i}", [64, T], F32) for i in range(6)]
                sqd, b_sqd = sbt(es, "atSq", [64, T], BF16)
                rsd, b_rsd = sbt(es, "atRs", [64, T], F32)
                mixo = [sbt(es, f"atM{i}", [128, T], BF16) for i in range(2)]
                rq_, rp_, rr_, ra_, rm_ = Rot(Qsb), Rot(ptl), Rot(rcp), Rot(av), Rot(mixo)
                scb = Rot([(psA, bA), (psB, bB), (psE2, bE2)])
                accb = Rot([banks[4], banks[5]])
                for g in ("p", "s"):
                    gd = G[g]
                    nseq, n, nk = (4, 256, 256) if g == "p" else (1, NST, NK)
                    for sq_i in range(nseq):
                        k0 = sq_i * n
                        nkb = nk // 128
                        ktiles = sorted(set((k0 + j * 128) // T for j in range(nkb)))
                        S.dma(sp, Ksb[:, :, 0:nk], gd[kname][:, :, k0:k0 + nk].rearrange("c p t -> p c t"), reads=[DB(kname, g, t) for t in ktiles], writes=[b_K])
                        for kb0 in range(0, nkb, 8):
                            kb1 = min(nkb, kb0 + 8)
                            S.dma(sp, V1[:, kb0:kb1, :, :].rearrange("p b h e -> p b (h e)"), gd[vname][k0 + kb0 * 128:k0 + kb1 * 128, :, :].rearrange("(b p) h e -> p b (h e)", p=128),
                                  reads=[DB(vname, g, t) for t in ktiles] + [DB(vname + "1", g)], writes=[b_V1])
                        nqt = (n + T - 1) // T
                        nq = min(T, n)

                        def loadQ(qt_):
                            Qn, b_Qn = rq_.next()
                            q0_ = k0 + qt_ * T
                            if kind == "d":
                                for g4 in range(4):
                                    S.dma(sp, Qn[32 * g4:32 * g4 + 32, :, g4, 0:nq], gd[qname][:, 32 * g4:32 * g4 + 32, q0_:q0_ + nq].rearrange("c p t -> p c t"),
                                          reads=[DB(qname, g, q0_ // T)], writes=[b_Qn])
                            else:
                                S.dma(sp, Qn[:, :, 0:nq], gd[qname][:, :, q0_:q0_ + nq].rearrange("c p t -> p c t"), reads=[DB(qname, g, q0_ // T)], writes=[b_Qn])
                            return Qn, b_Qn

                        Qnext = loadQ(0)
                        for qt in range(nqt):
                            q0 = k0 + qt * T
                            Q, b_Q = Qnext
                            if qt + 1 < nqt:
                                Qnext = loadQ(qt + 1)
                            for pr_i in range(4):
                                if kind == "d":
                                    hd = pr_i
                                    c = hd // 2
                                    ops_ = [(32 * (2 * (hd % 2) + m), c, hd) for m in range(2)]
                                else:
                                    ops_ = [(0, 2 * pr_i + m, 2 * pr_i + m) for m in range(2)]
                                acc = [accb.next() for _ in range(2)]

                                def qk(kb, ops_=ops_, Q=Q, b_Q=b_Q):
                                    sc, bsc = scb.next()
                                    for j, (rb_, cc, vh) in enumerate(ops_):
                                        if kind == "d":
                                            g4 = rb_ // 32
                                            S.op(pe, lambda j=j, g4=g4, cc=cc, kb=kb, sc=sc: PEh.matmul(sc[:, j, 0:nq], lhsT=Ksb[:, cc, kb * 128:(kb + 1) * 128], rhs=Q[:, cc, g4, 0:nq], start=True, stop=True),
                                                 reads=[b_K, b_Q], writes=[bsc[j]])
                                        else:
                                            S.op(pe, lambda j=j, cc=cc, kb=kb, sc=sc: PEh.matmul(sc[:, j, 0:nq], lhsT=Ksb[0:96, cc, kb * 128:(kb + 1) * 128], rhs=Q[0:96, cc, 0:nq], start=True, stop=True),
                                                 reads=[b_K, b_Q], writes=[bsc[j]])
                                    return sc, bsc

                                pend = [qk(0)]
                                if nkb > 1:
                                    pend.append(qk(1))
                                for kb in range(nkb):
                                    if kb + 2 < nkb:
                                        pend.append(qk(kb + 2))
                                    sc, bsc = pend.pop(0)
                                    pt, bpt = rp_.next()
                                    S.op(act, lambda sc=sc, pt=pt: A.activation(out=pt[:, :, 0:nq], in_=sc[:, :, 0:nq], func=AF.Exp, scale=scale), reads=[bsc[0], bsc[1]], writes=[bpt])
                                    for j, (rb_, cc, vh) in enumerate(ops_):
                                        S.op(pe, lambda j=j, vh=vh, kb=kb, pt=pt: PEh.matmul(acc[j][0][:, 0:nq], lhsT=V1[:, kb, vh, :], rhs=pt[:, j, 0:nq], start=(kb == 0), stop=(kb == nkb - 1)),
                                             reads=[b_V1, bpt], writes=[acc[j][1]])
                                res = []
                                for j in range(2):
                                    a_ps, ba = acc[j]
                                    r, br = rr_.next()
                                    o, bo = ra_.next()
                                    S.op(dve, lambda a_ps=a_ps, o=o: V.tensor_copy(out=o[0:64, 0:nq], in_=a_ps[0:64, 0:nq]), reads=[ba], writes=[bo])
                                    S.op(dve, lambda a_ps=a_ps, r=r: V.tensor_copy(out=r[0:64, 0:nq], in_=a_ps[64:128, 0:nq]), reads=[ba], writes=[br])
                                    S.op(act, lambda r=r: A.activation(out=r[0:64, 0:nq], in_=r[0:64, 0:nq], func=AF.Ln), reads=[br], writes=[br])
                                    S.op(act, lambda r=r: A.activation(out=r[0:64, 0:nq], in_=r[0:64, 0:nq], func=AF.Exp, scale=-1.0), reads=[br], writes=[br])
                                    S.op(pool, lambda r=r, o=o: P.tensor_tensor(out=o[0:64, 0:nq], in0=o[0:64, 0:nq], in1=r[0:64, 0:nq], op=ALU.mult), reads=[bo, br], writes=[bo])
                                    res.append((o, bo))
                                if kind == "d":
                                    hd = pr_i
                                    if hd % 2 == 0:
                                        mx, bmx = rm_.next()
                                    d_, bd_ = ra_.next()
                                    S.op(dve, lambda d_=d_: V.scalar_tensor_tensor(out=d_[0:64, 0:nq], in0=res[1][0][0:64, 0:nq], scalar=lamv[0:64, l, 0:1], in1=res[0][0][0:64, 0:nq], op0=ALU.mult, op1=ALU.add),
                                         reads=[res[0][1], res[1][1], b_lamv], writes=[bd_])
                                    S.op(act, lambda d_=d_: A.activation(out=sqd[:, 0:nq], in_=d_[0:64, 0:nq], func=AF.Square), reads=[bd_], writes=[b_sqd])
                                    ps, bps = acc[1]
                                    S.op(pe, lambda ps=ps: PEh.matmul(ps[0:64, 0:nq], lhsT=cmb[0:64, M_ONES, 0:64], rhs=sqd[:, 0:nq], start=True, stop=True), reads=[b_sqd, b_cmb], writes=[bps])
                                    rstd_from((ps, bps), nq, 1.0 / 64, rsd, b_rsd, nrows=64)
                                    S.op(dve, lambda d_=d_: V.scalar_tensor_tensor(out=d_[0:64, 0:nq], in0=d_[0:64, 0:nq], scalar=lamv[0:64, l, 2:3], in1=rsd[0:64, 0:nq], op0=ALU.mult, op1=ALU.mult),
                                         reads=[bd_, b_rsd, b_lamv], writes=[bd_])
                                    pb = (hd % 2) * 64
                                    S.op(pool, lambda d_=d_, mx=mx, pb=pb: P.tensor_copy(out=mx[pb:pb + 64, 0:nq], in_=d_[0:64, 0:nq]), reads=[bd_], writes=[bmx])
                                    if hd % 2 == 1:
                                        S.dma(sp, gd["mixT"][2 + hd // 2, :, q0:q0 + nq], mx[:, 0:nq], reads=[bmx], writes=[DB("mixd", g, q0 // T, hd // 2)])
                                else:
                                    mx, bmx = rm_.next()
                                    for j in range(2):
                                        S.op(pool, lambda j=j, mx=mx: P.tensor_copy(out=mx[j * 64:(j + 1) * 64, 0:nq], in_=res[j][0][0:64, 0:nq]), reads=[res[j][1]], writes=[bmx])
                                    S.dma(sp, gd["mixT"][4 + pr_i, :, q0:q0 + nq], mx[:, 0:nq], reads=[bmx], writes=[DB("mixm", g, q0 // T, pr_i)])
            S.barrier()

        def phase_p5(l, last):
            with ExitStack() as es:
                wo, b_wo = sbt(es, "p5wo", [128, 8, D], BF16)
                wd, b_wd = sbt(es, "p5wd", [128, 22, D], BF16)
                S.dma(pool, wo[:], w_out[l].rearrange("(c p) n -> p c n", p=128), writes=[b_wo])
                for j in range(2):
                    S.dma(pool, wd[:, j * 11:(j + 1) * 11, :], w_down[l, j * 1408:(j + 1) * 1408, :].rearrange("(c p) n -> p c n", p=128), writes=[b_wd])
                xs_ = [sbt(es, f"p5x{i}", [128, 8, T], F32) for i in range(2)]
                mix, b_mix = sbt(es, "p5mix", [128, 8, T], BF16)
                sq, b_sq = sbt(es, "p5sq", [128, 8, T], BF16)
                xt = [sbt(es, f"p5xt{i}", [128, T], F32) for i in range(2)]
                h2, b_h2 = sbt(es, "p5h", [128, 8, T], BF16)
                rstd, b_rstd = sbt(es, "p5rstd", [128, T], F32)
                actt, b_act = sbt(es, "p5act", [128, 22, T], BF16)
                sg = [sbt(es, f"p5sg{i}", [128, T], F32) for i in range(2)]
                wg = [sbt(es, f"p5wg{i}", [128, 8, 2, 256], BF16) for i in range(3)]
                tok = [sbt(es, f"p5tok{i}", [128, D], F32) for i in range(2)]
                rsg, rwg, rtk, rxt = Rot(sg), Rot(wg), Rot(tok), Rot(xt)
                pw = Rot([banks[6], banks[5]])
                pgu = Rot([(psA, bA), (psB, bB)])
                pd = Rot([banks[4], banks[5]])
                tiles = [(g, t) for g in ("p", "s") for t in range(G[g]["ntok"] // T)]
                ssb = banks[6]

                def A1(i):
                    g, t = tiles[i]
                    gd = G[g]
                    cond = gd["cond"]
                    t0 = t * T
                    x_sb, b_x = xs_[i % 2]
                    S.dma(sp, x_sb[:], gd["xT"][:, :, t0:t0 + T].rearrange("c p t -> p c t"), reads=[DB("xT", g, t)], writes=[b_x])
                    mixreads = [DB("mixr", g, t)] + [DB("mixd", g, t, j) for j in range(2)] + [DB("mixm", g, t, j) for j in range(4)]
                    S.dma(sp, mix[:], gd["mixT"][:, :, t0:t0 + T].rearrange("c p t -> p c t"), reads=mixreads, writes=[b_mix])
                    for oc in range(8):
                        ps, bps = pw.next()
                        for kc in range(8):
                            S.op(pe, lambda oc=oc, kc=kc, ps=ps: PEh.matmul(ps[:, 0:T], lhsT=wo[:, kc, oc * 128:(oc + 1) * 128], rhs=mix[:, kc, :], start=(kc == 0), stop=(kc == 7)),
                                 reads=[b_wo, b_mix], writes=[bps], inc=(kc == 7))
                        S.op(dve, lambda oc=oc, ps=ps: V.scalar_tensor_tensor(out=x_sb[:, oc, :], in0=ps[:, 0:T], scalar=modv[:, l, 16 + oc, cond:cond + 1], in1=x_sb[:, oc, :], op0=ALU.mult, op1=ALU.add),
                             reads=[bps, b_x, b_modv], writes=[b_x])
                    S.op(act, lambda: A.activation(out=sq[:], in_=x_sb[:], func=AF.Square), reads=[b_x], writes=[b_sq])

                def A2ss(i):
                    ss, bss = ssb
                    for c in range(8):
                        S.op(pe, lambda c=c: PEh.matmul(ss[:, 0:T], lhsT=cmb[:, M_ONES, :], rhs=sq[:, c, :], start=(c == 0), stop=(c == 7)), reads=[b_sq, b_cmb], writes=[bss], inc=(c == 7))

                def A2rest(i):
                    g, t = tiles[i]
                    cond = G[g]["cond"]
                    x_sb, b_x = xs_[i % 2]
                    rstd_from(ssb, T, 1.0 / D, rstd, b_rstd)
                    for c in range(8):
                        tt_, btt = rxt.next()
                        S.op(dve, lambda c=c, tt_=tt_: V.scalar_tensor_tensor(out=tt_[:], in0=x_sb[:, c, :], scalar=g1v[:, l, 1, c, cond:cond + 1], in1=rstd[:], op0=ALU.mult, op1=ALU.mult),
                             reads=[b_x, b_g1v, b_rstd], writes=[btt])
                        S.op(act, lambda c=c, tt_=tt_: A.activation(out=h2[:, c, :], in_=tt_[:], func=AF.Identity, bias=modv[:, l, 24 + c, cond:cond + 1]),
                             reads=[btt, b_modv], writes=[b_h2])

                def Up(i):
                    for pj in range(11):
                        w, bw = rwg.next()
                        for gu in range(2):
                            S.dma(sp, w[:, :, gu, :], wguS[l, :, :, gu * DFF + pj * 256:gu * DFF + (pj + 1) * 256], reads=[DB("wguS", l)], writes=[bw])
                        for f2 in range(2):
                            fc = pj * 2 + f2
                            gu_ps, bgu = pgu.next()
                            for gu in range(2):
                                for kc in range(8):
                                    S.op(pe, lambda gu=gu, kc=kc, f2=f2, w=w, gu_ps=gu_ps: PEh.matmul(gu_ps[:, gu, :], lhsT=w[:, kc, gu, f2 * 128:(f2 + 1) * 128], rhs=h2[:, kc, :], start=(kc == 0), stop=(kc == 7)),
                                         reads=[bw, b_h2], writes=[bgu[gu]], inc=(kc == 7))
                            s_, bs_ = rsg.next()
                            S.op(act, lambda gu_ps=gu_ps, s_=s_: A.activation(out=s_[:], in_=gu_ps[:, 0, :], func=AF.Silu), reads=[bgu[0]], writes=[bs_])
                            S.op(dve, lambda fc=fc, gu_ps=gu_ps, s_=s_: V.tensor_tensor(out=actt[:, fc, :], in0=gu_ps[:, 1, :], in1=s_[:], op=ALU.mult), reads=[bgu[1], bs_], writes=[b_act])

                def Down(i, ocs):
                    g, t = tiles[i]
                    cond = G[g]["cond"]
                    x_sb, b_x = xs_[i % 2]
                    for oc in ocs:
                        ps, bps = pd.next()
                        for fc in range(22):
                            S.op(pe, lambda oc=oc, fc=fc, ps=ps: PEh.matmul(ps[:, 0:T], lhsT=wd[:, fc, oc * 128:(oc + 1) * 128], rhs=actt[:, fc, :], start=(fc == 0), stop=(fc == 21)),
                                 reads=[b_wd, b_act], writes=[bps], inc=(fc == 21))
                        S.op(dve, lambda oc=oc, ps=ps: V.scalar_tensor_tensor(out=x_sb[:, oc, :], in0=ps[:, 0:T], scalar=modv[:, l, 40 + oc, cond:cond + 1], in1=x_sb[:, oc, :], op0=ALU.mult, op1=ALU.add),
                             reads=[bps, b_x, b_modv], writes=[b_x])

                def Store(i):
                    g, t = tiles[i]
                    gd = G[g]
                    t0 = t * T
                    x_sb, b_x = xs_[i % 2]
                    if not last:
                        S.dma(sp, gd["xT"][:, :, t0:t0 + T].rearrange("c p t -> p c t"), x_sb[:], reads=[b_x], writes=[DB("xT", g, t)])
                    else:
                        ydst = y_p if g == "p" else y_s
                        for tb in range(4):
                            tk, btk = rtk.next()
                            for half in range(2):
                                ps, bps = pw.next()
                                for j in range(4):
                                    c = half * 4 + j
                                    S.op(pe, lambda c=c, j=j, tb=tb, ps=ps: PEh.transpose(out=ps[:, j * 128:(j + 1) * 128], in_=x_sb[:, c, tb * 128:(tb + 1) * 128], identity=cmf[:, M_I, :]),
                                         reads=[b_x, b_cmf], writes=[bps])
                                S.op(act, lambda half=half, ps=ps, tk=tk: A.copy(out=tk[:, half * 512:(half + 1) * 512], in_=ps[:, 0:512]), reads=[bps], writes=[btk])
                            S.dma(sp, ydst[t0 + tb * 128:t0 + (tb + 1) * 128, :], tk[:], reads=[btk])

                n = len(tiles)
                A1(0)
                A2ss(0)
                A2rest(0)
                for i in range(n):
                    Up(i)
                    if i + 1 < n:
                        A1(i + 1)
                    Down(i, range(0, 4))
                    if i + 1 < n:
                        A2ss(i + 1)
                    Down(i, range(4, 8))
                    if i + 1 < n:
                        A2rest(i + 1)
                    Store(i)
            S.barrier()

        stop_after = os.environ.get("K_STOP", "")
        S.barrier()
        prologue_x()
        S.barrier()
        if stop_after != "px":
            prologue_mod()
        S.barrier()
        def cast_wgu(l):
            for kc in range(8):
                S.dma(pool, wguS[l, :, kc, :], w_gu[l, kc * 128:(kc + 1) * 128, :], writes=[DB("wguS", l)])

        if stop_after not in ("px", "pm"):
            cast_wgu(0)
        with ExitStack() as es_on:
            onesb, b_onesb = sbt(es_on, "onesb", [128, 8, 64], BF16)
            S.op(pool, lambda: P.memset(onesb[:], 1.0), writes=[b_onesb])
            for g in ("p", "s"):
                for kb in range(G[g]["nkeys"] // 128):
                    S.dma(sp, G[g]["dvt"][kb * 128:(kb + 1) * 128, :, 64:128], onesb[:, 0:4, :], reads=[b_onesb], writes=[DB("dvt1", g)])
                    S.dma(sp, G[g]["mvt"][kb * 128:(kb + 1) * 128, :, 64:128], onesb[:, :, :], reads=[b_onesb], writes=[DB("mvt1", g)])
            S.barrier()
        for l in range(depth):
            if stop_after in ("px", "pm", "wg"):
                break
            phase_p1(l)
            if stop_after == "p1":
                break
            phase_p2(l)
            if stop_after == "p2":
                break
            if l + 1 < depth:
                cast_wgu(l + 1)
            phase_attn(l, "d")
            phase_attn(l, "m")
            if stop_after == "p4":
                break
            phase_p5(l, last=(l == depth - 1))
        S.finish()
        print("instructions emitted:", S.n_inst, flush=True)
    return nc


_PROG_CACHE = {}


def _host_inputs(inputs):
    f = lambda a: np.ascontiguousarray(np.asarray(a, dtype=np.float32))
    x_prompt, x_sample = f(inputs["x_prompt"]), f(inputs["x_sample"])
    c, c_ctx = f(inputs["c"]), f(inputs["c_ctx"])
    pvec = np.zeros((128, DEPTH, NPV), np.float32)
    for l in range(DEPTH):
        pvec[:, l, PV_N1:PV_N1 + 8] = f(inputs["norm1"])[l].reshape(8, 128).T
        pvec[:, l, PV_N2:PV_N2 + 8] = f(inputs["norm2"])[l].reshape(8, 128).T
        pvec[:, l, PV_BM:PV_BM + 48] = f(inputs["b_mod"])[l].reshape(48, 128).T
        pvec[:, l, PV_QN:PV_QN + 6] = f(inputs["mla_q_norm"])[l].reshape(6, 128).T
        pvec[:, l, PV_KVN:PV_KVN + 2] = f(inputs["mla_kv_norm"])[l].reshape(2, 128).T
        pvec[:, l, PV_GN:PV_GN + 2] = f(inputs["ret_gn_gain"])[l].reshape(2, 128).T
        pvec[:, l, PV_DQG] = np.tile(f(inputs["diff_qk_gain"])[l, 0], 4)
        pvec[:, l, PV_DKG] = np.tile(f(inputs["diff_qk_gain"])[l, 1], 4)
        pvec[:, l, PV_SUB] = np.tile(f(inputs["diff_subln_gain"])[l], 2)
        pvec[:96, l, PV_MQG] = f(inputs["mla_qk_gain"])[l, 0]
        pvec[:96, l, PV_MKG] = f(inputs["mla_qk_gain"])[l, 1]
        rd = f(inputs["ret_decay"])[l]
        pvec[:, l, PV_RD:PV_RD + 8] = rd.reshape(1, 8)
        for dr in range(2):
            for hp in range(2):
                pvec[0:64, l, PV_RD + 8 + dr * 2 + hp] = rd[dr, 2 * hp]
                pvec[64:128, l, PV_RD + 8 + dr * 2 + hp] = rd[dr, 2 * hp + 1]
    dlam = f(inputs["diff_lambda"]).reshape(1, DEPTH, 128)
    shared = dict(pvec=pvec, dlam=dlam, w_mod=f(inputs["w_mod"]), w_in=f(inputs["w_in"]), w_uq=f(inputs["w_uq"]), w_ukv=f(inputs["w_ukv"]),
                  w_out=f(inputs["w_out"]), w_gu=f(inputs["w_gu"]), w_down=f(inputs["w_down"]), rope=_rope_tables(), cmat=_const_mats())
    in_maps = []
    for i in range(NCORES):
        cond = np.stack([c_ctx, c[i]], 0)
        condT = np.ascontiguousarray(cond.reshape(2, 8, 128).transpose(2, 1, 0))
        m = dict(shared)
        m.update(xs=x_sample[i], xp=np.ascontiguousarray(x_prompt[4 * i:4 * i + 4].reshape(NPT, D)), condT=condT,
                 st0=f(inputs["state_ret"])[i], cdk=f(inputs["cache_diff_k"])[i].reshape(DEPTH, NCTX, 256),
                 cdv=f(inputs["cache_diff_v"])[i].reshape(DEPTH, NCTX, 256), cckv=f(inputs["cache_mla_ckv"])[i],
                 ckr=f(inputs["cache_mla_kr"])[i])
        in_maps.append(m)
    return in_maps


def kernel(**inputs):
    depth = int(os.environ.get("K_DEPTH", DEPTH))
    nc = build_program(depth)
    in_maps = _host_inputs(inputs)
    res = run_bass_kernel_spmd(nc, in_maps, core_ids=list(range(NCORES)))
    R = res.results
    y_p = np.concatenate([np.asarray(r["y_p"]).reshape(4, 256, D) for r in R], 0)
    y_s = np.stack([np.asarray(r["y_s"]) for r in R], 0)
    st = np.concatenate([np.asarray(r["o_state"]) for r in R], 0)
    dk = np.concatenate([np.asarray(r["o_dk"]).reshape(4, DEPTH, 256, 4, 2, 32) for r in R], 0)
    dv = np.concatenate([np.asarray(r["o_dv"]).reshape(4, DEPTH, 256, 4, 64) for r in R], 0)
    ckv = np.concatenate([np.asarray(r["o_ckv"]) for r in R], 0)
    kr = np.concatenate([np.asarray(r["o_kr"]) for r in R], 0)
    return (y_p.astype(np.float32), y_s.astype(np.float32), st.astype(np.float32), dk.astype(np.float32),
            dv.astype(np.float32), ckv.astype(np.float32), kr.astype(np.float32))
```
